# Optimizing a Trainium2 kernel written in Bass

```python
import math
import jax, jax.numpy as jnp
from jax import lax
import numpy as np

D_MODEL = 2048
BATCH = 8
SEQ = 2048
DEPTH = 1

D_MIX = D_MODEL
HEAD_DIM = 64
D_ATTN = D_MIX // 2
N_Q_HEADS = D_ATTN // HEAD_DIM
N_KV_HEADS = 4
Q_PER_KV = N_Q_HEADS // N_KV_HEADS
D_KV = N_KV_HEADS * HEAD_DIM
WINDOW = 128
BLOCK = 128
ROPE_THETA = 10000.0
D_SSM = D_MIX - D_ATTN
SSM_GROUP = 16
N_SSM_GROUPS = D_SSM // SSM_GROUP
SSM_STATE = 64
D_IN = D_ATTN + 2 * D_KV + D_SSM
D_FF = ((8 * D_MODEL // 3 + 255) // 256) * 256
RMS_EPS = 1e-6

kernel_name = 'hymba_swa_sink_s5_sandwich_block'


def rms_norm(x, g):
    xf = x.astype(jnp.float32)
    y = xf * lax.rsqrt(jnp.mean(xf * xf, axis=-1, keepdims=True) + RMS_EPS)
    return (y * g.astype(jnp.float32)).astype(x.dtype)


def rotary(t, positions):
    half = HEAD_DIM // 2
    inv_freq = ROPE_THETA ** (-jnp.arange(half, dtype=jnp.float32) / half)
    ang = positions.astype(jnp.float32)[:, :, None] * inv_freq
    cos = jnp.cos(ang)[:, :, None, :]
    sin = jnp.sin(ang)[:, :, None, :]
    tf = t.astype(jnp.float32)
    t1, t2 = tf[..., :half], tf[..., half:]
    return jnp.concatenate([t1 * cos - t2 * sin, t2 * cos + t1 * sin], axis=-1).astype(t.dtype)


def sliding_window_attention(q, k, v, sinks):
    B, L = q.shape[0], q.shape[1]
    nb = L // BLOCK
    qb = q.reshape(B, nb, BLOCK, N_KV_HEADS, Q_PER_KV, HEAD_DIM)
    kb = k.reshape(B, nb, BLOCK, N_KV_HEADS, HEAD_DIM)
    vb = v.reshape(B, nb, BLOCK, N_KV_HEADS, HEAD_DIM)
    pad = ((0, 0), (1, 0), (0, 0), (0, 0), (0, 0))
    kk = jnp.concatenate([jnp.pad(kb, pad)[:, :-1], kb], axis=2)
    vv = jnp.concatenate([jnp.pad(vb, pad)[:, :-1], vb], axis=2)
    scale = 1.0 / math.sqrt(HEAD_DIM)
    scores = jnp.einsum('bnqkgd,bnskd->bnkgqs', qb, kk).astype(jnp.float32) * scale
    blk = jnp.arange(nb, dtype=jnp.int32)[:, None] * BLOCK
    q_pos = blk + jnp.arange(BLOCK, dtype=jnp.int32)[None, :]
    k_pos = blk - BLOCK + jnp.arange(2 * BLOCK, dtype=jnp.int32)[None, :]
    diff = q_pos[:, :, None] - k_pos[:, None, :]
    mask = (diff >= 0) & (diff < WINDOW) & (k_pos[:, None, :] >= 0)
    scores = jnp.where(mask[None, :, None, None], scores, -jnp.inf)
    sink = sinks.astype(jnp.float32).reshape(N_KV_HEADS, Q_PER_KV)[None, None, :, :, None, None]
    m = jnp.maximum(jnp.max(scores, axis=-1, keepdims=True), sink)
    p = jnp.exp(scores - m)
    probs = p / (jnp.sum(p, axis=-1, keepdims=True) + jnp.exp(sink - m))
    out = jnp.einsum('bnkgqs,bnskd->bnqkgd', probs.astype(v.dtype), vv)
    return out.reshape(B, L, N_Q_HEADS * HEAD_DIM)


def s5_ssm(u, a_re, a_im, log_dt, b_re, b_im, c_re, c_im, d_skip):
    L = u.shape[1]
    uf = u.astype(jnp.float32)
    dt = jnp.exp(log_dt.astype(jnp.float32))[:, None]
    ar = a_re.astype(jnp.float32)
    ai = a_im.astype(jnp.float32)
    mag = jnp.exp(ar * dt)
    lam_re = mag * jnp.cos(ai * dt)
    lam_im = mag * jnp.sin(ai * dt)
    den = ar * ar + ai * ai
    nr = lam_re - 1.0
    ni = lam_im
    f_re = (nr * ar + ni * ai) / den
    f_im = (ni * ar - nr * ai) / den
    br = b_re.astype(jnp.float32)
    bi = b_im.astype(jnp.float32)
    bbar_re = f_re[..., None] * br - f_im[..., None] * bi
    bbar_im = f_re[..., None] * bi + f_im[..., None] * br
    bu_re = jnp.einsum('blgp,gnp->blgn', uf, bbar_re)
    bu_im = jnp.einsum('blgp,gnp->blgn', uf, bbar_im)
    shp = (1, L) + lam_re.shape
    a_seq_re = jnp.broadcast_to(lam_re[None, None], shp)
    a_seq_im = jnp.broadcast_to(lam_im[None, None], shp)

    def combine(earlier, later):
        a1r, a1i, b1r, b1i = earlier
        a2r, a2i, b2r, b2i = later
        return (a2r * a1r - a2i * a1i,
                a2r * a1i + a2i * a1r,
                a2r * b1r - a2i * b1i + b2r,
                a2r * b1i + a2i * b1r + b2i)

    _, _, s_re, s_im = lax.associative_scan(combine, (a_seq_re, a_seq_im, bu_re, bu_im), axis=1)
    y = (jnp.einsum('blgn,gpn->blgp', s_re, c_re.astype(jnp.float32))
         - jnp.einsum('blgn,gpn->blgp', s_im, c_im.astype(jnp.float32))
         + d_skip.astype(jnp.float32) * uf)
    return y.astype(u.dtype)


def hybrid_mixer(xn, positions, w_in, sinks, a_re, a_im, log_dt, b_re, b_im, c_re, c_im,
                 d_skip, w_glu, b_glu, g_attn_out, g_ssm_out, w_o):
    B, L = xn.shape[0], xn.shape[1]
    proj = jnp.einsum('bld,de->ble', xn, w_in)
    q, k, v, u = jnp.split(proj, [D_ATTN, D_ATTN + D_KV, D_ATTN + 2 * D_KV], axis=-1)
    q = rotary(q.reshape(B, L, N_Q_HEADS, HEAD_DIM), positions)
    k = rotary(k.reshape(B, L, N_KV_HEADS, HEAD_DIM), positions)
    v = v.reshape(B, L, N_KV_HEADS, HEAD_DIM)
    attn = sliding_window_attention(q, k, v, sinks)
    y = s5_ssm(u.reshape(B, L, N_SSM_GROUPS, SSM_GROUP), a_re, a_im, log_dt,
               b_re, b_im, c_re, c_im, d_skip).reshape(B, L, D_SSM)
    z = jax.nn.gelu(y)
    ssm = z * jax.nn.sigmoid(jnp.einsum('blc,ce->ble', z, w_glu) + b_glu)
    mixed = jnp.concatenate([rms_norm(attn, g_attn_out), rms_norm(ssm, g_ssm_out)], axis=-1)
    return jnp.einsum('blc,cd->bld', mixed, w_o)


def swiglu(xn, w_gate, w_up, w_down):
    hid = jax.nn.silu(jnp.einsum('bld,df->blf', xn, w_gate)) * jnp.einsum('bld,df->blf', xn, w_up)
    return jnp.einsum('blf,fd->bld', hid, w_down)


def setup_inputs(seed: int = 0) -> dict:
    key = jax.random.key(seed)
    ks = jax.random.split(key, 24)
    f32 = jnp.float32

    def nrm(k, shape, scale):
        return jax.random.normal(k, shape, f32) * scale

    def gain(k, width):
        return 1.0 + nrm(k, (DEPTH, width), 0.05)

    G, N, P = N_SSM_GROUPS, SSM_STATE, SSM_GROUP
    x = nrm(ks[0], (BATCH, SEQ, D_MODEL), 1.0)
    positions = jnp.tile(jnp.arange(SEQ, dtype=jnp.int32)[None, :], (BATCH, 1))
    return {
        'x': x,
        'positions': positions,
        'g_pre_mix': gain(ks[1], D_MODEL),
        'w_in': nrm(ks[2], (DEPTH, D_MODEL, D_IN), D_MODEL ** -0.5),
        'sinks': nrm(ks[3], (DEPTH, N_Q_HEADS), 1.0),
        'a_re': -0.5 + nrm(ks[4], (DEPTH, G, N), 0.01),
        'a_im': math.pi * jnp.arange(N, dtype=f32)[None, None, :] + nrm(ks[5], (DEPTH, G, N), 0.01),
        'log_dt': jax.random.uniform(ks[6], (DEPTH, G), f32, math.log(1e-3), math.log(1e-1)),
        'b_re': nrm(ks[7], (DEPTH, G, N, P), (2 * P) ** -0.5),
        'b_im': nrm(ks[8], (DEPTH, G, N, P), (2 * P) ** -0.5),
        'c_re': nrm(ks[9], (DEPTH, G, P, N), (2 * N) ** -0.5),
        'c_im': nrm(ks[10], (DEPTH, G, P, N), (2 * N) ** -0.5),
        'd_skip': nrm(ks[11], (DEPTH, G, P), 1.0),
        'w_glu': nrm(ks[12], (DEPTH, D_SSM, D_SSM), D_SSM ** -0.5),
        'b_glu': nrm(ks[13], (DEPTH, D_SSM), 0.01),
        'g_attn_out': gain(ks[14], D_ATTN),
        'g_ssm_out': gain(ks[15], D_SSM),
        'w_o': nrm(ks[16], (DEPTH, D_MIX, D_MODEL), D_MIX ** -0.5),
        'g_post_mix': gain(ks[17], D_MODEL),
        'g_pre_ffn': gain(ks[18], D_MODEL),
        'w_gate': nrm(ks[19], (DEPTH, D_MODEL, D_FF), D_MODEL ** -0.5),
        'w_up': nrm(ks[20], (DEPTH, D_MODEL, D_FF), D_MODEL ** -0.5),
        'w_down': nrm(ks[21], (DEPTH, D_FF, D_MODEL), D_FF ** -0.5),
        'g_post_ffn': gain(ks[22], D_MODEL),
    }


def reference(x, positions, g_pre_mix, w_in, sinks, a_re, a_im, log_dt, b_re, b_im, c_re, c_im,
              d_skip, w_glu, b_glu, g_attn_out, g_ssm_out, w_o, g_post_mix, g_pre_ffn,
              w_gate, w_up, w_down, g_post_ffn):
    h = x
    for i in range(DEPTH):
        mix = hybrid_mixer(rms_norm(h, g_pre_mix[i]), positions, w_in[i], sinks[i], a_re[i], a_im[i],
                           log_dt[i], b_re[i], b_im[i], c_re[i], c_im[i], d_skip[i], w_glu[i],
                           b_glu[i], g_attn_out[i], g_ssm_out[i], w_o[i])
        h = h + rms_norm(mix, g_post_mix[i])
        ff = swiglu(rms_norm(h, g_pre_ffn[i]), w_gate[i], w_up[i], w_down[i])
        h = h + rms_norm(ff, g_post_ffn[i])
    return h
```

```python
import math
import numpy as np
from contextlib import ExitStack
import concourse.bass as bass
import concourse.mybir as mybir
from concourse.bass_utils import run_bass_kernel_spmd

F32 = mybir.dt.float32
BF16 = mybir.dt.bfloat16
I32 = mybir.dt.int32
AF = mybir.ActivationFunctionType
ALU = mybir.AluOpType
AX = mybir.AxisListType

ENGS = ["tensor", "vector", "scalar", "gpsimd", "sync"]


class Op:
    __slots__ = ("eng", "fn", "deps", "signal", "semval", "dma", "dsem", "dval", "prev_on_sem")

    def __init__(self, eng, fn, dma):
        self.eng = eng
        self.fn = fn
        self.deps = []
        self.signal = False
        self.semval = 0
        self.dma = dma
        self.dsem = None
        self.dval = 0
        self.prev_on_sem = None


class Prog:
    def __init__(self, nc, n_dma_sems=32):
        self.nc = nc
        self.ops = {e: [] for e in ENGS}
        self.res = {}
        self.n_dma_sems = n_dma_sems
        self.dma_rr = 0
        self.dma_last = [None] * n_dma_sems
        self.dma_cum = [0] * n_dma_sems

    def add(self, eng, fn, reads=(), writes=(), dma=False):
        op = Op(eng, fn, dma)
        deps = {}
        for k in reads:
            st = self.res.get(k)
            if st is not None and st[0] is not None:
                deps[id(st[0])] = st[0]
        for k in writes:
            st = self.res.get(k)
            if st is not None:
                if st[0] is not None:
                    deps[id(st[0])] = st[0]
                for r in st[1]:
                    deps[id(r)] = r
        for k in reads:
            st = self.res.get(k)
            if st is None:
                self.res[k] = [None, [op]]
            else:
                st[1].append(op)
        for k in writes:
            self.res[k] = [op, []]
        for d in deps.values():
            if d is op:
                continue
            if (not d.dma) and d.eng == eng and eng == "tensor":
                continue
            op.deps.append(d)
            d.signal = True
        if dma:
            s = self.dma_rr
            self.dma_rr = (self.dma_rr + 1) % self.n_dma_sems
            op.dsem = s
            self.dma_cum[s] += 16
            op.dval = self.dma_cum[s]
            op.prev_on_sem = self.dma_last[s]
            self.dma_last[s] = op
        self.ops[eng].append(op)
        return op

    def pe(self, fn, reads=(), writes=()):
        return self.add("tensor", fn, reads, writes)

    def dve(self, fn, reads=(), writes=()):
        return self.add("vector", fn, reads, writes)

    def act(self, fn, reads=(), writes=()):
        return self.add("scalar", fn, reads, writes)

    def pool(self, fn, reads=(), writes=()):
        return self.add("vector" if POOL_AS_DVE else "gpsimd", fn, reads, writes)

    def dma(self, eng, out, in_, reads=(), writes=(), **kw):
        return self.add(eng, lambda e: e.dma_start(out=out, in_=in_, **kw), reads, writes, dma=True)

    def barrier(self):
        lasts = []
        for e in ENGS:
            for op in reversed(self.ops[e]):
                if (not op.dma) and op.fn is not None:
                    lasts.append(op)
                    break
        dl = [d for d in self.dma_last if d is not None]
        for e in ENGS:
            op = Op(e, None, False)
            for d in lasts:
                if d.eng != e:
                    op.deps.append(d)
                    d.signal = True
            op.deps.extend(dl)
            self.ops[e].append(op)
        self.res = {}

    def emit(self):
        nc = self.nc
        self.barrier()
        for e in ENGS:
            cum = 0
            for op in self.ops[e]:
                if op.dma:
                    continue
                if op.signal:
                    cum += 1
                    op.semval = cum
            self.sig_counts = getattr(self, "sig_counts", {})
            self.sig_counts[e] = (cum, len(self.ops[e]))
        with ExitStack() as st:
            esem = {e: st.enter_context(nc.semaphore("es_" + e)) for e in ENGS}
            dsem = [st.enter_context(nc.semaphore("ds_%d" % i)) for i in range(self.n_dma_sems)]
            block = st.enter_context(nc.Block())

            def run(eng, e):
                waited = {}

                def wait_for(d):
                    if d.dma:
                        key, sem, val = ("d", d.dsem), dsem[d.dsem], d.dval
                    else:
                        key, sem, val = ("e", d.eng), esem[d.eng], d.semval
                    if waited.get(key, 0) < val:
                        eng.wait_ge(sem, val)
                        waited[key] = val

                for op in self.ops[e]:
                    for d in op.deps:
                        wait_for(d)
                    if op.dma and op.prev_on_sem is not None:
                        wait_for(op.prev_on_sem)
                    if op.fn is None:
                        continue
                    inst = op.fn(eng)
                    if op.dma:
                        inst.then_inc(dsem[op.dsem], 16)
                    elif op.signal:
                        inst.then_inc(esem[e], 1)

            for e in ENGS:
                getattr(block, e)(lambda eng, e=e: run(eng, e))


D = 2048
L = 2048
NT = 16
DFF = 5632
NFC = 44
EPS = 1e-6
BASE = 17408
STOP = -1
POOL_AS_DVE = True
ATT_LEVEL = 99
ATT_TILES = 16
INV2PI = 1.0 / (2.0 * math.pi)
TWO_PI = 2.0 * math.pi * (1.0 - 2e-7)
MAGIC = 12582912.0
GELU_C = 2.0 * math.sqrt(2.0 / math.pi)

C_ID = 0
C_MASK = 128
C_INVF = 384
C_SINK = 416
C_GSSM = 432
C_BGLU = 440
C_DSK = 448
C_ARA = 456
C_AIA = 488
C_LDA = 520
NSM = 552


def build_nc():
    nc = bass.Bass("TRN2", target_bir_lowering=False)

    def din(name, shape, dt=F32):
        return nc.dram_tensor(name, list(shape), dt, kind="ExternalInput").ap()

    x = din("x", [L, D])
    pos = din("pos", [128, NT], I32)
    smalls = din("smalls", [128, NSM])
    gvecs = din("gvecs", [5, D])
    w_in = din("w_in", [D, 2560])
    w_glu = din("w_glu", [1024, 1024])
    w_o = din("w_o", [D, D])
    w_gate = din("w_gate", [D, DFF])
    w_up = din("w_up", [D, DFF])
    w_down = din("w_down", [DFF, D])
    lb3 = din("lb3", [128, 3, 1024])
    bexp = din("bexp", [128, 2, 1024])
    maska = din("maska", [128, 4])
    cexp = din("cexp", [128, 2, 4096])
    tokc = din("tokc", [128, 2048])
    out = nc.dram_tensor("out", [L, D], F32, kind="ExternalOutput").ap()
    hscr = nc.dram_tensor("hscr", [L, D], F32, kind="Internal").ap()

    cnt = [0]

    def sb(off, shape, dt):
        cnt[0] += 1
        return nc.alloc_sbuf_tensor_at("t%d" % cnt[0], list(shape), dt, offset=BASE + off)

    def cap(t, off, dims):
        return bass.AP(tensor=t, offset=off, ap=[list(d) for d in dims])

    P = Prog(nc)
    with ExitStack() as st:
        ps = st.enter_context(nc.psum_tensor("ps", [128, 8, 512], F32))
        psb = ps[:, :, :].bitcast(BF16)

        actT = sb(0, [128, 16, 2048], BF16)
        uT = sb(65536, [128, 8, 2048], BF16)
        qT = sb(98304, [128, 8, 2048], BF16)
        kT2 = sb(131072, [128, 4, 2176], BF16)
        v1 = sb(148480, [128, 17, 4, 65], BF16)
        cosT = sb(157696, [128, 16, 32], F32)
        sinT = sb(157696 + 2048, [128, 16, 32], F32)
        nsinT = sb(157696 + 4096, [128, 16, 32], F32)
        SCR = 163840
        CONST = 207872
        sm = sb(CONST, [128, NSM], F32)
        identb = sb(CONST + 2208, [128, 128], BF16)
        maskb = sb(CONST + 2464, [128, 2, 128], BF16)
        stat = sb(CONST + 2976, [128, 64], F32)
        ngs = sb(CONST + 3232, [128, 4], F32)
        onesb = sb(CONST + 3264, [128, 2], BF16)
        rstd_s = sb(CONST + 3296, [128, 16], F32)
        posi = sb(CONST + 3360, [128, 16], I32)
        posf = sb(CONST + 3424, [128, 16], F32)
        thp = sb(CONST + 3488, [128, 32], F32)
        rho = sb(CONST + 3616, [128, 32], F32)
        rlast = sb(CONST + 3744, [128, 32, 2], F32)
        identf = sm[:, C_ID:C_ID + 128]
        magp = sb(CONST + 4000, [128, 4], F32)
        maskA = sb(CONST + 4032, [128, 4], F32)

        P.dma("sync", sm[:, :], smalls, writes=["sm"])
        P.dma("sync", posi[:, :], pos, writes=["posi"])
        P.dma("sync", maskA[:, :], maska, writes=["maskA"])
        P.dve(lambda e: e.tensor_copy(out=identb[:, :], in_=sm[:, C_ID:C_ID + 128]), reads=["sm"], writes=["identb"])
        P.dve(lambda e: e.tensor_copy(out=maskb[:, :, :], in_=sm[:, C_MASK:C_MASK + 256].rearrange("p (a b) -> p a b", a=2)),
              reads=["sm"], writes=["maskb"])
        P.dve(lambda e: e.memset(onesb[:, :], 1.0), writes=["onesb"])
        P.dve(lambda e: e.memset(magp[:, 0:1], MAGIC), writes=["magp0"])
        P.dve(lambda e: e.memset(magp[:, 1:2], -MAGIC), writes=["magp1"])
        P.dve(lambda e: e.memset(magp[:, 2:3], math.pi / 2), writes=["magp2"])
        P.dve(lambda e: e.tensor_reduce(out=ngs[:, :], in_=sm[:, C_SINK:C_SINK + 16].rearrange("p (a b) -> p a b", a=4),
                                        axis=AX.X, op=ALU.max, negate=True), reads=["sm"], writes=["ngs"])
        P.dve(lambda e: e.memset(kT2[:, :, 0:128], 0.0), writes=["kpad"])
        P.dve(lambda e: e.memset(v1[:, 0, :, :], 0.0), writes=["vpad"])
        P.dve(lambda e: e.memset(v1[:, 1:17, :, 64:65], 1.0), writes=["vones"])
        P.dve(lambda e: e.tensor_copy(out=posf[:, :], in_=posi[:, :]), reads=["posi"], writes=["posf"])

        rt = [sb(65536 + 2048 * i, [128, 16, 32], F32) for i in range(4)]
        P.dve(lambda e: e.tensor_tensor(out=rt[0][:, :, :], in0=cap(posf, 0, [[16, 128], [1, 16], [0, 32]]),
                                        in1=cap(sm, C_INVF, [[NSM, 128], [0, 16], [1, 32]]), op=ALU.mult),
              reads=["posf", "sm"], writes=["rt0"])
        P.dve(lambda e: e.tensor_scalar(out=rt[0][:, :, :], in0=rt[0][:, :, :], scalar1=INV2PI, scalar2=None, op0=ALU.mult),
              reads=["rt0"], writes=["rt0"])
        P.dve(lambda e: e.tensor_scalar(out=rt[1][:, :, :], in0=rt[0][:, :, :], scalar1=MAGIC, scalar2=MAGIC, op0=ALU.add,
                                        op1=ALU.subtract), reads=["rt0"], writes=["rt1"])
        P.dve(lambda e: e.tensor_tensor(out=rt[2][:, :, :], in0=rt[0][:, :, :], in1=rt[1][:, :, :], op=ALU.subtract),
              reads=["rt0", "rt1"], writes=["rt2"])
        P.dve(lambda e: e.scalar_tensor_tensor(out=rt[3][:, :, :], in0=rt[2][:, :, :], scalar=-1.0, in1=rt[2][:, :, :],
                                               op0=ALU.mult, op1=ALU.max), reads=["rt2"], writes=["rt3"])
        P.act(lambda e: e.activation(out=sinT[:, :, :], in_=rt[2][:, :, :], func=AF.Sin, scale=TWO_PI), reads=["rt2"], writes=["sinT"])
        P.act(lambda e: e.activation(out=nsinT[:, :, :], in_=rt[2][:, :, :], func=AF.Sin, scale=-TWO_PI), reads=["rt2"], writes=["nsinT"])
        P.act(lambda e: e.activation(out=cosT[:, :, :], in_=rt[3][:, :, :], func=AF.Sin, scale=-TWO_PI, bias=math.pi / 2),
              reads=["rt3"], writes=["cosT"])
        if STOP == 0:
            P.emit()
            return nc

        def rstd_from(ss_key, ss_ap, dst_ap, dst_key, n, tmp_ap, tmp_key):
            P.dve(lambda e: e.tensor_scalar(out=tmp_ap, in0=ss_ap, scalar1=1.0 / n, scalar2=EPS, op0=ALU.mult, op1=ALU.add),
                  reads=[ss_key], writes=[tmp_key])
            P.act(lambda e: e.activation(out=tmp_ap, in_=tmp_ap, func=AF.Ln), reads=[tmp_key], writes=[tmp_key])
            P.act(lambda e: e.activation(out=dst_ap, in_=tmp_ap, func=AF.Exp, scale=-0.5), reads=[tmp_key], writes=[dst_key])

        xt = [sb(SCR + 8192 * i, [128, 2048], F32) for i in range(2)]
        xs = [sb(SCR + 16384 + 4096 * i, [128, 2048], BF16) for i in range(2)]
        gbc = sb(SCR + 24576, [128, 2048], F32)
        junk = sb(SCR + 32768, [128, 2048], BF16)
        P.dma("sync", gbc[:, :], gvecs[0:1, :].partition_broadcast(128), writes=["gbc"])

        def a1_pre(tt):
            b = tt % 2
            c0 = 4 * b
            P.dma("sync", xt[b][:, :], x[tt * 128:(tt + 1) * 128, :], writes=[("xt", b)])
            P.dve(lambda e: e.scalar_tensor_tensor(out=junk[:, :], in0=xt[b][:, :], scalar=1.0, in1=xt[b][:, :],
                                                   op0=ALU.mult, op1=ALU.mult, accum_out=stat[:, c0:c0 + 1]),
                  reads=[("xt", b)], writes=["junk", ("st", c0)])
            rstd_from(("st", c0), stat[:, c0:c0 + 1], stat[:, c0 + 2:c0 + 3], ("st", c0 + 2), D, stat[:, c0 + 1:c0 + 2],
                      ("st", c0 + 1))
            P.dve(lambda e: e.scalar_tensor_tensor(out=xs[b][:, :], in0=xt[b][:, :], scalar=stat[:, c0 + 2:c0 + 3],
                                                   in1=gbc[:, :], op0=ALU.mult, op1=ALU.mult),
                  reads=[("xt", b), ("st", c0 + 2), "gbc"], writes=[("xs", b)])

        def a1_post(tt):
            b = tt % 2
            bk = 2 * b

            def tr(e):
                last = None
                for c in range(16):
                    last = e.transpose(out=psb[:, bk + c // 8, (c % 8) * 128:(c % 8) * 128 + 128],
                                       in_=xs[b][:, c * 128:(c + 1) * 128], identity=identb[:, :])
                return last
            P.pe(tr, reads=[("xs", b), "identb"], writes=[("ps", bk), ("ps", bk + 1)])
            P.act(lambda e: e.activation(out=actT[:, 0:8, tt * 128:(tt + 1) * 128],
                                         in_=psb[:, bk, :].rearrange("p (a b) -> p a b", a=8), func=AF.Copy),
                  reads=[("ps", bk)], writes=[("actT", tt, 0)])
            P.act(lambda e: e.activation(out=actT[:, 8:16, tt * 128:(tt + 1) * 128],
                                         in_=psb[:, bk + 1, :].rearrange("p (a b) -> p a b", a=8), func=AF.Copy),
                  reads=[("ps", bk + 1)], writes=[("actT", tt, 1)])

        a1_pre(0)
        for tt in range(NT):
            if tt + 1 < NT:
                a1_pre(tt + 1)
            a1_post(tt)
        P.barrier()
        if STOP == 1:
            P.emit()
            return nc

        wb = [sb(SCR + 8192 * i, [128, 16, 256], BF16) for i in range(2)]
        rAs = [sb(SCR + 16384 + 1024 * i, [128, 256], F32) for i in range(2)]
        rBs = [sb(SCR + 18432 + 1024 * i, [128, 256], F32) for i in range(2)]
        qrs = [sb(SCR + 20480 + 512 * i, [128, 256], BF16) for i in range(2)]
        kds = [sb(SCR + 21504 + 1024 * i, [128, 4, 2, 64], BF16) for i in range(2)]
        w_in_v = w_in.rearrange("(c p) n -> p c n", p=128)
        jobs = []
        for cb in range(6):
            for tt in range(NT):
                jobs.append((cb, tt, len(jobs) % 4, len(jobs) % 2))
        loaded = set()

        def a2_load(cb):
            if cb in loaded or cb >= 10:
                return
            loaded.add(cb)
            P.dma("gpsimd", wb[cb % 2][:, :, :], w_in_v[:, :, cb * 256:(cb + 1) * 256], writes=[("wb", cb % 2)])

        def a2_M(job):
            cb, tt, bk, par = job
            a2_load(cb)
            wbuf = wb[cb % 2]

            def mm(e):
                last = None
                for c in range(16):
                    last = e.matmul(ps[:, bk, 0:256], lhsT=actT[:, c, tt * 128:(tt + 1) * 128], rhs=wbuf[:, c, :],
                                    start=(c == 0), stop=(c == 15))
                return last
            P.pe(mm, reads=[("wb", cb % 2), ("actT", tt, 0), ("actT", tt, 1)], writes=[("ps", bk)])

        def a2_post(job):
            cb, tt, bk, par = job
            if cb == 5:
                P.act(lambda e: e.activation(out=v1[:, tt + 1, :, 0:64], in_=ps[:, bk, 0:256].rearrange("p (a b) -> p a b", a=4),
                                             func=AF.Copy), reads=[("ps", bk)], writes=[("v1", tt)])
                return
            rA, rB, qr, kd = rAs[par], rBs[par], qrs[par], kds[par]
            pv = ps[:, bk, 0:256].rearrange("p (h t d) -> p h t d", h=4, t=2)
            rBv = rB[:, :].rearrange("p (h t d) -> p h t d", h=4, t=2)
            P.dve(lambda e: e.tensor_tensor(out=rA[:, :].rearrange("p (h t d) -> p h t d", h=4, t=2), in0=pv,
                                            in1=cap(cosT, tt * 32, [[512, 128], [0, 4], [0, 2], [1, 32]]), op=ALU.mult),
                  reads=[("ps", bk), "cosT"], writes=[("rA", par)])
            P.dve(lambda e: e.tensor_tensor(out=rBv[:, :, 0, :], in0=pv[:, :, 1, :],
                                            in1=cap(nsinT, tt * 32, [[512, 128], [0, 4], [1, 32]]), op=ALU.mult),
                  reads=[("ps", bk), "nsinT"], writes=[("rB0", par)])
            P.dve(lambda e: e.tensor_tensor(out=rBv[:, :, 1, :], in0=pv[:, :, 0, :],
                                            in1=cap(sinT, tt * 32, [[512, 128], [0, 4], [1, 32]]), op=ALU.mult),
                  reads=[("ps", bk), "sinT"], writes=[("rB1", par)])
            rk = [("rA", par), ("rB0", par), ("rB1", par)]
            if cb < 4:
                P.dve(lambda e: e.tensor_tensor(out=qr[:, :], in0=rA[:, :], in1=rB[:, :], op=ALU.add), reads=rk,
                      writes=[("qr", par)])
                tb2 = 4 + par

                def trq(e):
                    last = None
                    for j in range(2):
                        last = e.transpose(out=psb[:, tb2, j * 128:(j + 1) * 128], in_=qr[:, j * 128:(j + 1) * 128],
                                           identity=identb[:, :])
                    return last
                P.pe(trq, reads=[("qr", par)], writes=[("ps", tb2)])
                P.act(lambda e: e.activation(out=qT[:, 2 * cb:2 * cb + 2, tt * 128:(tt + 1) * 128],
                                             in_=psb[:, tb2, 0:256].rearrange("p (a b) -> p a b", a=2), func=AF.Copy),
                      reads=[("ps", tb2)], writes=[("qT", cb, tt)])
            else:
                for dup in range(2):
                    P.dve(lambda e, dup=dup: e.tensor_tensor(out=kd[:, :, dup, :], in0=rA[:, :].rearrange("p (h d) -> p h d", h=4),
                                                             in1=rB[:, :].rearrange("p (h d) -> p h d", h=4), op=ALU.add),
                          reads=rk, writes=[("kd", par, dup)])
                tb2 = 6 + par

                def trk(e):
                    last = None
                    for j in range(4):
                        last = e.transpose(out=psb[:, tb2, j * 128:(j + 1) * 128],
                                           in_=kd[:, j, :, :].rearrange("p a b -> p (a b)"), identity=identb[:, :])
                    return last
                P.pe(trk, reads=[("kd", par, 0), ("kd", par, 1)], writes=[("ps", tb2)])
                P.act(lambda e: e.activation(out=kT2[:, :, 128 + tt * 128:128 + (tt + 1) * 128],
                                             in_=psb[:, tb2, 0:512].rearrange("p (a b) -> p a b", a=4), func=AF.Copy),
                      reads=[("ps", tb2)], writes=[("kT2", tt)])

        a2_load(0)
        a2_load(1)
        a2_M(jobs[0])
        for ji in range(len(jobs)):
            if ji + 1 < len(jobs):
                a2_M(jobs[ji + 1])
            a2_post(jobs[ji])
            if jobs[ji][1] == NT - 1:
                a2_load(jobs[ji][0] + 2)
        pc = 0
        for cb in range(6, 10):
            a2_load(cb)
            wbuf = wb[cb % 2]
            for j in range(2):
                uc = (cb - 6) * 2 + j
                for tb in range(4):
                    bk = pc % 4
                    pc += 1

                    def mmu(e, j=j, tb=tb, bk=bk, wbuf=wbuf):
                        last = None
                        for c in range(16):
                            last = e.matmul(ps[:, bk, :], lhsT=wbuf[:, c, j * 128:(j + 1) * 128],
                                            rhs=actT[:, c, tb * 512:(tb + 1) * 512], start=(c == 0), stop=(c == 15))
                        return last
                    P.pe(mmu, reads=[("wb", cb % 2)], writes=[("ps", bk)])
                    if (tb % 2) == 0:
                        P.act(lambda e, uc=uc, tb=tb, bk=bk: e.activation(out=uT[:, uc, tb * 512:(tb + 1) * 512],
                                                                          in_=ps[:, bk, :], func=AF.Copy),
                              reads=[("ps", bk)], writes=[("uT", uc, tb // 2)])
                    else:
                        P.dve(lambda e, uc=uc, tb=tb, bk=bk: e.tensor_copy(out=uT[:, uc, tb * 512:(tb + 1) * 512],
                                                                           in_=ps[:, bk, :]),
                              reads=[("ps", bk)], writes=[("uT", uc, tb // 2)])
            a2_load(cb + 2)
        P.barrier()
        if STOP == 2:
            P.emit()
            return nc

        Pbs = [sb(SCR + 2048 * i, [128, 1024], BF16) for i in range(2)]
        PTs = [sb(SCR + 4096 + 2048 * i, [128, 4, 2, 128], BF16) for i in range(2)]
        attn = [sb(SCR + 8192 + 4096 * i, [128, 1024], F32) for i in range(2)]
        anbs = [sb(SCR + 16384 + 2048 * i, [128, 1024], BF16) for i in range(2)]
        gat = sb(SCR + 20480, [128, 1024], F32)
        ajunk = sb(SCR + 24576, [128, 1024], BF16)
        asts = [sb(SCR + 26624 + 64 * i, [128, 16], F32) for i in range(4)]
        P.dma("sync", gat[:, :], gvecs[4:5, 0:1024].partition_broadcast(128), writes=["gat"])
        aits = [(tt, j) for tt in range(min(NT, ATT_TILES)) for j in range(4)]

        def att_X(i):
            tt, j = aits[i]
            sbk = 2 * (i % 2)
            ast = asts[i % 4]
            Pb = Pbs[i % 2]
            ka = ("ast", i % 4)

            def qk(e):
                last = None
                for hh in range(4):
                    h = 4 * j + hh
                    pb = (h % 2) * 64
                    last = e.matmul(ps[:, sbk + hh % 2, (hh // 2) * 256:(hh // 2) * 256 + 256],
                                    lhsT=qT[pb:pb + 64, h // 2, tt * 128:(tt + 1) * 128],
                                    rhs=kT2[pb:pb + 64, j, tt * 128:tt * 128 + 256], start=True, stop=True)
                return last
            P.pe(qk, reads=[], writes=[("ps", sbk), ("ps", sbk + 1)])
            sc = ps[:, sbk:sbk + 2, :]
            P.dve(lambda e: e.tensor_reduce(out=ast[:, 0:1], in_=sc, axis=AX.XY, op=ALU.max, negate=True),
                  reads=[("ps", sbk), ("ps", sbk + 1)], writes=[ka])
            P.dve(lambda e: e.tensor_scalar(out=ast[:, 1:2], in0=ast[:, 0:1], scalar1=0.125, scalar2=ngs[:, j:j + 1],
                                            op0=ALU.mult, op1=ALU.min), reads=[ka, "ngs"], writes=[ka])
            P.act(lambda e: e.activation(out=Pb[:, :].rearrange("p (a b) -> p a b", a=2), in_=sc, func=AF.Exp,
                                         bias=ast[:, 1:2], scale=0.125),
                  reads=[("ps", sbk), ("ps", sbk + 1), ka], writes=[("Pb", i % 2)])
            P.act(lambda e: e.activation(out=ast[:, 4:8], in_=sm[:, C_SINK + 4 * j:C_SINK + 4 * j + 4], func=AF.Exp,
                                         bias=ast[:, 1:2], scale=1.0), reads=[ka], writes=[("ase", i % 4)])

        def att_Y1(i):
            Pb = Pbs[i % 2]
            PT = PTs[i % 2]

            def trp(e):
                last = None
                for k in range(8):
                    last = e.transpose(out=psb[:, 4, k * 128:(k + 1) * 128], in_=Pb[:, k * 128:(k + 1) * 128],
                                       identity=identb[:, :])
                return last
            P.pe(trp, reads=[("Pb", i % 2)], writes=[("ps", 4)])
            P.dve(lambda e: e.tensor_tensor(out=PT[:, :, :, :], in0=psb[:, 4, :].rearrange("p (h k q) -> p h k q", h=4, k=2),
                                            in1=cap(maskb, 0, [[256, 128], [0, 4], [128, 2], [1, 128]]), op=ALU.mult),
                  reads=[("ps", 4), "maskb"], writes=[("PT", i % 2)])

        def att_Y2(i):
            tt, j = aits[i]
            ab = tt % 2
            PT = PTs[i % 2]
            ast = asts[i % 4]

            def pv_(e):
                last = None
                for i4 in range(4):
                    hh = (0, 2, 1, 3)[i4]
                    for kb in range(2):
                        last = e.matmul(ps[:, 5, hh * 65:hh * 65 + 65], lhsT=PT[:, i4, kb, :], rhs=v1[:, tt + kb, j, :],
                                        start=(kb == 0), stop=(kb == 1))
                return last
            P.pe(pv_, reads=[("PT", i % 2)], writes=[("ps", 5)])
            po = ps[:, 5, 0:260].rearrange("p (h d) -> p h d", h=4)
            P.dve(lambda e: e.tensor_tensor(out=ast[:, 8:12], in0=po[:, :, 64], in1=ast[:, 4:8], op=ALU.add),
                  reads=[("ps", 5), ("ase", i % 4)], writes=[("aden", i % 4)])
            P.dve(lambda e: e.reciprocal(out=ast[:, 12:16], in_=ast[:, 8:12]), reads=[("aden", i % 4)], writes=[("ard", i % 4)])
            P.dve(lambda e: e.tensor_tensor(out=attn[ab][:, j * 256:(j + 1) * 256].rearrange("p (h d) -> p h d", h=4),
                                            in0=po[:, :, 0:64], in1=cap(ast, 12, [[16, 128], [1, 4], [0, 64]]), op=ALU.mult),
                  reads=[("ps", 5), ("ard", i % 4)], writes=[("attn", ab, j)])
            if j != 3:
                return
            ak = [("attn", ab, jj) for jj in range(4)]
            c0 = 32 + 8 * ab
            anb = anbs[ab]
            P.dve(lambda e: e.scalar_tensor_tensor(out=ajunk[:, :], in0=attn[ab][:, :], scalar=1.0, in1=attn[ab][:, :],
                                                   op0=ALU.mult, op1=ALU.mult, accum_out=stat[:, c0:c0 + 1]),
                  reads=ak, writes=["ajunk", ("st", c0)])
            rstd_from(("st", c0), stat[:, c0:c0 + 1], stat[:, c0 + 2:c0 + 3], ("st", c0 + 2), 1024, stat[:, c0 + 1:c0 + 2],
                      ("st", c0 + 1))
            P.dve(lambda e: e.scalar_tensor_tensor(out=anb[:, :], in0=attn[ab][:, :], scalar=stat[:, c0 + 2:c0 + 3],
                                                   in1=gat[:, :], op0=ALU.mult, op1=ALU.mult),
                  reads=ak + [("st", c0 + 2), "gat"], writes=[("anb", ab)])

            def tra(e):
                last = None
                for c in range(8):
                    last = e.transpose(out=psb[:, 6, c * 128:(c + 1) * 128], in_=anb[:, c * 128:(c + 1) * 128],
                                       identity=identb[:, :])
                return last
            P.pe(tra, reads=[("anb", ab)], writes=[("ps", 6)])
            P.act(lambda e: e.activation(out=actT[:, 0:8, tt * 128:(tt + 1) * 128],
                                         in_=psb[:, 6, :].rearrange("p (a b) -> p a b", a=8), func=AF.Copy),
                  reads=[("ps", 6)], writes=[("mixT", tt)])

        na = len(aits)
        att_X(0)
        if na > 1:
            att_X(1)
        att_Y1(0)
        for i in range(na):
            if i + 2 < na:
                att_X(i + 2)
            if i + 1 < na:
                att_Y1(i + 1)
            att_Y2(i)
        P.barrier()
        if STOP == 3:
            P.emit()
            return nc

        R0 = 98304
        BbTr = sb(R0, [128, 8, 4, 128], BF16)
        BbTi = sb(R0 + 8192, [128, 8, 4, 128], BF16)
        CTr = sb(R0 + 16384, [128, 32, 128], BF16)
        CTi = sb(R0 + 24576, [128, 32, 128], BF16)
        tok = sb(R0 + 32768, [128, 2048], F32)
        fre = sb(49152, [128, 1024], F32)
        fim = sb(53248, [128, 1024], F32)
        lrB = sb(57344, [128, 1024], F32)
        liB = sb(61440, [128, 1024], F32)
        LBr = sb(R0 + 40960, [128, 8, 4, 128], BF16)
        LBi = sb(R0 + 49152, [128, 8, 4, 128], BF16)
        K1blk = sb(R0 + 57344, [128, 8, 128], BF16)
        CIr = sb(32768, [128, 32, 128], BF16)
        CIi = sb(40960, [128, 32, 128], BF16)
        S_ = [sb(SCR + 4096 * i, [128, 1024], F32) for i in range(10)]
        P.dma("sync", tok[:, :], tokc, writes=["tok"])
        P.act(lambda e: e.activation(out=stat[:, 32:64], in_=sm[:, C_LDA:C_LDA + 32], func=AF.Exp), reads=[], writes=["dtA"])
        P.dve(lambda e: e.tensor_tensor(out=rho[:, :], in0=sm[:, C_ARA:C_ARA + 32], in1=stat[:, 32:64], op=ALU.mult),
              reads=["dtA"], writes=["rho0"])
        P.act(lambda e: e.activation(out=rho[:, :], in_=rho[:, :], func=AF.Exp), reads=["rho0"], writes=["rho"])
        P.dve(lambda e: e.scalar_tensor_tensor(out=thp[:, :], in0=sm[:, C_AIA:C_AIA + 32], scalar=INV2PI, in1=stat[:, 32:64],
                                               op0=ALU.mult, op1=ALU.mult), reads=["dtA"], writes=["thp"])
        AR, AI, LD = S_[0], S_[1], S_[2]
        for i, t_ in enumerate((AR, AI, LD)):
            P.dma("sync", t_[:, :], lb3[:, i, :], writes=[("S", i)])
        P.act(lambda e: e.activation(out=LD[:, :], in_=LD[:, :], func=AF.Exp), reads=[("S", 2)], writes=[("S", 2)])
        P.dve(lambda e: e.tensor_tensor(out=S_[3][:, :], in0=AR[:, :], in1=LD[:, :], op=ALU.mult), reads=[("S", 0), ("S", 2)],
              writes=[("S", 3)])
        P.act(lambda e: e.activation(out=S_[3][:, :], in_=S_[3][:, :], func=AF.Exp), reads=[("S", 3)], writes=[("S", 3)])
        P.dve(lambda e: e.scalar_tensor_tensor(out=S_[4][:, :], in0=AI[:, :], scalar=INV2PI, in1=LD[:, :], op0=ALU.mult,
                                               op1=ALU.mult), reads=[("S", 1), ("S", 2)], writes=[("S", 4)])
        P.dve(lambda e: e.tensor_scalar(out=S_[5][:, :], in0=S_[4][:, :], scalar1=MAGIC, scalar2=MAGIC, op0=ALU.add,
                                        op1=ALU.subtract), reads=[("S", 4)], writes=[("S", 5)])
        P.dve(lambda e: e.tensor_tensor(out=S_[4][:, :], in0=S_[4][:, :], in1=S_[5][:, :], op=ALU.subtract),
              reads=[("S", 4), ("S", 5)], writes=[("S", 4)])
        P.dve(lambda e: e.scalar_tensor_tensor(out=S_[5][:, :], in0=S_[4][:, :], scalar=-1.0, in1=S_[4][:, :], op0=ALU.mult,
                                               op1=ALU.max), reads=[("S", 4)], writes=[("S", 5)])
        P.act(lambda e: e.activation(out=S_[6][:, :], in_=S_[4][:, :], func=AF.Sin, scale=TWO_PI), reads=[("S", 4)],
              writes=[("S", 6)])
        P.act(lambda e: e.activation(out=S_[7][:, :], in_=S_[5][:, :], func=AF.Sin, scale=-TWO_PI, bias=math.pi / 2),
              reads=[("S", 5)], writes=[("S", 7)])
        P.dve(lambda e: e.tensor_tensor(out=S_[6][:, :], in0=S_[6][:, :], in1=S_[3][:, :], op=ALU.mult),
              reads=[("S", 6), ("S", 3)], writes=[("S", 6)])
        P.dve(lambda e: e.tensor_tensor(out=S_[7][:, :], in0=S_[7][:, :], in1=S_[3][:, :], op=ALU.mult),
              reads=[("S", 7), ("S", 3)], writes=[("S", 7)])
        P.act(lambda e: e.activation(out=lrB[:, :], in_=S_[7][:, :], func=AF.Copy), reads=[("S", 7)], writes=["lrB"])
        P.act(lambda e: e.activation(out=liB[:, :], in_=S_[6][:, :], func=AF.Copy), reads=[("S", 6)], writes=["liB"])
        P.dve(lambda e: e.tensor_scalar(out=S_[7][:, :], in0=S_[7][:, :], scalar1=-1.0, scalar2=None, op0=ALU.add),
              reads=[("S", 7), "lrB"], writes=[("S", 7)])
        P.dve(lambda e: e.tensor_tensor(out=S_[3][:, :], in0=AR[:, :], in1=AR[:, :], op=ALU.mult), reads=[("S", 0)],
              writes=[("S", 3)])
        P.dve(lambda e: e.tensor_tensor(out=S_[4][:, :], in0=AI[:, :], in1=AI[:, :], op=ALU.mult), reads=[("S", 1)],
              writes=[("S", 4)])
        P.dve(lambda e: e.tensor_tensor(out=S_[3][:, :], in0=S_[3][:, :], in1=S_[4][:, :], op=ALU.add),
              reads=[("S", 3), ("S", 4)], writes=[("S", 3)])
        P.dve(lambda e: e.reciprocal(out=S_[3][:, :], in_=S_[3][:, :]), reads=[("S", 3)], writes=[("S", 3)])
        P.dve(lambda e: e.tensor_tensor(out=S_[4][:, :], in0=S_[7][:, :], in1=AR[:, :], op=ALU.mult),
              reads=[("S", 7), ("S", 0)], writes=[("S", 4)])
        P.dve(lambda e: e.tensor_tensor(out=S_[5][:, :], in0=S_[6][:, :], in1=AI[:, :], op=ALU.mult),
              reads=[("S", 6), ("S", 1)], writes=[("S", 5)])
        P.dve(lambda e: e.tensor_tensor(out=S_[4][:, :], in0=S_[4][:, :], in1=S_[5][:, :], op=ALU.add),
              reads=[("S", 4), ("S", 5)], writes=[("S", 4)])
        P.dve(lambda e: e.tensor_tensor(out=fre[:, :], in0=S_[4][:, :], in1=S_[3][:, :], op=ALU.mult),
              reads=[("S", 4), ("S", 3)], writes=["fre"])
        P.dve(lambda e: e.tensor_tensor(out=S_[4][:, :], in0=S_[6][:, :], in1=AR[:, :], op=ALU.mult),
              reads=[("S", 6), ("S", 0)], writes=[("S", 4)])
        P.dve(lambda e: e.tensor_tensor(out=S_[5][:, :], in0=S_[7][:, :], in1=AI[:, :], op=ALU.mult),
              reads=[("S", 7), ("S", 1)], writes=[("S", 5)])
        P.dve(lambda e: e.tensor_tensor(out=S_[4][:, :], in0=S_[4][:, :], in1=S_[5][:, :], op=ALU.subtract),
              reads=[("S", 4), ("S", 5)], writes=[("S", 4)])
        P.dve(lambda e: e.tensor_tensor(out=fim[:, :], in0=S_[4][:, :], in1=S_[3][:, :], op=ALU.mult),
              reads=[("S", 4), ("S", 3)], writes=["fim"])
        P.barrier()
        if STOP == 4:
            P.emit()
            return nc
        Bq = [sb(SCR + 4096 * i, [128, 1024], F32) for i in range(8)]
        Bcr, Bci, T1, T2, Bbr_, Bbi_, Lr_, Li_ = Bq
        P.dma("sync", Bcr[:, :], bexp[:, 0, :], writes=["Bcr"])
        P.dma("sync", Bci[:, :], bexp[:, 1, :], writes=["Bci"])

        def cmul(outr, outi, ar, ai, br, bi, kr, ki):
            P.dve(lambda e: e.tensor_tensor(out=T1[:, :], in0=ar[:, :], in1=br[:, :], op=ALU.mult), reads=kr, writes=["T1"])
            P.dve(lambda e: e.tensor_tensor(out=T2[:, :], in0=ai[:, :], in1=bi[:, :], op=ALU.mult), reads=kr, writes=["T2"])
            P.dve(lambda e: e.tensor_tensor(out=outr[:, :], in0=T1[:, :], in1=T2[:, :], op=ALU.subtract), reads=["T1", "T2"],
                  writes=[ki + "r"])
            P.dve(lambda e: e.tensor_tensor(out=T1[:, :], in0=ar[:, :], in1=bi[:, :], op=ALU.mult), reads=kr + [ki + "r"],
                  writes=["T1"])
            P.dve(lambda e: e.tensor_tensor(out=T2[:, :], in0=ai[:, :], in1=br[:, :], op=ALU.mult), reads=kr + [ki + "r"],
                  writes=["T2"])
            P.dve(lambda e: e.tensor_tensor(out=outi[:, :], in0=T1[:, :], in1=T2[:, :], op=ALU.add), reads=["T1", "T2"],
                  writes=[ki + "i"])
        cmul(Bbr_, Bbi_, fre, fim, Bcr, Bci, ["Bcr", "Bci", "fre", "fim"], "Bb")
        cmul(Lr_, Li_, lrB, liB, Bbr_, Bbi_, ["Bbr", "Bbi", "lrB", "liB"], "L")
        for src, dst, k in ((Bbr_, BbTr, "Bbr"), (Bbi_, BbTi, "Bbi"), (Lr_, LBr, "Lr"), (Li_, LBi, "Li")):
            for a in range(4):
                P.dve(lambda e, src=src, dst=dst, a=a: e.tensor_scalar(
                    out=dst[:, :, a, :], in0=src[:, :].rearrange("p (k n) -> p k n", k=8), scalar1=maskA[:, a:a + 1],
                    scalar2=None, op0=ALU.mult), reads=[k, "maskA"], writes=[("exp", k, a)])
        P.barrier()
        if STOP == 5:
            P.emit()
            return nc
        cA = sb(SCR + 40960, [128, 32], F32)
        sA = sb(SCR + 40960 + 128, [128, 32], F32)
        tA = sb(SCR + 40960 + 256, [128, 32], F32)
        uA = sb(SCR + 40960 + 384, [128, 32], F32)
        rho2 = stat[:, 32:64]
        P.dve(lambda e: e.tensor_scalar(out=tA[:, :], in0=thp[:, :], scalar1=MAGIC, scalar2=MAGIC, op0=ALU.add,
                                        op1=ALU.subtract), reads=[], writes=["tA"])
        P.dve(lambda e: e.tensor_tensor(out=tA[:, :], in0=thp[:, :], in1=tA[:, :], op=ALU.subtract), reads=["tA"],
              writes=["tA"])
        P.dve(lambda e: e.scalar_tensor_tensor(out=uA[:, :], in0=tA[:, :], scalar=-1.0, in1=tA[:, :], op0=ALU.mult,
                                               op1=ALU.max), reads=["tA"], writes=["uA"])
        P.act(lambda e: e.activation(out=sA[:, :], in_=tA[:, :], func=AF.Sin, scale=TWO_PI), reads=["tA"], writes=["sA"])
        P.act(lambda e: e.activation(out=cA[:, :], in_=uA[:, :], func=AF.Sin, scale=-TWO_PI, bias=magp[:, 2:3]),
              reads=["uA"], writes=["cA"])
        P.dve(lambda e: e.reciprocal(out=uA[:, :], in_=rho[:, :]), reads=["cA"], writes=["uA"])
        P.dve(lambda e: e.tensor_tensor(out=cA[:, :], in0=cA[:, :], in1=uA[:, :], op=ALU.mult), reads=["cA", "uA"],
              writes=["cA"])
        P.dve(lambda e: e.tensor_tensor(out=sA[:, :], in0=sA[:, :], in1=uA[:, :], op=ALU.mult), reads=["sA", "uA"],
              writes=["sA"])
        P.dve(lambda e: e.tensor_tensor(out=rho2, in0=rho[:, :], in1=rho[:, :], op=ALU.mult), reads=[], writes=["rho2"])
        Cre = sb(SCR, [128, 16, 128], F32)
        Cim = sb(SCR + 8192, [128, 16, 128], F32)
        U1 = sb(SCR + 16384, [128, 16, 128], F32)
        U2 = sb(SCR + 24576, [128, 16, 128], F32)
        for hp in range(2):
            psl = slice(16 * hp, 16 * hp + 16)
            P.dma("sync", Cre[:, :, :], cexp[:, 0, hp * 2048:(hp + 1) * 2048].rearrange("p (a b) -> p a b", a=16),
                  writes=["Cre"])
            P.dma("sync", Cim[:, :, :], cexp[:, 1, hp * 2048:(hp + 1) * 2048].rearrange("p (a b) -> p a b", a=16),
                  writes=["Cim"])
            P.act(lambda e, psl=psl: e.activation(out=CTr[:, psl, :], in_=Cre[:, :, :], func=AF.Copy), reads=["Cre"],
                  writes=["CTr"])
            P.act(lambda e, psl=psl: e.activation(out=CTi[:, psl, :], in_=Cim[:, :, :], func=AF.Copy, scale=-1.0),
                  reads=["Cim"], writes=["CTi"])
            cAb = cap(cA, 16 * hp, [[32, 128], [1, 16], [0, 128]])
            sAb = cap(sA, 16 * hp, [[32, 128], [1, 16], [0, 128]])
            P.dve(lambda e, cAb=cAb: e.tensor_tensor(out=U1[:, :, :], in0=Cre[:, :, :], in1=cAb, op=ALU.mult),
                  reads=["Cre", "cA"], writes=["U1"])
            P.dve(lambda e, sAb=sAb: e.tensor_tensor(out=U2[:, :, :], in0=Cim[:, :, :], in1=sAb, op=ALU.mult),
                  reads=["Cim", "sA"], writes=["U2"])
            P.dve(lambda e, psl=psl: e.tensor_tensor(out=CIr[:, psl, :], in0=U1[:, :, :], in1=U2[:, :, :], op=ALU.add),
                  reads=["U1", "U2"], writes=["CIr"])
            P.dve(lambda e, sAb=sAb: e.tensor_tensor(out=U1[:, :, :], in0=Cre[:, :, :], in1=sAb, op=ALU.mult),
                  reads=["Cre", "sA", "CIr"], writes=["U1"])
            P.dve(lambda e, cAb=cAb: e.tensor_tensor(out=U2[:, :, :], in0=Cim[:, :, :], in1=cAb, op=ALU.mult),
                  reads=["Cim", "cA", "CIr"], writes=["U2"])
            P.dve(lambda e, psl=psl: e.tensor_tensor(out=CIi[:, psl, :], in0=U1[:, :, :], in1=U2[:, :, :], op=ALU.subtract),
                  reads=["U1", "U2"], writes=["CIi"])
        Xs = [sb(SCR + 32768 + 2048 * i, [128, 8, 128], BF16) for i in range(2)]
        for blk in range(8):
            xb = Xs[blk % 2]
            tbk = 2 * (blk % 2)

            def trx(e, blk=blk, tbk=tbk):
                last = None
                for a in range(4):
                    for ri, Bt in enumerate((BbTr, BbTi)):
                        k = 2 * a + ri
                        last = e.transpose(out=psb[:, tbk, k * 128:(k + 1) * 128], in_=Bt[:, blk, a, :], identity=identb[:, :])
                return last
            P.pe(trx, reads=["BbTr", "BbTi"], writes=[("ps", tbk)])
            P.act(lambda e, xb=xb, tbk=tbk: e.activation(out=xb[:, :, :], in_=psb[:, tbk, :].rearrange("p (a b) -> p a b", a=8),
                                                         func=AF.Copy), reads=[("ps", tbk)], writes=[("Xs", blk % 2)])

            def mk1(e, blk=blk, xb=xb, tbk=tbk):
                last = None
                for a in range(4):
                    pp = 4 * blk + a
                    for ri, Ct in enumerate((CIr, CIi)):
                        k = 2 * a + ri
                        last = e.matmul(ps[:, tbk + 1, 0:128], lhsT=xb[:, k, :], rhs=Ct[:, pp, :], start=(k == 0), stop=(k == 7))
                return last
            P.pe(mk1, reads=[("Xs", blk % 2), "CIr", "CIi"], writes=[("ps", tbk + 1)])
            P.act(lambda e, blk=blk, tbk=tbk: e.activation(out=K1blk[:, blk, :], in_=ps[:, tbk + 1, 0:128], func=AF.Copy,
                                                           scale=-1.0), reads=[("ps", tbk + 1)], writes=["K1blk"])
        P.barrier()
        if STOP == 6:
            P.emit()
            return nc

        NCH = 512

        def sl2(off, n, dt):
            return [sb(SCR + off + n * i, [128, NCH], dt) for i in range(2)]
        yqs = sl2(0, 2048, F32)
        kfqs = sl2(4096, 2048, F32)
        SINfs = sl2(8192, 2048, F32)
        COSfs = sl2(12288, 2048, F32)
        tb16 = [[sb(SCR + 16384 + 1024 * (3 * s_ + k), [128, NCH], BF16) for k in range(3)] for s_ in range(2)]
        pbuf = [[sb(SCR + 22528 + 1024 * (4 * s_ + k), [128, NCH], BF16) for k in range(4)] for s_ in range(2)]
        Rre = sb(SCR + 30720, [128, NCH], BF16)
        Rim = sb(SCR + 31744, [128, NCH], BF16)
        qbufs = [[sb(SCR + 32768 + 1024 * (4 * s_ + k), [128, NCH], BF16) for k in range(4)] for s_ in range(2)]
        ysb = sb(49152, [128, 1024], F32)
        gtmp = sb(53248, [128, 1024], F32)
        gsig = [sb(57344 + 2048 * i, [128, 1024], BF16) for i in range(2)]

        iters = [(blk, half, a) for blk in range(8) for half in range(2) for a in range(4)]

        def eo(ap_, which):
            return ap_.rearrange("p (c t) -> p c t", t=2)[:, :, which]

        def stageA0(idx):
            blk, half, a = iters[idx]
            s_ = idx % 2
            pp = 4 * blk + a
            yq, kfq = yqs[s_], kfqs[s_]
            ky, kk = ("yq", s_), ("kfq", s_)
            tokv = eo(tok[:, half * 1024:(half + 1) * 1024], 1)
            P.act(lambda e: e.activation(out=yq[:, :], in_=tokv, func=AF.Copy, scale=thp[:, pp:pp + 1]),
                  reads=["tok", "thp"], writes=[ky])
            P.act(lambda e: e.activation(out=kfq[:, :], in_=yq[:, :], func=AF.Identity, bias=magp[:, 0:1], scale=1.0),
                  reads=[ky, "magp"], writes=[kk])
            P.act(lambda e: e.activation(out=kfq[:, :], in_=kfq[:, :], func=AF.Identity, bias=magp[:, 1:2], scale=1.0),
                  reads=[kk, "magp"], writes=[kk])
            P.add("gpsimd", lambda e: e.tensor_tensor(out=yq[:, :], in0=yq[:, :], in1=kfq[:, :], op=ALU.subtract),
                  reads=[ky, kk], writes=[ky])

        def stageA(idx):
            blk, half, a = iters[idx]
            s_ = idx % 2
            pp = 4 * blk + a
            ukey = ("uT", blk, half)
            SINb, NSINb, COSb = tb16[s_]
            pb = pbuf[s_]
            yq, kfq, SINf, COSf = yqs[s_], kfqs[s_], SINfs[s_], COSfs[s_]
            ky, kk, ksf, kcf = ("yq", s_), ("kfq", s_), ("SINf", s_), ("COSf", s_)
            b0 = 2 * s_
            ue = eo(uT[:, blk, half * 1024:(half + 1) * 1024], 0)
            uo = eo(uT[:, blk, half * 1024:(half + 1) * 1024], 1)

            def bu(e):
                last = None
                for ri, (Lt, Bt) in enumerate(((LBr, BbTr), (LBi, BbTi))):
                    e.matmul(ps[:, b0 + ri, :], lhsT=Lt[:, blk, a, :], rhs=ue, start=True, stop=False)
                    last = e.matmul(ps[:, b0 + ri, :], lhsT=Bt[:, blk, a, :], rhs=uo, start=False, stop=True)
                return last
            P.pe(bu, reads=[ukey], writes=[("ps", b0), ("ps", b0 + 1)])
            P.act(lambda e: e.activation(out=kfq[:, :], in_=yq[:, :], func=AF.Abs), reads=[ky], writes=[kk])
            P.act(lambda e: e.activation(out=SINf[:, :], in_=yq[:, :], func=AF.Sin, scale=TWO_PI), reads=[ky], writes=[ksf])
            P.act(lambda e: e.activation(out=COSf[:, :], in_=kfq[:, :], func=AF.Sin, scale=-TWO_PI, bias=magp[:, 2:3]),
                  reads=[kk, "magp"], writes=[kcf])
            P.act(lambda e: e.activation(out=SINb[:, :], in_=yq[:, :], func=AF.Sin, scale=TWO_PI), reads=[ky],
                  writes=[("SINb", s_)])
            P.act(lambda e: e.activation(out=NSINb[:, :], in_=yq[:, :], func=AF.Sin, scale=-TWO_PI), reads=[ky],
                  writes=[("NSINb", s_)])
            P.act(lambda e: e.activation(out=COSb[:, :], in_=kfq[:, :], func=AF.Sin, scale=-TWO_PI, bias=magp[:, 2:3]),
                  reads=[kk, "magp"], writes=[("COSb", s_)])
            bre = ps[:, b0, :]
            bim = ps[:, b0 + 1, :]
            P.dve(lambda e: e.tensor_tensor(out=pb[0][:, :], in0=bre, in1=COSf[:, :], op=ALU.mult),
                  reads=[("ps", b0), kcf], writes=[("p", s_, 0)])
            P.dve(lambda e: e.tensor_tensor(out=pb[1][:, :], in0=bim, in1=SINf[:, :], op=ALU.mult),
                  reads=[("ps", b0 + 1), ksf], writes=[("p", s_, 1)])
            P.dve(lambda e: e.tensor_tensor(out=pb[2][:, :], in0=bim, in1=COSf[:, :], op=ALU.mult),
                  reads=[("ps", b0 + 1), kcf], writes=[("p", s_, 2)])
            P.dve(lambda e: e.scalar_tensor_tensor(out=pb[3][:, :], in0=bre, scalar=-1.0, in1=SINf[:, :], op0=ALU.mult,
                                                   op1=ALU.mult), reads=[("ps", b0), ksf], writes=[("p", s_, 3)])

        def stageB(idx):
            blk, half, a = iters[idx]
            s_ = idx % 2
            pp = 4 * blk + a
            SINb, NSINb, COSb = tb16[s_]
            pb = pbuf[s_]
            qbuf = qbufs[s_]
            rb = cap(stat, 32 + pp, [[64, 128], [0, NCH]])
            i0 = rlast[:, pp, 0:1] if half == 1 else 0.0
            i1 = rlast[:, pp, 1:2] if half == 1 else 0.0

            def addE(k0, bank):
                def f(e):
                    e.matmul(ps[:, bank, :], lhsT=identb[:, :], rhs=pb[k0][:, :], start=True, stop=False)
                    return e.matmul(ps[:, bank, :], lhsT=identb[:, :], rhs=pb[k0 + 1][:, :], start=False, stop=True)
                return f
            P.pe(addE(0, 6), reads=[("p", s_, 0), ("p", s_, 1)], writes=[("ps", 6)])
            P.pe(addE(2, 7), reads=[("p", s_, 2), ("p", s_, 3)], writes=[("ps", 7)])
            P.dve(lambda e: e.tensor_tensor_scan(out=Rre[:, :], data0=rb, data1=ps[:, 6, :], initial=i0, op0=ALU.mult,
                                                 op1=ALU.add), reads=[("ps", 6), "rho2", ("rl", pp)], writes=["Rre"])
            P.dve(lambda e: e.tensor_tensor(out=qbuf[0][:, :], in0=Rre[:, :], in1=COSb[:, :], op=ALU.mult),
                  reads=["Rre", ("COSb", s_)], writes=[("q", s_, 0)])
            P.dve(lambda e: e.tensor_tensor(out=qbuf[3][:, :], in0=Rre[:, :], in1=SINb[:, :], op=ALU.mult),
                  reads=["Rre", ("SINb", s_)], writes=[("q", s_, 3)])
            P.dve(lambda e: e.tensor_tensor_scan(out=Rim[:, :], data0=rb, data1=ps[:, 7, :], initial=i1, op0=ALU.mult,
                                                 op1=ALU.add), reads=[("ps", 7), "rho2", ("rl", pp)], writes=["Rim"])
            P.dve(lambda e: e.tensor_tensor(out=qbuf[1][:, :], in0=Rim[:, :], in1=NSINb[:, :], op=ALU.mult),
                  reads=["Rim", ("NSINb", s_)], writes=[("q", s_, 1)])
            P.dve(lambda e: e.tensor_tensor(out=qbuf[2][:, :], in0=Rim[:, :], in1=COSb[:, :], op=ALU.mult),
                  reads=["Rim", ("COSb", s_)], writes=[("q", s_, 2)])
            if half == 0:
                P.dve(lambda e: e.tensor_copy(out=rlast[:, pp, 0:1], in_=Rre[:, NCH - 1:NCH]), reads=["Rre"],
                      writes=[("rl", pp)])
                P.dve(lambda e: e.tensor_copy(out=rlast[:, pp, 1:2], in_=Rim[:, NCH - 1:NCH]), reads=["Rim", ("rl", pp)],
                      writes=[("rl", pp)])

        def stageC(idx):
            blk, half, a = iters[idx]
            s_ = idx % 2
            pp = 4 * blk + a
            ukey = ("uT", blk, half)
            usl = uT[:, blk, half * 1024:(half + 1) * 1024]
            qbuf = qbufs[s_]

            def cp(e):
                if a == 0:
                    e.matmul(ps[:, 5, :], lhsT=K1blk[:, blk, :], rhs=eo(usl, 1), start=True, stop=False)
                e.matmul(ps[:, 4, :], lhsT=CTr[:, pp, :], rhs=qbuf[0][:, :], start=(a == 0), stop=False)
                e.matmul(ps[:, 4, :], lhsT=CTr[:, pp, :], rhs=qbuf[1][:, :], start=False, stop=False)
                e.matmul(ps[:, 4, :], lhsT=CTi[:, pp, :], rhs=qbuf[2][:, :], start=False, stop=False)
                e.matmul(ps[:, 4, :], lhsT=CTi[:, pp, :], rhs=qbuf[3][:, :], start=False, stop=(a == 3))
                e.matmul(ps[:, 5, :], lhsT=CIr[:, pp, :], rhs=qbuf[0][:, :], start=False, stop=False)
                e.matmul(ps[:, 5, :], lhsT=CIr[:, pp, :], rhs=qbuf[1][:, :], start=False, stop=False)
                e.matmul(ps[:, 5, :], lhsT=CIi[:, pp, :], rhs=qbuf[2][:, :], start=False, stop=False)
                return e.matmul(ps[:, 5, :], lhsT=CIi[:, pp, :], rhs=qbuf[3][:, :], start=False, stop=(a == 3))
            P.pe(cp, reads=[("q", s_, k) for k in range(4)] + [ukey], writes=[("ps", 4), ("ps", 5)])
            if a != 3:
                return
            for which, bank in ((1, 4), (0, 5)):
                P.dve(lambda e, which=which, bank=bank: e.scalar_tensor_tensor(
                    out=eo(usl, which), in0=eo(usl, which), scalar=sm[:, C_DSK + blk:C_DSK + blk + 1], in1=ps[:, bank, :],
                    op0=ALU.mult, op1=ALU.add), reads=[ukey, ("ps", bank)], writes=[ukey])

        stageA0(0)
        stageA0(1)
        stageA(0)
        for idx in range(len(iters)):
            if idx + 2 < len(iters):
                stageA0(idx + 2)
            if idx + 1 < len(iters):
                stageA(idx + 1)
            stageB(idx)
            if idx >= 1:
                stageC(idx - 1)
        stageC(len(iters) - 1)
        gi = 0
        for blk in range(8):
            for half in range(2):
                tsl = slice(half * 1024, (half + 1) * 1024)
                ukey = ("uT", blk, half)
                g_ = (ysb, gtmp)[gi % 2]
                gs = gsig[gi % 2]
                gk, gsk = ("gel", gi % 2), ("gsig", gi % 2)
                gi += 1
                P.dve(lambda e, blk=blk, tsl=tsl, g_=g_: e.tensor_tensor(out=g_[:, :], in0=uT[:, blk, tsl], in1=uT[:, blk, tsl],
                                                                         op=ALU.mult), reads=[ukey], writes=[gk])
                P.dve(lambda e, g_=g_: e.tensor_scalar(out=g_[:, :], in0=g_[:, :], scalar1=0.044715, scalar2=1.0, op0=ALU.mult,
                                                       op1=ALU.add), reads=[gk], writes=[gk])
                P.dve(lambda e, blk=blk, tsl=tsl, g_=g_: e.tensor_tensor(out=g_[:, :], in0=g_[:, :], in1=uT[:, blk, tsl],
                                                                         op=ALU.mult), reads=[gk, ukey], writes=[gk])
                P.act(lambda e, g_=g_, gs=gs: e.activation(out=gs[:, :], in_=g_[:, :], func=AF.Sigmoid, scale=GELU_C),
                      reads=[gk], writes=[gsk])
                P.dve(lambda e, blk=blk, tsl=tsl, gs=gs: e.tensor_tensor(out=uT[:, blk, tsl], in0=gs[:, :], in1=uT[:, blk, tsl],
                                                                         op=ALU.mult), reads=[gsk, ukey], writes=[ukey])
        P.barrier()
        if STOP == 7:
            P.emit()
            return nc

        wg = sb(SCR + 24576, [128, 8, 1024], BF16)
        sg = [sb(SCR + 2048 * i, [128, 512], F32) for i in range(2)]
        ssm = [sb(SCR + 4096 + 2048 * i, [128, 512], F32) for i in range(2)]
        sqb = [sb(SCR + 8192 + 1024 * i, [128, 512], BF16) for i in range(2)]
        rbc = sb(SCR + 12288, [128, 2048], F32)
        ones128 = sb(SCR + 20480, [128, 128], BF16)
        Wo = sb(R0, [128, 16, 2048], BF16)
        w_o_v = w_o.rearrange("(c p) n -> p c n", p=128)
        P.dma("gpsimd", wg[:, :, :], w_glu.rearrange("(c p) n -> p c n", p=128), writes=["wg"])
        for q4 in range(4):
            P.dma("gpsimd", Wo[:, 4 * q4:4 * q4 + 4, :], w_o_v[:, 4 * q4:4 * q4 + 4, :], writes=[("Wo", q4)])
        P.dve(lambda e: e.memset(ones128[:, :], 1.0), writes=["ones128"])
        glu_it = [(e8, tb) for e8 in range(8) for tb in range(4)]

        def glu_mm(i):
            e8, tb = glu_it[i]
            bk = i % 4

            def mg(e):
                last = None
                for c in range(8):
                    last = e.matmul(ps[:, bk, :], lhsT=wg[:, c, e8 * 128:(e8 + 1) * 128],
                                    rhs=uT[:, c, tb * 512:(tb + 1) * 512], start=(c == 0), stop=(c == 7))
                return last
            P.pe(mg, reads=["wg"], writes=[("ps", bk)])

        def glu_ew(i):
            e8, tb = glu_it[i]
            bk = i % 4
            b2 = i % 2
            P.act(lambda e: e.activation(out=sg[b2][:, :], in_=ps[:, bk, :], func=AF.Sigmoid,
                                         bias=sm[:, C_BGLU + e8:C_BGLU + e8 + 1], scale=1.0),
                  reads=[("ps", bk)], writes=[("sg", b2)])
            P.dve(lambda e: e.tensor_tensor(out=ssm[b2][:, :], in0=uT[:, e8, tb * 512:(tb + 1) * 512], in1=sg[b2][:, :],
                                            op=ALU.mult), reads=[("sg", b2)], writes=[("ssm", b2)])
            P.dve(lambda e: e.tensor_tensor(out=sqb[b2][:, :], in0=ssm[b2][:, :], in1=ssm[b2][:, :], op=ALU.mult),
                  reads=[("ssm", b2)], writes=[("sqb", b2)])
            P.dve(lambda e: e.tensor_scalar(out=actT[:, 8 + e8, tb * 512:(tb + 1) * 512], in0=ssm[b2][:, :],
                                            scalar1=sm[:, C_GSSM + e8:C_GSSM + e8 + 1], scalar2=None, op0=ALU.mult),
                  reads=[("ssm", b2)], writes=[("mixS", e8)])

        def glu_sq(i):
            e8, tb = glu_it[i]
            b2 = i % 2
            P.pe(lambda e: e.matmul(ps[:, 4 + tb, :], lhsT=ones128[:, :], rhs=sqb[b2][:, :], start=(e8 == 0), stop=(e8 == 7)),
                 reads=[("sqb", b2), "ones128"], writes=[("ps", 4 + tb)])

        glu_mm(0)
        for i in range(len(glu_it)):
            glu_ew(i)
            if i + 1 < len(glu_it):
                glu_mm(i + 1)
            glu_sq(i)
        P.dve(lambda e: e.tensor_scalar(out=rbc[:, :].rearrange("p (a b) -> p a b", a=4), in0=ps[:, 4:8, :], scalar1=1.0 / 1024,
                                        scalar2=EPS, op0=ALU.mult, op1=ALU.add), reads=[("ps", 4 + t) for t in range(4)],
              writes=["rbc"])
        P.act(lambda e: e.activation(out=rbc[:, :], in_=rbc[:, :], func=AF.Sqrt), reads=["rbc"], writes=["rbc"])
        P.dve(lambda e: e.reciprocal(out=rbc[:, :], in_=rbc[:, :]), reads=["rbc"], writes=["rbc"])
        for e8 in range(8):
            P.dve(lambda e, e8=e8: e.tensor_tensor(out=actT[:, 8 + e8, :], in0=actT[:, 8 + e8, :], in1=rbc[:, :], op=ALU.mult),
                  reads=["rbc", ("mixS", e8)], writes=[("mixS", e8)])
        P.barrier()
        if STOP == 8:
            P.emit()
            return nc

        gpm = sb(65536, [128, 2048], F32)
        gpf = sb(65536 + 8192, [128, 2048], F32)
        xt2 = [sb(65536 + 16384 + 8192 * i, [128, 2048], F32) for i in range(2)]
        Abuf = [sb(SCR + 8192 * i, [128, 2048], F32) for i in range(2)]
        hnb = [sb(SCR + 16384 + 4096 * i, [128, 2048], BF16) for i in range(2)]
        ojb = sb(SCR + 24576, [128, 2048], BF16)
        P.dma("sync", gpm[:, :], gvecs[1:2, :].partition_broadcast(128), writes=["gpm"])
        P.dma("sync", gpf[:, :], gvecs[2:3, :].partition_broadcast(128), writes=["gpf"])

        def v4(ap_):
            return ap_.rearrange("p (a b) -> p a b", a=4)

        def wo_mm(tt):
            tsl = slice(tt * 128, (tt + 1) * 128)
            bset = 4 * (tt % 2)

            def mo(e):
                last = None
                for cbk in range(4):
                    for c in range(16):
                        last = e.matmul(ps[:, bset + cbk, :], lhsT=actT[:, c, tsl], rhs=Wo[:, c, cbk * 512:(cbk + 1) * 512],
                                        start=(c == 0), stop=(c == 15))
                return last
            P.pe(mo, reads=[("act", tt)], writes=[("ps", bset + i) for i in range(4)])

        def wo_post(tt):
            tsl = slice(tt * 128, (tt + 1) * 128)
            s_ = tt % 2
            bset = 4 * s_
            c0 = 16 + 8 * s_
            A = Abuf[s_]
            pk = [("ps", bset + i) for i in range(4)]
            acc = ps[:, bset:bset + 4, :]
            P.dma("sync", xt2[s_][:, :], x[tsl, :], writes=[("xt2", s_)])
            P.act(lambda e: e.activation(out=v4(A[:, :]), in_=acc, func=AF.Copy), reads=pk, writes=[("A", s_)])
            P.dve(lambda e: e.scalar_tensor_tensor(out=ojb[:, :], in0=A[:, :], scalar=1.0, in1=A[:, :], op0=ALU.mult,
                                                   op1=ALU.mult, accum_out=stat[:, c0:c0 + 1]),
                  reads=[("A", s_)], writes=["oj", ("st", c0)])
            rstd_from(("st", c0), stat[:, c0:c0 + 1], stat[:, c0 + 2:c0 + 3], ("st", c0 + 2), D, stat[:, c0 + 1:c0 + 2],
                      ("st", c0 + 1))
            P.dve(lambda e: e.scalar_tensor_tensor(out=A[:, :], in0=A[:, :], scalar=stat[:, c0 + 2:c0 + 3], in1=gpm[:, :],
                                                   op0=ALU.mult, op1=ALU.mult), reads=[("A", s_), ("st", c0 + 2), "gpm"],
                  writes=[("A", s_)])
            P.dve(lambda e: e.tensor_tensor(out=A[:, :], in0=A[:, :], in1=xt2[s_][:, :], op=ALU.add),
                  reads=[("A", s_), ("xt2", s_)], writes=[("A", s_)])
            P.dma("sync", hscr[tsl, :], A[:, :], reads=[("A", s_)], writes=[("hscr", tt)])
            P.dve(lambda e: e.scalar_tensor_tensor(out=ojb[:, :], in0=A[:, :], scalar=1.0, in1=A[:, :], op0=ALU.mult,
                                                   op1=ALU.mult, accum_out=stat[:, c0 + 3:c0 + 4]),
                  reads=[("A", s_)], writes=["oj", ("st", c0 + 3)])
            rstd_from(("st", c0 + 3), stat[:, c0 + 3:c0 + 4], stat[:, c0 + 5:c0 + 6], ("st", c0 + 5), D,
                      stat[:, c0 + 4:c0 + 5], ("st", c0 + 4))
            P.dve(lambda e: e.scalar_tensor_tensor(out=hnb[s_][:, :], in0=A[:, :], scalar=stat[:, c0 + 5:c0 + 6], in1=gpf[:, :],
                                                   op0=ALU.mult, op1=ALU.mult), reads=[("A", s_), ("st", c0 + 5), "gpf"],
                  writes=[("hnb", s_)])

            def trh(e):
                last = None
                for c in range(16):
                    last = e.transpose(out=psb[:, bset + c // 8, (c % 8) * 128:(c % 8) * 128 + 128],
                                       in_=hnb[s_][:, c * 128:(c + 1) * 128], identity=identb[:, :])
                return last
            P.pe(trh, reads=[("hnb", s_)], writes=[("ps", bset), ("ps", bset + 1)])
            P.act(lambda e: e.activation(out=actT[:, 0:8, tsl], in_=psb[:, bset, :].rearrange("p (a b) -> p a b", a=8),
                                         func=AF.Copy), reads=[("ps", bset)], writes=[("act", tt)])
            P.dve(lambda e: e.tensor_copy(out=actT[:, 8:16, tsl], in_=psb[:, bset + 1, :].rearrange("p (a b) -> p a b", a=8)),
                  reads=[("ps", bset + 1), ("act", tt)], writes=[("act", tt)])

        wo_mm(0)
        for tt in range(NT):
            if tt + 1 < NT:
                wo_mm(tt + 1)
            wo_post(tt)
        P.barrier()
        if STOP == 9:
            P.emit()
            return nc

        hidT = sb(65536, [128, NFC, 512], BF16)
        ff = sb(110592, [128, 4, 2048], F32)
        wpool = [sb(143360 + 4096 * i, [128, 4, 512], BF16) for i in range(8)]
        ht = sb(176128, [128, 2048], F32)
        gpo = sb(184320, [128, 2048], F32)
        sgf = [sb(192512 + 2048 * i, [128, 512], F32) for i in range(2)]
        fj = sb(196608, [128, 2048], BF16)
        P.dma("sync", gpo[:, :], gvecs[3:4, :].partition_broadcast(128), writes=["gpo"])
        wg_v = w_gate.rearrange("(c p) n -> p c n", p=128)
        wu_v = w_up.rearrange("(c p) n -> p c n", p=128)
        wd_v = w_down.rearrange("(f p) n -> p f n", p=128)
        nld = 0
        for tb in range(4):
            tsl = slice(tb * 512, (tb + 1) * 512)
            for blk in range(11):
                for cq in range(4):
                    gb = wpool[nld % 8]
                    gk = ("wp", nld % 8)
                    nld += 1
                    ub = wpool[nld % 8]
                    uk = ("wp", nld % 8)
                    nld += 1
                    P.dma("gpsimd", gb[:, :, :], wg_v[:, 4 * cq:4 * cq + 4, blk * 512:(blk + 1) * 512], writes=[gk])
                    P.dma("gpsimd", ub[:, :, :], wu_v[:, 4 * cq:4 * cq + 4, blk * 512:(blk + 1) * 512], writes=[uk])
                    for fcl in range(4):
                        def mgu(e, gb=gb, ub=ub, fcl=fcl, cq=cq, tsl=tsl):
                            last = None
                            for c4 in range(4):
                                c = 4 * cq + c4
                                e.matmul(ps[:, fcl, :], lhsT=gb[:, c4, fcl * 128:(fcl + 1) * 128], rhs=actT[:, c, tsl],
                                         start=(c == 0), stop=(c == 15))
                                last = e.matmul(ps[:, 4 + fcl, :], lhsT=ub[:, c4, fcl * 128:(fcl + 1) * 128],
                                                rhs=actT[:, c, tsl], start=(c == 0), stop=(c == 15))
                            return last
                        P.pe(mgu, reads=[gk, uk], writes=[("ps", fcl), ("ps", 4 + fcl)])
                        if cq == 3:
                            fc = 4 * blk + fcl
                            b2 = fc % 2
                            P.act(lambda e, fcl=fcl, b2=b2: e.activation(out=sgf[b2][:, :], in_=ps[:, fcl, :], func=AF.Silu),
                                  reads=[("ps", fcl)], writes=[("sgf", b2)])
                            P.dve(lambda e, fcl=fcl, b2=b2, fc=fc: e.tensor_tensor(out=hidT[:, fc, :], in0=sgf[b2][:, :],
                                                                                   in1=ps[:, 4 + fcl, :], op=ALU.mult),
                                  reads=[("sgf", b2), ("ps", 4 + fcl)], writes=[("hid", fc)])
            for db in range(4):
                bs = 4 * (db % 2)
                for fq in range(11):
                    wdb = wpool[nld % 8]
                    wk = ("wp", nld % 8)
                    nld += 1
                    P.dma("gpsimd", wdb[:, :, :], wd_v[:, 4 * fq:4 * fq + 4, db * 512:(db + 1) * 512], writes=[wk])

                    def md(e, wdb=wdb, fq=fq, bs=bs):
                        last = None
                        for f4 in range(4):
                            fc = fq * 4 + f4
                            for t4 in range(4):
                                last = e.matmul(ps[:, bs + t4, :], lhsT=hidT[:, fc, t4 * 128:(t4 + 1) * 128], rhs=wdb[:, f4, :],
                                                start=(fc == 0), stop=(fc == NFC - 1))
                        return last
                    P.pe(md, reads=[wk] + [("hid", fq * 4 + f4) for f4 in range(4)], writes=[("ps", bs + t4) for t4 in range(4)])
                for t4 in range(4):
                    if t4 % 2 == 0:
                        P.act(lambda e, t4=t4, db=db, bs=bs: e.activation(out=ff[:, t4, db * 512:(db + 1) * 512],
                                                                          in_=ps[:, bs + t4, :], func=AF.Copy),
                              reads=[("ps", bs + t4)], writes=[("ff", t4, db)])
                    else:
                        P.dve(lambda e, t4=t4, db=db, bs=bs: e.tensor_copy(out=ff[:, t4, db * 512:(db + 1) * 512],
                                                                           in_=ps[:, bs + t4, :]),
                              reads=[("ps", bs + t4)], writes=[("ff", t4, db)])
            for t4 in range(4):
                tt = tb * 4 + t4
                fk = [("ff", t4, db) for db in range(4)]
                P.dma("sync", ht[:, :], hscr[tt * 128:(tt + 1) * 128, :], reads=[("hscr", tt)], writes=["ht"])
                P.dve(lambda e, t4=t4: e.scalar_tensor_tensor(out=fj[:, :], in0=ff[:, t4, :], scalar=1.0, in1=ff[:, t4, :],
                                                              op0=ALU.mult, op1=ALU.mult, accum_out=stat[:, 14:15]),
                      reads=fk, writes=["fj", "st14"])
                rstd_from("st14", stat[:, 14:15], stat[:, 3:4], "st3", D, stat[:, 15:16], "st15")
                P.dve(lambda e, t4=t4: e.scalar_tensor_tensor(out=ff[:, t4, :], in0=ff[:, t4, :], scalar=stat[:, 3:4],
                                                              in1=gpo[:, :], op0=ALU.mult, op1=ALU.mult),
                      reads=fk + ["st3", "gpo"], writes=fk)
                P.dve(lambda e, t4=t4: e.tensor_tensor(out=ff[:, t4, :], in0=ff[:, t4, :], in1=ht[:, :], op=ALU.add),
                      reads=fk + ["ht"], writes=fk)
                P.dma("sync", out[tt * 128:(tt + 1) * 128, :], ff[:, t4, :], reads=fk, writes=[("out", tt)])
        P.emit()
        print('sig counts', P.sig_counts, 'dma cum', max(P.dma_cum))
    return nc


def _host_layouts(inp):
    f32 = np.float32
    G, N, Pp = 64, 64, 16
    sm = np.zeros((128, NSM), f32)
    sm[:, C_ID:C_ID + 128] = np.eye(128, dtype=f32)
    kk = np.arange(128)[:, None]
    qq = np.arange(128)[None, :]
    sm[:, C_MASK:C_MASK + 128] = (kk > qq).astype(f32)
    sm[:, C_MASK + 128:C_MASK + 256] = (kk <= qq).astype(f32)
    half = 32
    inv_freq = (np.float32(10000.0) ** (-np.arange(half, dtype=f32) / np.float32(half))).astype(f32)
    sm[:, C_INVF:C_INVF + 32] = inv_freq[None, :]
    sm[:, C_SINK:C_SINK + 16] = inp["sinks"][0][None, :]
    sm[:, C_GSSM:C_GSSM + 8] = inp["g_ssm_out"][0].reshape(8, 128).T
    sm[:, C_BGLU:C_BGLU + 8] = inp["b_glu"][0].reshape(8, 128).T
    sm[:, C_DSK:C_DSK + 8] = inp["d_skip"][0].reshape(8, 8, 16).reshape(8, 128).T
    a_re, a_im, ldt = inp["a_re"][0], inp["a_im"][0], inp["log_dt"][0]
    for b in range(2):
        sm[64 * b:64 * b + 64, C_ARA:C_ARA + 32] = a_re[b::2, :].T
        sm[64 * b:64 * b + 64, C_AIA:C_AIA + 32] = a_im[b::2, :].T
        sm[64 * b:64 * b + 64, C_LDA:C_LDA + 32] = np.broadcast_to(ldt[b::2][None, :], (64, 32))
    lb3 = np.zeros((128, 3, 8, 2, 64), f32)
    for gq in range(8):
        rows = slice(16 * gq, 16 * gq + 16)
        for blk in range(8):
            g = 8 * blk + gq
            lb3[rows, 0, blk, :, :] = a_re[g][None, None, :]
            lb3[rows, 1, blk, :, :] = a_im[g][None, None, :]
            lb3[rows, 2, blk, :, :] = ldt[g]
    lb3 = lb3.reshape(128, 3, 1024)
    bexp = np.zeros((128, 2, 8, 2, 64), f32)
    cexp = np.zeros((128, 2, 32, 8, 16), f32)
    maska = np.zeros((128, 4), f32)
    b_re, b_im, c_re, c_im = inp["b_re"][0], inp["b_im"][0], inp["c_re"][0], inp["c_im"][0]
    for g in range(G):
        blk, gq = divmod(g, 8)
        a, b = divmod(gq, 2)
        rows = slice(16 * gq, 16 * gq + 16)
        bexp[rows, 0, blk, b, :] = b_re[g].T
        bexp[rows, 1, blk, b, :] = b_im[g].T
        maska[rows, a] = 1.0
        pp = g // 2
        cexp[64 * b:64 * b + 64, 0, pp, gq, :] = c_re[g].T
        cexp[64 * b:64 * b + 64, 1, pp, gq, :] = c_im[g].T
    bexp = bexp.reshape(128, 2, 1024)
    cexp = cexp.reshape(128, 2, 4096)
    gv = np.zeros((5, D), f32)
    gv[0] = inp["g_pre_mix"][0]
    gv[1] = inp["g_post_mix"][0]
    gv[2] = inp["g_pre_ffn"][0]
    gv[3] = inp["g_post_ffn"][0]
    gv[4, :1024] = inp["g_attn_out"][0]
    tokc = np.broadcast_to(np.arange(1, 2049, dtype=f32)[None, :], (128, 2048)).copy()
    shared = {
        "smalls": sm, "gvecs": gv, "lb3": lb3, "bexp": bexp, "cexp": cexp, "tokc": tokc, "maska": maska,
        "w_in": np.ascontiguousarray(inp["w_in"][0]), "w_glu": np.ascontiguousarray(inp["w_glu"][0]),
        "w_o": np.ascontiguousarray(inp["w_o"][0]), "w_gate": np.ascontiguousarray(inp["w_gate"][0]),
        "w_up": np.ascontiguousarray(inp["w_up"][0]), "w_down": np.ascontiguousarray(inp["w_down"][0]),
    }
    return shared


def kernel(**inputs):
    inp = {k: np.asarray(v) for k, v in inputs.items()}
    shared = _host_layouts(inp)
    nc = build_nc()
    in_maps = []
    for c in range(8):
        m = dict(shared)
        m["x"] = np.ascontiguousarray(inp["x"][c])
        m["pos"] = np.ascontiguousarray(inp["positions"][c].astype(np.int32).reshape(16, 128).T)
        in_maps.append(m)
    res = run_bass_kernel_spmd(nc, in_maps, core_ids=list(range(8)))
    return np.stack([np.asarray(r["out"], dtype=np.float32) for r in res.results], axis=0)
```

```python
import math
import numpy as np
from contextlib import ExitStack
import concourse.bass as bass
import concourse.mybir as mybir
from concourse.bass_utils import run_bass_kernel_spmd

F32 = mybir.dt.float32
BF16 = mybir.dt.bfloat16
I32 = mybir.dt.int32
AF = mybir.ActivationFunctionType
ALU = mybir.AluOpType
AX = mybir.AxisListType

ENGS = ["tensor", "vector", "scalar", "gpsimd", "sync"]


class Op:
    __slots__ = ("eng", "fn", "deps", "signal", "semval", "dma", "dsem", "dval", "prev_on_sem")

    def __init__(self, eng, fn, dma):
        self.eng = eng
        self.fn = fn
        self.deps = []
        self.signal = False
        self.semval = 0
        self.dma = dma
        self.dsem = None
        self.dval = 0
        self.prev_on_sem = None


class Prog:
    def __init__(self, nc, n_dma_sems=32):
        self.nc = nc
        self.ops = {e: [] for e in ENGS}
        self.res = {}
        self.n_dma_sems = n_dma_sems
        self.dma_rr = 0
        self.dma_last = [None] * n_dma_sems
        self.dma_cum = [0] * n_dma_sems

    def add(self, eng, fn, reads=(), writes=(), dma=False):
        op = Op(eng, fn, dma)
        deps = {}
        for k in reads:
            st = self.res.get(k)
            if st is not None and st[0] is not None:
                deps[id(st[0])] = st[0]
        for k in writes:
            st = self.res.get(k)
            if st is not None:
                if st[0] is not None:
                    deps[id(st[0])] = st[0]
                for r in st[1]:
                    deps[id(r)] = r
        for k in reads:
            st = self.res.get(k)
            if st is None:
                self.res[k] = [None, [op]]
            else:
                st[1].append(op)
        for k in writes:
            self.res[k] = [op, []]
        for d in deps.values():
            if d is op:
                continue
            if (not d.dma) and d.eng == eng and eng == "tensor":
                continue
            op.deps.append(d)
            d.signal = True
        if dma:
            s = self.dma_rr
            self.dma_rr = (self.dma_rr + 1) % self.n_dma_sems
            op.dsem = s
            self.dma_cum[s] += 16
            op.dval = self.dma_cum[s]
            op.prev_on_sem = self.dma_last[s]
            self.dma_last[s] = op
        self.ops[eng].append(op)
        return op

    def pe(self, fn, reads=(), writes=()):
        return self.add("tensor", fn, reads, writes)

    def dve(self, fn, reads=(), writes=()):
        return self.add("vector", fn, reads, writes)

    def act(self, fn, reads=(), writes=()):
        return self.add("scalar", fn, reads, writes)

    def pool(self, fn, reads=(), writes=()):
        return self.add("vector" if POOL_AS_DVE else "gpsimd", fn, reads, writes)

    def dma(self, eng, out, in_, reads=(), writes=(), **kw):
        return self.add(eng, lambda e: e.dma_start(out=out, in_=in_, **kw), reads, writes, dma=True)

    def barrier(self):
        lasts = []
        for e in ENGS:
            for op in reversed(self.ops[e]):
                if (not op.dma) and op.fn is not None:
                    lasts.append(op)
                    break
        dl = [d for d in self.dma_last if d is not None]
        for e in ENGS:
            op = Op(e, None, False)
            for d in lasts:
                if d.eng != e:
                    op.deps.append(d)
                    d.signal = True
            op.deps.extend(dl)
            self.ops[e].append(op)
        self.res = {}

    def emit(self):
        nc = self.nc
        self.barrier()
        for e in ENGS:
            cum = 0
            for op in self.ops[e]:
                if op.dma:
                    continue
                if op.signal:
                    cum += 1
                    op.semval = cum
            self.sig_counts = getattr(self, "sig_counts", {})
            self.sig_counts[e] = (cum, len(self.ops[e]))
        with ExitStack() as st:
            esem = {e: st.enter_context(nc.semaphore("es_" + e)) for e in ENGS}
            dsem = [st.enter_context(nc.semaphore("ds_%d" % i)) for i in range(self.n_dma_sems)]
            block = st.enter_context(nc.Block())

            def run(eng, e):
                waited = {}

                def wait_for(d):
                    if d.dma:
                        key, sem, val = ("d", d.dsem), dsem[d.dsem], d.dval
                    else:
                        key, sem, val = ("e", d.eng), esem[d.eng], d.semval
                    if waited.get(key, 0) < val:
                        eng.wait_ge(sem, val)
                        waited[key] = val

                for op in self.ops[e]:
                    for d in op.deps:
                        wait_for(d)
                    if op.dma and op.prev_on_sem is not None:
                        wait_for(op.prev_on_sem)
                    if op.fn is None:
                        continue
                    inst = op.fn(eng)
                    if op.dma:
                        inst.then_inc(dsem[op.dsem], 16)
                    elif op.signal:
                        inst.then_inc(esem[e], 1)

            for e in ENGS:
                getattr(block, e)(lambda eng, e=e: run(eng, e))


D = 2048
L = 2048
NT = 16
DFF = 5632
NFC = 44
EPS = 1e-6
BASE = 17408
STOP = -1
POOL_AS_DVE = True
ATT_LEVEL = 99
ATT_TILES = 16
INV2PI = 1.0 / (2.0 * math.pi)
TWO_PI = 2.0 * math.pi * (1.0 - 2e-7)
MAGIC = 12582912.0
GELU_C = 2.0 * math.sqrt(2.0 / math.pi)

C_ID = 0
C_MASK = 128
C_INVF = 384
C_SINK = 416
C_GSSM = 432
C_BGLU = 440
C_DSK = 448
C_ARA = 456
C_AIA = 488
C_LDA = 520
NSM = 552


def build_nc():
    nc = bass.Bass("TRN2", target_bir_lowering=False)

    def din(name, shape, dt=F32):
        return nc.dram_tensor(name, list(shape), dt, kind="ExternalInput").ap()

    x = din("x", [L, D])
    pos = din("pos", [128, NT], I32)
    smalls = din("smalls", [128, NSM])
    gvecs = din("gvecs", [5, D])
    w_in = din("w_in", [D, 2560])
    w_glu = din("w_glu", [1024, 1024])
    w_o = din("w_o", [D, D])
    w_gate = din("w_gate", [D, DFF])
    w_up = din("w_up", [D, DFF])
    w_down = din("w_down", [DFF, D])
    lb3 = din("lb3", [128, 3, 1024])
    bexp = din("bexp", [128, 2, 1024])
    maska = din("maska", [128, 4])
    cexp = din("cexp", [128, 2, 4096])
    tokc = din("tokc", [128, 2048])
    out = nc.dram_tensor("out", [L, D], F32, kind="ExternalOutput").ap()
    hscr = nc.dram_tensor("hscr", [L, D], F32, kind="Internal").ap()

    cnt = [0]

    def sb(off, shape, dt):
        cnt[0] += 1
        return nc.alloc_sbuf_tensor_at("t%d" % cnt[0], list(shape), dt, offset=BASE + off)

    def cap(t, off, dims):
        return bass.AP(tensor=t, offset=off, ap=[list(d) for d in dims])

    P = Prog(nc)
    with ExitStack() as st:
        ps = st.enter_context(nc.psum_tensor("ps", [128, 8, 512], F32))
        psb = ps[:, :, :].bitcast(BF16)

        actT = sb(0, [128, 16, 2048], BF16)
        uT = sb(65536, [128, 8, 2048], BF16)
        qT = sb(98304, [128, 8, 2048], BF16)
        kT2 = sb(131072, [128, 4, 2176], BF16)
        v1 = sb(148480, [128, 17, 4, 65], BF16)
        cosT = sb(157696, [128, 16, 32], F32)
        sinT = sb(157696 + 2048, [128, 16, 32], F32)
        nsinT = sb(157696 + 4096, [128, 16, 32], F32)
        SCR = 163840
        CONST = 207872
        sm = sb(CONST, [128, NSM], F32)
        identb = sb(CONST + 2208, [128, 128], BF16)
        maskb = sb(CONST + 2464, [128, 2, 128], BF16)
        stat = sb(CONST + 2976, [128, 64], F32)
        ngs = sb(CONST + 3232, [128, 4], F32)
        onesb = sb(CONST + 3264, [128, 2], BF16)
        rstd_s = sb(CONST + 3296, [128, 16], F32)
        posi = sb(CONST + 3360, [128, 16], I32)
        posf = sb(CONST + 3424, [128, 16], F32)
        thp = sb(CONST + 3488, [128, 32], F32)
        rho = sb(CONST + 3616, [128, 32], F32)
        rlast = sb(CONST + 3744, [128, 32, 2], F32)
        identf = sm[:, C_ID:C_ID + 128]
        magp = sb(CONST + 4000, [128, 4], F32)
        maskA = sb(CONST + 4032, [128, 4], F32)

        P.dma("sync", sm[:, :], smalls, writes=["sm"])
        P.dma("sync", posi[:, :], pos, writes=["posi"])
        P.dma("sync", maskA[:, :], maska, writes=["maskA"])
        P.dve(lambda e: e.tensor_copy(out=identb[:, :], in_=sm[:, C_ID:C_ID + 128]), reads=["sm"], writes=["identb"])
        P.dve(lambda e: e.tensor_copy(out=maskb[:, :, :], in_=sm[:, C_MASK:C_MASK + 256].rearrange("p (a b) -> p a b", a=2)),
              reads=["sm"], writes=["maskb"])
        P.dve(lambda e: e.memset(onesb[:, :], 1.0), writes=["onesb"])
        P.dve(lambda e: e.memset(magp[:, 0:1], MAGIC), writes=["magp0"])
        P.dve(lambda e: e.memset(magp[:, 1:2], -MAGIC), writes=["magp1"])
        P.dve(lambda e: e.memset(magp[:, 2:3], math.pi / 2), writes=["magp2"])
        P.dve(lambda e: e.tensor_reduce(out=ngs[:, :], in_=sm[:, C_SINK:C_SINK + 16].rearrange("p (a b) -> p a b", a=4),
                                        axis=AX.X, op=ALU.max, negate=True), reads=["sm"], writes=["ngs"])
        P.dve(lambda e: e.memset(kT2[:, :, 0:128], 0.0), writes=["kpad"])
        P.dve(lambda e: e.memset(v1[:, 0, :, :], 0.0), writes=["vpad"])
        P.dve(lambda e: e.memset(v1[:, 1:17, :, 64:65], 1.0), writes=["vones"])
        P.dve(lambda e: e.tensor_copy(out=posf[:, :], in_=posi[:, :]), reads=["posi"], writes=["posf"])

        rt = [sb(65536 + 2048 * i, [128, 16, 32], F32) for i in range(4)]
        P.dve(lambda e: e.tensor_tensor(out=rt[0][:, :, :], in0=cap(posf, 0, [[16, 128], [1, 16], [0, 32]]),
                                        in1=cap(sm, C_INVF, [[NSM, 128], [0, 16], [1, 32]]), op=ALU.mult),
              reads=["posf", "sm"], writes=["rt0"])
        P.dve(lambda e: e.tensor_scalar(out=rt[0][:, :, :], in0=rt[0][:, :, :], scalar1=INV2PI, scalar2=None, op0=ALU.mult),
              reads=["rt0"], writes=["rt0"])
        P.dve(lambda e: e.tensor_scalar(out=rt[1][:, :, :], in0=rt[0][:, :, :], scalar1=MAGIC, scalar2=MAGIC, op0=ALU.add,
                                        op1=ALU.subtract), reads=["rt0"], writes=["rt1"])
        P.dve(lambda e: e.tensor_tensor(out=rt[2][:, :, :], in0=rt[0][:, :, :], in1=rt[1][:, :, :], op=ALU.subtract),
              reads=["rt0", "rt1"], writes=["rt2"])
        P.dve(lambda e: e.scalar_tensor_tensor(out=rt[3][:, :, :], in0=rt[2][:, :, :], scalar=-1.0, in1=rt[2][:, :, :],
                                               op0=ALU.mult, op1=ALU.max), reads=["rt2"], writes=["rt3"])
        P.act(lambda e: e.activation(out=sinT[:, :, :], in_=rt[2][:, :, :], func=AF.Sin, scale=TWO_PI), reads=["rt2"], writes=["sinT"])
        P.act(lambda e: e.activation(out=nsinT[:, :, :], in_=rt[2][:, :, :], func=AF.Sin, scale=-TWO_PI), reads=["rt2"], writes=["nsinT"])
        P.act(lambda e: e.activation(out=cosT[:, :, :], in_=rt[3][:, :, :], func=AF.Sin, scale=-TWO_PI, bias=math.pi / 2),
              reads=["rt3"], writes=["cosT"])
        if STOP == 0:
            P.emit()
            return nc

        def rstd_from(ss_key, ss_ap, dst_ap, dst_key, n, tmp_ap, tmp_key):
            P.dve(lambda e: e.tensor_scalar(out=tmp_ap, in0=ss_ap, scalar1=1.0 / n, scalar2=EPS, op0=ALU.mult, op1=ALU.add),
                  reads=[ss_key], writes=[tmp_key])
            P.act(lambda e: e.activation(out=tmp_ap, in_=tmp_ap, func=AF.Ln), reads=[tmp_key], writes=[tmp_key])
            P.act(lambda e: e.activation(out=dst_ap, in_=tmp_ap, func=AF.Exp, scale=-0.5), reads=[tmp_key], writes=[dst_key])

        xt = [sb(SCR + 8192 * i, [128, 2048], F32) for i in range(2)]
        xs = [sb(SCR + 16384 + 4096 * i, [128, 2048], BF16) for i in range(2)]
        gbc = sb(SCR + 24576, [128, 2048], F32)
        junk = sb(SCR + 32768, [128, 2048], BF16)
        P.dma("sync", gbc[:, :], gvecs[0:1, :].partition_broadcast(128), writes=["gbc"])

        def a1_pre(tt):
            b = tt % 2
            c0 = 4 * b
            P.dma("sync", xt[b][:, :], x[tt * 128:(tt + 1) * 128, :], writes=[("xt", b)])
            P.dve(lambda e: e.scalar_tensor_tensor(out=junk[:, :], in0=xt[b][:, :], scalar=1.0, in1=xt[b][:, :],
                                                   op0=ALU.mult, op1=ALU.mult, accum_out=stat[:, c0:c0 + 1]),
                  reads=[("xt", b)], writes=["junk", ("st", c0)])
            rstd_from(("st", c0), stat[:, c0:c0 + 1], stat[:, c0 + 2:c0 + 3], ("st", c0 + 2), D, stat[:, c0 + 1:c0 + 2],
                      ("st", c0 + 1))
            P.dve(lambda e: e.scalar_tensor_tensor(out=xs[b][:, :], in0=xt[b][:, :], scalar=stat[:, c0 + 2:c0 + 3],
                                                   in1=gbc[:, :], op0=ALU.mult, op1=ALU.mult),
                  reads=[("xt", b), ("st", c0 + 2), "gbc"], writes=[("xs", b)])

        def a1_post(tt):
            b = tt % 2
            bk = 2 * b

            def tr(e):
                last = None
                for c in range(16):
                    last = e.transpose(out=psb[:, bk + c // 8, (c % 8) * 128:(c % 8) * 128 + 128],
                                       in_=xs[b][:, c * 128:(c + 1) * 128], identity=identb[:, :])
                return last
            P.pe(tr, reads=[("xs", b), "identb"], writes=[("ps", bk), ("ps", bk + 1)])
            P.act(lambda e: e.activation(out=actT[:, 0:8, tt * 128:(tt + 1) * 128],
                                         in_=psb[:, bk, :].rearrange("p (a b) -> p a b", a=8), func=AF.Copy),
                  reads=[("ps", bk)], writes=[("actT", tt, 0)])
            P.act(lambda e: e.activation(out=actT[:, 8:16, tt * 128:(tt + 1) * 128],
                                         in_=psb[:, bk + 1, :].rearrange("p (a b) -> p a b", a=8), func=AF.Copy),
                  reads=[("ps", bk + 1)], writes=[("actT", tt, 1)])

        a1_pre(0)
        for tt in range(NT):
            if tt + 1 < NT:
                a1_pre(tt + 1)
            a1_post(tt)
        P.barrier()
        if STOP == 1:
            P.emit()
            return nc

        wb = [sb(SCR + 8192 * i, [128, 16, 256], BF16) for i in range(2)]
        rAs = [sb(SCR + 16384 + 1024 * i, [128, 256], F32) for i in range(2)]
        rBs = [sb(SCR + 18432 + 1024 * i, [128, 256], F32) for i in range(2)]
        qrs = [sb(SCR + 20480 + 512 * i, [128, 256], BF16) for i in range(2)]
        kds = [sb(SCR + 21504 + 1024 * i, [128, 4, 2, 64], BF16) for i in range(2)]
        w_in_v = w_in.rearrange("(c p) n -> p c n", p=128)
        jobs = []
        for cb in range(6):
            for tt in range(NT):
                jobs.append((cb, tt, len(jobs) % 4, len(jobs) % 2))
        loaded = set()

        def a2_load(cb):
            if cb in loaded or cb >= 10:
                return
            loaded.add(cb)
            P.dma("gpsimd", wb[cb % 2][:, :, :], w_in_v[:, :, cb * 256:(cb + 1) * 256], writes=[("wb", cb % 2)])

        def a2_M(job):
            cb, tt, bk, par = job
            a2_load(cb)
            wbuf = wb[cb % 2]

            def mm(e):
                last = None
                for c in range(16):
                    last = e.matmul(ps[:, bk, 0:256], lhsT=actT[:, c, tt * 128:(tt + 1) * 128], rhs=wbuf[:, c, :],
                                    start=(c == 0), stop=(c == 15))
                return last
            P.pe(mm, reads=[("wb", cb % 2), ("actT", tt, 0), ("actT", tt, 1)], writes=[("ps", bk)])

        def a2_post(job):
            cb, tt, bk, par = job
            if cb == 5:
                P.act(lambda e: e.activation(out=v1[:, tt + 1, :, 0:64], in_=ps[:, bk, 0:256].rearrange("p (a b) -> p a b", a=4),
                                             func=AF.Copy), reads=[("ps", bk)], writes=[("v1", tt)])
                return
            rA, rB, qr, kd = rAs[par], rBs[par], qrs[par], kds[par]
            pv = ps[:, bk, 0:256].rearrange("p (h t d) -> p h t d", h=4, t=2)
            rBv = rB[:, :].rearrange("p (h t d) -> p h t d", h=4, t=2)
            P.dve(lambda e: e.tensor_tensor(out=rA[:, :].rearrange("p (h t d) -> p h t d", h=4, t=2), in0=pv,
                                            in1=cap(cosT, tt * 32, [[512, 128], [0, 4], [0, 2], [1, 32]]), op=ALU.mult),
                  reads=[("ps", bk), "cosT"], writes=[("rA", par)])
            P.dve(lambda e: e.tensor_tensor(out=rBv[:, :, 0, :], in0=pv[:, :, 1, :],
                                            in1=cap(nsinT, tt * 32, [[512, 128], [0, 4], [1, 32]]), op=ALU.mult),
                  reads=[("ps", bk), "nsinT"], writes=[("rB0", par)])
            P.dve(lambda e: e.tensor_tensor(out=rBv[:, :, 1, :], in0=pv[:, :, 0, :],
                                            in1=cap(sinT, tt * 32, [[512, 128], [0, 4], [1, 32]]), op=ALU.mult),
                  reads=[("ps", bk), "sinT"], writes=[("rB1", par)])
            rk = [("rA", par), ("rB0", par), ("rB1", par)]
            if cb < 4:
                P.dve(lambda e: e.tensor_tensor(out=qr[:, :], in0=rA[:, :], in1=rB[:, :], op=ALU.add), reads=rk,
                      writes=[("qr", par)])
                tb2 = 4 + par

                def trq(e):
                    last = None
                    for j in range(2):
                        last = e.transpose(out=psb[:, tb2, j * 128:(j + 1) * 128], in_=qr[:, j * 128:(j + 1) * 128],
                                           identity=identb[:, :])
                    return last
                P.pe(trq, reads=[("qr", par)], writes=[("ps", tb2)])
                P.act(lambda e: e.activation(out=qT[:, 2 * cb:2 * cb + 2, tt * 128:(tt + 1) * 128],
                                             in_=psb[:, tb2, 0:256].rearrange("p (a b) -> p a b", a=2), func=AF.Copy),
                      reads=[("ps", tb2)], writes=[("qT", cb, tt)])
            else:
                for dup in range(2):
                    P.dve(lambda e, dup=dup: e.tensor_tensor(out=kd[:, :, dup, :], in0=rA[:, :].rearrange("p (h d) -> p h d", h=4),
                                                             in1=rB[:, :].rearrange("p (h d) -> p h d", h=4), op=ALU.add),
                          reads=rk, writes=[("kd", par, dup)])
                tb2 = 6 + par

                def trk(e):
                    last = None
                    for j in range(4):
                        last = e.transpose(out=psb[:, tb2, j * 128:(j + 1) * 128],
                                           in_=kd[:, j, :, :].rearrange("p a b -> p (a b)"), identity=identb[:, :])
                    return last
                P.pe(trk, reads=[("kd", par, 0), ("kd", par, 1)], writes=[("ps", tb2)])
                P.act(lambda e: e.activation(out=kT2[:, :, 128 + tt * 128:128 + (tt + 1) * 128],
                                             in_=psb[:, tb2, 0:512].rearrange("p (a b) -> p a b", a=4), func=AF.Copy),
                      reads=[("ps", tb2)], writes=[("kT2", tt)])

        a2_load(0)
        a2_load(1)
        a2_M(jobs[0])
        for ji in range(len(jobs)):
            if ji + 1 < len(jobs):
                a2_M(jobs[ji + 1])
            a2_post(jobs[ji])
            if jobs[ji][1] == NT - 1:
                a2_load(jobs[ji][0] + 2)
        pc = 0
        for cb in range(6, 10):
            a2_load(cb)
            wbuf = wb[cb % 2]
            for j in range(2):
                uc = (cb - 6) * 2 + j
                for tb in range(4):
                    bk = pc % 4
                    pc += 1

                    def mmu(e, j=j, tb=tb, bk=bk, wbuf=wbuf):
                        last = None
                        for c in range(16):
                            last = e.matmul(ps[:, bk, :], lhsT=wbuf[:, c, j * 128:(j + 1) * 128],
                                            rhs=actT[:, c, tb * 512:(tb + 1) * 512], start=(c == 0), stop=(c == 15))
                        return last
                    P.pe(mmu, reads=[("wb", cb % 2)], writes=[("ps", bk)])
                    if (tb % 2) == 0:
                        P.act(lambda e, uc=uc, tb=tb, bk=bk: e.activation(out=uT[:, uc, tb * 512:(tb + 1) * 512],
                                                                          in_=ps[:, bk, :], func=AF.Copy),
                              reads=[("ps", bk)], writes=[("uT", uc, tb // 2)])
                    else:
                        P.dve(lambda e, uc=uc, tb=tb, bk=bk: e.tensor_copy(out=uT[:, uc, tb * 512:(tb + 1) * 512],
                                                                           in_=ps[:, bk, :]),
                              reads=[("ps", bk)], writes=[("uT", uc, tb // 2)])
            a2_load(cb + 2)
        P.barrier()
        if STOP == 2:
            P.emit()
            return nc

        Pbs = [sb(SCR + 2048 * i, [128, 1024], BF16) for i in range(2)]
        PTs = [sb(SCR + 4096 + 2048 * i, [128, 4, 2, 128], BF16) for i in range(2)]
        attn = [sb(SCR + 8192 + 4096 * i, [128, 1024], F32) for i in range(2)]
        anbs = [sb(SCR + 16384 + 2048 * i, [128, 1024], BF16) for i in range(2)]
        gat = sb(SCR + 20480, [128, 1024], F32)
        ajunk = sb(SCR + 24576, [128, 1024], BF16)
        asts = [sb(SCR + 26624 + 64 * i, [128, 16], F32) for i in range(4)]
        P.dma("sync", gat[:, :], gvecs[4:5, 0:1024].partition_broadcast(128), writes=["gat"])
        aits = [(tt, j) for tt in range(min(NT, ATT_TILES)) for j in range(4)]

        def att_X(i):
            tt, j = aits[i]
            sbk = 2 * (i % 2)
            ast = asts[i % 4]
            Pb = Pbs[i % 2]
            ka = ("ast", i % 4)

            def qk(e):
                last = None
                for hh in range(4):
                    h = 4 * j + hh
                    pb = (h % 2) * 64
                    last = e.matmul(ps[:, sbk + hh % 2, (hh // 2) * 256:(hh // 2) * 256 + 256],
                                    lhsT=qT[pb:pb + 64, h // 2, tt * 128:(tt + 1) * 128],
                                    rhs=kT2[pb:pb + 64, j, tt * 128:tt * 128 + 256], start=True, stop=True)
                return last
            P.pe(qk, reads=[], writes=[("ps", sbk), ("ps", sbk + 1)])
            sc = ps[:, sbk:sbk + 2, :]
            P.dve(lambda e: e.tensor_reduce(out=ast[:, 0:1], in_=sc, axis=AX.XY, op=ALU.max, negate=True),
                  reads=[("ps", sbk), ("ps", sbk + 1)], writes=[ka])
            P.dve(lambda e: e.tensor_scalar(out=ast[:, 1:2], in0=ast[:, 0:1], scalar1=0.125, scalar2=ngs[:, j:j + 1],
                                            op0=ALU.mult, op1=ALU.min), reads=[ka, "ngs"], writes=[ka])
            P.act(lambda e: e.activation(out=Pb[:, :].rearrange("p (a b) -> p a b", a=2), in_=sc, func=AF.Exp,
                                         bias=ast[:, 1:2], scale=0.125),
                  reads=[("ps", sbk), ("ps", sbk + 1), ka], writes=[("Pb", i % 2)])
            P.act(lambda e: e.activation(out=ast[:, 4:8], in_=sm[:, C_SINK + 4 * j:C_SINK + 4 * j + 4], func=AF.Exp,
                                         bias=ast[:, 1:2], scale=1.0), reads=[ka], writes=[("ase", i % 4)])

        def att_Y1(i):
            Pb = Pbs[i % 2]
            PT = PTs[i % 2]

            def trp(e):
                last = None
                for k in range(8):
                    last = e.transpose(out=psb[:, 4, k * 128:(k + 1) * 128], in_=Pb[:, k * 128:(k + 1) * 128],
                                       identity=identb[:, :])
                return last
            P.pe(trp, reads=[("Pb", i % 2)], writes=[("ps", 4)])
            P.dve(lambda e: e.tensor_tensor(out=PT[:, :, :, :], in0=psb[:, 4, :].rearrange("p (h k q) -> p h k q", h=4, k=2),
                                            in1=cap(maskb, 0, [[256, 128], [0, 4], [128, 2], [1, 128]]), op=ALU.mult),
                  reads=[("ps", 4), "maskb"], writes=[("PT", i % 2)])

        def att_Y2(i):
            tt, j = aits[i]
            ab = tt % 2
            PT = PTs[i % 2]
            ast = asts[i % 4]

            def pv_(e):
                last = None
                for i4 in range(4):
                    hh = (0, 2, 1, 3)[i4]
                    for kb in range(2):
                        last = e.matmul(ps[:, 5, hh * 65:hh * 65 + 65], lhsT=PT[:, i4, kb, :], rhs=v1[:, tt + kb, j, :],
                                        start=(kb == 0), stop=(kb == 1))
                return last
            P.pe(pv_, reads=[("PT", i % 2)], writes=[("ps", 5)])
            po = ps[:, 5, 0:260].rearrange("p (h d) -> p h d", h=4)
            P.dve(lambda e: e.tensor_tensor(out=ast[:, 8:12], in0=po[:, :, 64], in1=ast[:, 4:8], op=ALU.add),
                  reads=[("ps", 5), ("ase", i % 4)], writes=[("aden", i % 4)])
            P.dve(lambda e: e.reciprocal(out=ast[:, 12:16], in_=ast[:, 8:12]), reads=[("aden", i % 4)], writes=[("ard", i % 4)])
            P.dve(lambda e: e.tensor_tensor(out=attn[ab][:, j * 256:(j + 1) * 256].rearrange("p (h d) -> p h d", h=4),
                                            in0=po[:, :, 0:64], in1=cap(ast, 12, [[16, 128], [1, 4], [0, 64]]), op=ALU.mult),
                  reads=[("ps", 5), ("ard", i % 4)], writes=[("attn", ab, j)])
            if j != 3:
                return
            ak = [("attn", ab, jj) for jj in range(4)]
            c0 = 32 + 8 * ab
            anb = anbs[ab]
            P.dve(lambda e: e.scalar_tensor_tensor(out=ajunk[:, :], in0=attn[ab][:, :], scalar=1.0, in1=attn[ab][:, :],
                                                   op0=ALU.mult, op1=ALU.mult, accum_out=stat[:, c0:c0 + 1]),
                  reads=ak, writes=["ajunk", ("st", c0)])
            rstd_from(("st", c0), stat[:, c0:c0 + 1], stat[:, c0 + 2:c0 + 3], ("st", c0 + 2), 1024, stat[:, c0 + 1:c0 + 2],
                      ("st", c0 + 1))
            P.dve(lambda e: e.scalar_tensor_tensor(out=anb[:, :], in0=attn[ab][:, :], scalar=stat[:, c0 + 2:c0 + 3],
                                                   in1=gat[:, :], op0=ALU.mult, op1=ALU.mult),
                  reads=ak + [("st", c0 + 2), "gat"], writes=[("anb", ab)])

            def tra(e):
                last = None
                for c in range(8):
                    last = e.transpose(out=psb[:, 6, c * 128:(c + 1) * 128], in_=anb[:, c * 128:(c + 1) * 128],
                                       identity=identb[:, :])
                return last
            P.pe(tra, reads=[("anb", ab)], writes=[("ps", 6)])
            P.act(lambda e: e.activation(out=actT[:, 0:8, tt * 128:(tt + 1) * 128],
                                         in_=psb[:, 6, :].rearrange("p (a b) -> p a b", a=8), func=AF.Copy),
                  reads=[("ps", 6)], writes=[("mixT", tt)])

        na = len(aits)
        att_X(0)
        if na > 1:
            att_X(1)
        att_Y1(0)
        for i in range(na):
            if i + 2 < na:
                att_X(i + 2)
            if i + 1 < na:
                att_Y1(i + 1)
            att_Y2(i)
        P.barrier()
        if STOP == 3:
            P.emit()
            return nc

        R0 = 98304
        BbTr = sb(R0, [128, 8, 4, 128], BF16)
        BbTi = sb(R0 + 8192, [128, 8, 4, 128], BF16)
        CTr = sb(R0 + 16384, [128, 32, 128], BF16)
        CTi = sb(R0 + 24576, [128, 32, 128], BF16)
        tok = sb(R0 + 32768, [128, 2048], F32)
        fre = sb(49152, [128, 1024], F32)
        fim = sb(53248, [128, 1024], F32)
        lrB = sb(57344, [128, 1024], F32)
        liB = sb(61440, [128, 1024], F32)
        LBr = sb(R0 + 40960, [128, 8, 4, 128], BF16)
        LBi = sb(R0 + 49152, [128, 8, 4, 128], BF16)
        K1blk = sb(R0 + 57344, [128, 8, 128], BF16)
        CIr = sb(32768, [128, 32, 128], BF16)
        CIi = sb(40960, [128, 32, 128], BF16)
        S_ = [sb(SCR + 4096 * i, [128, 1024], F32) for i in range(10)]
        P.dma("sync", tok[:, :], tokc, writes=["tok"])
        P.act(lambda e: e.activation(out=stat[:, 32:64], in_=sm[:, C_LDA:C_LDA + 32], func=AF.Exp), reads=[], writes=["dtA"])
        P.dve(lambda e: e.tensor_tensor(out=rho[:, :], in0=sm[:, C_ARA:C_ARA + 32], in1=stat[:, 32:64], op=ALU.mult),
              reads=["dtA"], writes=["rho0"])
        P.act(lambda e: e.activation(out=rho[:, :], in_=rho[:, :], func=AF.Exp), reads=["rho0"], writes=["rho"])
        P.dve(lambda e: e.scalar_tensor_tensor(out=thp[:, :], in0=sm[:, C_AIA:C_AIA + 32], scalar=INV2PI, in1=stat[:, 32:64],
                                               op0=ALU.mult, op1=ALU.mult), reads=["dtA"], writes=["thp"])
        AR, AI, LD = S_[0], S_[1], S_[2]
        for i, t_ in enumerate((AR, AI, LD)):
            P.dma("sync", t_[:, :], lb3[:, i, :], writes=[("S", i)])
        P.act(lambda e: e.activation(out=LD[:, :], in_=LD[:, :], func=AF.Exp), reads=[("S", 2)], writes=[("S", 2)])
        P.dve(lambda e: e.tensor_tensor(out=S_[3][:, :], in0=AR[:, :], in1=LD[:, :], op=ALU.mult), reads=[("S", 0), ("S", 2)],
              writes=[("S", 3)])
        P.act(lambda e: e.activation(out=S_[3][:, :], in_=S_[3][:, :], func=AF.Exp), reads=[("S", 3)], writes=[("S", 3)])
        P.dve(lambda e: e.scalar_tensor_tensor(out=S_[4][:, :], in0=AI[:, :], scalar=INV2PI, in1=LD[:, :], op0=ALU.mult,
                                               op1=ALU.mult), reads=[("S", 1), ("S", 2)], writes=[("S", 4)])
        P.dve(lambda e: e.tensor_scalar(out=S_[5][:, :], in0=S_[4][:, :], scalar1=MAGIC, scalar2=MAGIC, op0=ALU.add,
                                        op1=ALU.subtract), reads=[("S", 4)], writes=[("S", 5)])
        P.dve(lambda e: e.tensor_tensor(out=S_[4][:, :], in0=S_[4][:, :], in1=S_[5][:, :], op=ALU.subtract),
              reads=[("S", 4), ("S", 5)], writes=[("S", 4)])
        P.dve(lambda e: e.scalar_tensor_tensor(out=S_[5][:, :], in0=S_[4][:, :], scalar=-1.0, in1=S_[4][:, :], op0=ALU.mult,
                                               op1=ALU.max), reads=[("S", 4)], writes=[("S", 5)])
        P.act(lambda e: e.activation(out=S_[6][:, :], in_=S_[4][:, :], func=AF.Sin, scale=TWO_PI), reads=[("S", 4)],
              writes=[("S", 6)])
        P.act(lambda e: e.activation(out=S_[7][:, :], in_=S_[5][:, :], func=AF.Sin, scale=-TWO_PI, bias=math.pi / 2),
              reads=[("S", 5)], writes=[("S", 7)])
        P.dve(lambda e: e.tensor_tensor(out=S_[6][:, :], in0=S_[6][:, :], in1=S_[3][:, :], op=ALU.mult),
              reads=[("S", 6), ("S", 3)], writes=[("S", 6)])
        P.dve(lambda e: e.tensor_tensor(out=S_[7][:, :], in0=S_[7][:, :], in1=S_[3][:, :], op=ALU.mult),
              reads=[("S", 7), ("S", 3)], writes=[("S", 7)])
        P.act(lambda e: e.activation(out=lrB[:, :], in_=S_[7][:, :], func=AF.Copy), reads=[("S", 7)], writes=["lrB"])
        P.act(lambda e: e.activation(out=liB[:, :], in_=S_[6][:, :], func=AF.Copy), reads=[("S", 6)], writes=["liB"])
        P.dve(lambda e: e.tensor_scalar(out=S_[7][:, :], in0=S_[7][:, :], scalar1=-1.0, scalar2=None, op0=ALU.add),
              reads=[("S", 7), "lrB"], writes=[("S", 7)])
        P.dve(lambda e: e.tensor_tensor(out=S_[3][:, :], in0=AR[:, :], in1=AR[:, :], op=ALU.mult), reads=[("S", 0)],
              writes=[("S", 3)])
        P.dve(lambda e: e.tensor_tensor(out=S_[4][:, :], in0=AI[:, :], in1=AI[:, :], op=ALU.mult), reads=[("S", 1)],
              writes=[("S", 4)])
        P.dve(lambda e: e.tensor_tensor(out=S_[3][:, :], in0=S_[3][:, :], in1=S_[4][:, :], op=ALU.add),
              reads=[("S", 3), ("S", 4)], writes=[("S", 3)])
        P.dve(lambda e: e.reciprocal(out=S_[3][:, :], in_=S_[3][:, :]), reads=[("S", 3)], writes=[("S", 3)])
        P.dve(lambda e: e.tensor_tensor(out=S_[4][:, :], in0=S_[7][:, :], in1=AR[:, :], op=ALU.mult),
              reads=[("S", 7), ("S", 0)], writes=[("S", 4)])
        P.dve(lambda e: e.tensor_tensor(out=S_[5][:, :], in0=S_[6][:, :], in1=AI[:, :], op=ALU.mult),
              reads=[("S", 6), ("S", 1)], writes=[("S", 5)])
        P.dve(lambda e: e.tensor_tensor(out=S_[4][:, :], in0=S_[4][:, :], in1=S_[5][:, :], op=ALU.add),
              reads=[("S", 4), ("S", 5)], writes=[("S", 4)])
        P.dve(lambda e: e.tensor_tensor(out=fre[:, :], in0=S_[4][:, :], in1=S_[3][:, :], op=ALU.mult),
              reads=[("S", 4), ("S", 3)], writes=["fre"])
        P.dve(lambda e: e.tensor_tensor(out=S_[4][:, :], in0=S_[6][:, :], in1=AR[:, :], op=ALU.mult),
              reads=[("S", 6), ("S", 0)], writes=[("S", 4)])
        P.dve(lambda e: e.tensor_tensor(out=S_[5][:, :], in0=S_[7][:, :], in1=AI[:, :], op=ALU.mult),
              reads=[("S", 7), ("S", 1)], writes=[("S", 5)])
        P.dve(lambda e: e.tensor_tensor(out=S_[4][:, :], in0=S_[4][:, :], in1=S_[5][:, :], op=ALU.subtract),
              reads=[("S", 4), ("S", 5)], writes=[("S", 4)])
        P.dve(lambda e: e.tensor_tensor(out=fim[:, :], in0=S_[4][:, :], in1=S_[3][:, :], op=ALU.mult),
              reads=[("S", 4), ("S", 3)], writes=["fim"])
        P.barrier()
        if STOP == 4:
            P.emit()
            return nc
        Bq = [sb(SCR + 4096 * i, [128, 1024], F32) for i in range(8)]
        Bcr, Bci, T1, T2, Bbr_, Bbi_, Lr_, Li_ = Bq
        P.dma("sync", Bcr[:, :], bexp[:, 0, :], writes=["Bcr"])
        P.dma("sync", Bci[:, :], bexp[:, 1, :], writes=["Bci"])

        def cmul(outr, outi, ar, ai, br, bi, kr, ki):
            P.dve(lambda e: e.tensor_tensor(out=T1[:, :], in0=ar[:, :], in1=br[:, :], op=ALU.mult), reads=kr, writes=["T1"])
            P.dve(lambda e: e.tensor_tensor(out=T2[:, :], in0=ai[:, :], in1=bi[:, :], op=ALU.mult), reads=kr, writes=["T2"])
            P.dve(lambda e: e.tensor_tensor(out=outr[:, :], in0=T1[:, :], in1=T2[:, :], op=ALU.subtract), reads=["T1", "T2"],
                  writes=[ki + "r"])
            P.dve(lambda e: e.tensor_tensor(out=T1[:, :], in0=ar[:, :], in1=bi[:, :], op=ALU.mult), reads=kr + [ki + "r"],
                  writes=["T1"])
            P.dve(lambda e: e.tensor_tensor(out=T2[:, :], in0=ai[:, :], in1=br[:, :], op=ALU.mult), reads=kr + [ki + "r"],
                  writes=["T2"])
            P.dve(lambda e: e.tensor_tensor(out=outi[:, :], in0=T1[:, :], in1=T2[:, :], op=ALU.add), reads=["T1", "T2"],
                  writes=[ki + "i"])
        cmul(Bbr_, Bbi_, fre, fim, Bcr, Bci, ["Bcr", "Bci", "fre", "fim"], "Bb")
        cmul(Lr_, Li_, lrB, liB, Bbr_, Bbi_, ["Bbr", "Bbi", "lrB", "liB"], "L")
        for src, dst, k in ((Bbr_, BbTr, "Bbr"), (Bbi_, BbTi, "Bbi"), (Lr_, LBr, "Lr"), (Li_, LBi, "Li")):
            for a in range(4):
                P.dve(lambda e, src=src, dst=dst, a=a: e.tensor_scalar(
                    out=dst[:, :, a, :], in0=src[:, :].rearrange("p (k n) -> p k n", k=8), scalar1=maskA[:, a:a + 1],
                    scalar2=None, op0=ALU.mult), reads=[k, "maskA"], writes=[("exp", k, a)])
        P.barrier()
        if STOP == 5:
            P.emit()
            return nc
        cA = sb(SCR + 40960, [128, 32], F32)
        sA = sb(SCR + 40960 + 128, [128, 32], F32)
        tA = sb(SCR + 40960 + 256, [128, 32], F32)
        uA = sb(SCR + 40960 + 384, [128, 32], F32)
        rho2 = stat[:, 32:64]
        P.dve(lambda e: e.tensor_scalar(out=tA[:, :], in0=thp[:, :], scalar1=MAGIC, scalar2=MAGIC, op0=ALU.add,
                                        op1=ALU.subtract), reads=[], writes=["tA"])
        P.dve(lambda e: e.tensor_tensor(out=tA[:, :], in0=thp[:, :], in1=tA[:, :], op=ALU.subtract), reads=["tA"],
              writes=["tA"])
        P.dve(lambda e: e.scalar_tensor_tensor(out=uA[:, :], in0=tA[:, :], scalar=-1.0, in1=tA[:, :], op0=ALU.mult,
                                               op1=ALU.max), reads=["tA"], writes=["uA"])
        P.act(lambda e: e.activation(out=sA[:, :], in_=tA[:, :], func=AF.Sin, scale=TWO_PI), reads=["tA"], writes=["sA"])
        P.act(lambda e: e.activation(out=cA[:, :], in_=uA[:, :], func=AF.Sin, scale=-TWO_PI, bias=magp[:, 2:3]),
              reads=["uA"], writes=["cA"])
        P.dve(lambda e: e.reciprocal(out=uA[:, :], in_=rho[:, :]), reads=["cA"], writes=["uA"])
        P.dve(lambda e: e.tensor_tensor(out=cA[:, :], in0=cA[:, :], in1=uA[:, :], op=ALU.mult), reads=["cA", "uA"],
              writes=["cA"])
        P.dve(lambda e: e.tensor_tensor(out=sA[:, :], in0=sA[:, :], in1=uA[:, :], op=ALU.mult), reads=["sA", "uA"],
              writes=["sA"])
        P.dve(lambda e: e.tensor_tensor(out=rho2, in0=rho[:, :], in1=rho[:, :], op=ALU.mult), reads=[], writes=["rho2"])
        Cre = sb(SCR, [128, 16, 128], F32)
        Cim = sb(SCR + 8192, [128, 16, 128], F32)
        U1 = sb(SCR + 16384, [128, 16, 128], F32)
        U2 = sb(SCR + 24576, [128, 16, 128], F32)
        for hp in range(2):
            psl = slice(16 * hp, 16 * hp + 16)
            P.dma("sync", Cre[:, :, :], cexp[:, 0, hp * 2048:(hp + 1) * 2048].rearrange("p (a b) -> p a b", a=16),
                  writes=["Cre"])
            P.dma("sync", Cim[:, :, :], cexp[:, 1, hp * 2048:(hp + 1) * 2048].rearrange("p (a b) -> p a b", a=16),
                  writes=["Cim"])
            P.act(lambda e, psl=psl: e.activation(out=CTr[:, psl, :], in_=Cre[:, :, :], func=AF.Copy), reads=["Cre"],
                  writes=["CTr"])
            P.act(lambda e, psl=psl: e.activation(out=CTi[:, psl, :], in_=Cim[:, :, :], func=AF.Copy, scale=-1.0),
                  reads=["Cim"], writes=["CTi"])
            cAb = cap(cA, 16 * hp, [[32, 128], [1, 16], [0, 128]])
            sAb = cap(sA, 16 * hp, [[32, 128], [1, 16], [0, 128]])
            P.dve(lambda e, cAb=cAb: e.tensor_tensor(out=U1[:, :, :], in0=Cre[:, :, :], in1=cAb, op=ALU.mult),
                  reads=["Cre", "cA"], writes=["U1"])
            P.dve(lambda e, sAb=sAb: e.tensor_tensor(out=U2[:, :, :], in0=Cim[:, :, :], in1=sAb, op=ALU.mult),
                  reads=["Cim", "sA"], writes=["U2"])
            P.dve(lambda e, psl=psl: e.tensor_tensor(out=CIr[:, psl, :], in0=U1[:, :, :], in1=U2[:, :, :], op=ALU.add),
                  reads=["U1", "U2"], writes=["CIr"])
            P.dve(lambda e, sAb=sAb: e.tensor_tensor(out=U1[:, :, :], in0=Cre[:, :, :], in1=sAb, op=ALU.mult),
                  reads=["Cre", "sA", "CIr"], writes=["U1"])
            P.dve(lambda e, cAb=cAb: e.tensor_tensor(out=U2[:, :, :], in0=Cim[:, :, :], in1=cAb, op=ALU.mult),
                  reads=["Cim", "cA", "CIr"], writes=["U2"])
            P.dve(lambda e, psl=psl: e.tensor_tensor(out=CIi[:, psl, :], in0=U1[:, :, :], in1=U2[:, :, :], op=ALU.subtract),
                  reads=["U1", "U2"], writes=["CIi"])
        Xs = [sb(SCR + 32768 + 2048 * i, [128, 8, 128], BF16) for i in range(2)]
        for blk in range(8):
            xb = Xs[blk % 2]
            tbk = 2 * (blk % 2)

            def trx(e, blk=blk, tbk=tbk):
                last = None
                for a in range(4):
                    for ri, Bt in enumerate((BbTr, BbTi)):
                        k = 2 * a + ri
                        last = e.transpose(out=psb[:, tbk, k * 128:(k + 1) * 128], in_=Bt[:, blk, a, :], identity=identb[:, :])
                return last
            P.pe(trx, reads=["BbTr", "BbTi"], writes=[("ps", tbk)])
            P.act(lambda e, xb=xb, tbk=tbk: e.activation(out=xb[:, :, :], in_=psb[:, tbk, :].rearrange("p (a b) -> p a b", a=8),
                                                         func=AF.Copy), reads=[("ps", tbk)], writes=[("Xs", blk % 2)])

            def mk1(e, blk=blk, xb=xb, tbk=tbk):
                last = None
                for a in range(4):
                    pp = 4 * blk + a
                    for ri, Ct in enumerate((CIr, CIi)):
                        k = 2 * a + ri
                        last = e.matmul(ps[:, tbk + 1, 0:128], lhsT=xb[:, k, :], rhs=Ct[:, pp, :], start=(k == 0), stop=(k == 7))
                return last
            P.pe(mk1, reads=[("Xs", blk % 2), "CIr", "CIi"], writes=[("ps", tbk + 1)])
            P.act(lambda e, blk=blk, tbk=tbk: e.activation(out=K1blk[:, blk, :], in_=ps[:, tbk + 1, 0:128], func=AF.Copy,
                                                           scale=-1.0), reads=[("ps", tbk + 1)], writes=["K1blk"])
        P.barrier()
        if STOP == 6:
            P.emit()
            return nc

        NCH = 512

        def sl2(off, n, dt):
            return [sb(SCR + off + n * i, [128, NCH], dt) for i in range(2)]
        yqs = sl2(0, 2048, F32)
        kfqs = sl2(4096, 2048, F32)
        SINfs = sl2(8192, 2048, F32)
        COSfs = sl2(12288, 2048, F32)
        tb16 = [[sb(SCR + 16384 + 1024 * (3 * s_ + k), [128, NCH], BF16) for k in range(3)] for s_ in range(2)]
        pbuf = [[sb(SCR + 22528 + 1024 * (4 * s_ + k), [128, NCH], BF16) for k in range(4)] for s_ in range(2)]
        Rre = sb(SCR + 30720, [128, NCH], BF16)
        Rim = sb(SCR + 31744, [128, NCH], BF16)
        qbufs = [[sb(SCR + 32768 + 1024 * (4 * s_ + k), [128, NCH], BF16) for k in range(4)] for s_ in range(2)]
        ysb = sb(49152, [128, 1024], F32)
        gtmp = sb(53248, [128, 1024], F32)
        gsig = [sb(57344 + 2048 * i, [128, 1024], BF16) for i in range(2)]

        iters = [(blk, half, a) for blk in range(8) for half in range(2) for a in range(4)]

        def eo(ap_, which):
            return ap_.rearrange("p (c t) -> p c t", t=2)[:, :, which]

        def stageA0(idx):
            blk, half, a = iters[idx]
            s_ = idx % 2
            pp = 4 * blk + a
            yq, kfq = yqs[s_], kfqs[s_]
            ky, kk = ("yq", s_), ("kfq", s_)
            tokv = eo(tok[:, half * 1024:(half + 1) * 1024], 1)
            P.act(lambda e: e.activation(out=yq[:, :], in_=tokv, func=AF.Copy, scale=thp[:, pp:pp + 1]),
                  reads=["tok", "thp"], writes=[ky])
            P.act(lambda e: e.activation(out=kfq[:, :], in_=yq[:, :], func=AF.Identity, bias=magp[:, 0:1], scale=1.0),
                  reads=[ky, "magp"], writes=[kk])
            P.act(lambda e: e.activation(out=kfq[:, :], in_=kfq[:, :], func=AF.Identity, bias=magp[:, 1:2], scale=1.0),
                  reads=[kk, "magp"], writes=[kk])
            P.add("gpsimd", lambda e: e.tensor_tensor(out=yq[:, :], in0=yq[:, :], in1=kfq[:, :], op=ALU.subtract),
                  reads=[ky, kk], writes=[ky])

        def stageA(idx):
            blk, half, a = iters[idx]
            s_ = idx % 2
            pp = 4 * blk + a
            ukey = ("uT", blk, half)
            SINb, NSINb, COSb = tb16[s_]
            pb = pbuf[s_]
            yq, kfq, SINf, COSf = yqs[s_], kfqs[s_], SINfs[s_], COSfs[s_]
            ky, kk, ksf, kcf = ("yq", s_), ("kfq", s_), ("SINf", s_), ("COSf", s_)
            b0 = 2 * s_
            ue = eo(uT[:, blk, half * 1024:(half + 1) * 1024], 0)
            uo = eo(uT[:, blk, half * 1024:(half + 1) * 1024], 1)

            def bu(e):
                last = None
                for ri, (Lt, Bt) in enumerate(((LBr, BbTr), (LBi, BbTi))):
                    e.matmul(ps[:, b0 + ri, :], lhsT=Lt[:, blk, a, :], rhs=ue, start=True, stop=False)
                    last = e.matmul(ps[:, b0 + ri, :], lhsT=Bt[:, blk, a, :], rhs=uo, start=False, stop=True)
                return last
            P.pe(bu, reads=[ukey], writes=[("ps", b0), ("ps", b0 + 1)])
            P.act(lambda e: e.activation(out=kfq[:, :], in_=yq[:, :], func=AF.Abs), reads=[ky], writes=[kk])
            P.act(lambda e: e.activation(out=SINf[:, :], in_=yq[:, :], func=AF.Sin, scale=TWO_PI), reads=[ky], writes=[ksf])
            P.act(lambda e: e.activation(out=COSf[:, :], in_=kfq[:, :], func=AF.Sin, scale=-TWO_PI, bias=magp[:, 2:3]),
                  reads=[kk, "magp"], writes=[kcf])
            P.act(lambda e: e.activation(out=SINb[:, :], in_=yq[:, :], func=AF.Sin, scale=TWO_PI), reads=[ky],
                  writes=[("SINb", s_)])
            P.act(lambda e: e.activation(out=NSINb[:, :], in_=yq[:, :], func=AF.Sin, scale=-TWO_PI), reads=[ky],
                  writes=[("NSINb", s_)])
            P.act(lambda e: e.activation(out=COSb[:, :], in_=kfq[:, :], func=AF.Sin, scale=-TWO_PI, bias=magp[:, 2:3]),
                  reads=[kk, "magp"], writes=[("COSb", s_)])
            bre = ps[:, b0, :]
            bim = ps[:, b0 + 1, :]
            P.dve(lambda e: e.tensor_tensor(out=pb[0][:, :], in0=bre, in1=COSf[:, :], op=ALU.mult),
                  reads=[("ps", b0), kcf], writes=[("p", s_, 0)])
            P.dve(lambda e: e.tensor_tensor(out=pb[1][:, :], in0=bim, in1=SINf[:, :], op=ALU.mult),
                  reads=[("ps", b0 + 1), ksf], writes=[("p", s_, 1)])
            P.dve(lambda e: e.tensor_tensor(out=pb[2][:, :], in0=bim, in1=COSf[:, :], op=ALU.mult),
                  reads=[("ps", b0 + 1), kcf], writes=[("p", s_, 2)])
            P.dve(lambda e: e.scalar_tensor_tensor(out=pb[3][:, :], in0=bre, scalar=-1.0, in1=SINf[:, :], op0=ALU.mult,
                                                   op1=ALU.mult), reads=[("ps", b0), ksf], writes=[("p", s_, 3)])

        def stageB(idx):
            blk, half, a = iters[idx]
            s_ = idx % 2
            pp = 4 * blk + a
            SINb, NSINb, COSb = tb16[s_]
            pb = pbuf[s_]
            qbuf = qbufs[s_]
            rb = cap(stat, 32 + pp, [[64, 128], [0, NCH]])
            i0 = rlast[:, pp, 0:1] if half == 1 else 0.0
            i1 = rlast[:, pp, 1:2] if half == 1 else 0.0

            def addE(k0, bank):
                def f(e):
                    e.matmul(ps[:, bank, :], lhsT=identb[:, :], rhs=pb[k0][:, :], start=True, stop=False)
                    return e.matmul(ps[:, bank, :], lhsT=identb[:, :], rhs=pb[k0 + 1][:, :], start=False, stop=True)
                return f
            P.pe(addE(0, 6), reads=[("p", s_, 0), ("p", s_, 1)], writes=[("ps", 6)])
            P.pe(addE(2, 7), reads=[("p", s_, 2), ("p", s_, 3)], writes=[("ps", 7)])
            P.dve(lambda e: e.tensor_tensor_scan(out=Rre[:, :], data0=rb, data1=ps[:, 6, :], initial=i0, op0=ALU.mult,
                                                 op1=ALU.add), reads=[("ps", 6), "rho2", ("rl", pp)], writes=["Rre"])
            P.dve(lambda e: e.tensor_tensor(out=qbuf[0][:, :], in0=Rre[:, :], in1=COSb[:, :], op=ALU.mult),
                  reads=["Rre", ("COSb", s_)], writes=[("q", s_, 0)])
            P.dve(lambda e: e.tensor_tensor(out=qbuf[3][:, :], in0=Rre[:, :], in1=SINb[:, :], op=ALU.mult),
                  reads=["Rre", ("SINb", s_)], writes=[("q", s_, 3)])
            P.dve(lambda e: e.tensor_tensor_scan(out=Rim[:, :], data0=rb, data1=ps[:, 7, :], initial=i1, op0=ALU.mult,
                                                 op1=ALU.add), reads=[("ps", 7), "rho2", ("rl", pp)], writes=["Rim"])
            P.dve(lambda e: e.tensor_tensor(out=qbuf[1][:, :], in0=Rim[:, :], in1=NSINb[:, :], op=ALU.mult),
                  reads=["Rim", ("NSINb", s_)], writes=[("q", s_, 1)])
            P.dve(lambda e: e.tensor_tensor(out=qbuf[2][:, :], in0=Rim[:, :], in1=COSb[:, :], op=ALU.mult),
                  reads=["Rim", ("COSb", s_)], writes=[("q", s_, 2)])
            if half == 0:
                P.dve(lambda e: e.tensor_copy(out=rlast[:, pp, 0:1], in_=Rre[:, NCH - 1:NCH]), reads=["Rre"],
                      writes=[("rl", pp)])
                P.dve(lambda e: e.tensor_copy(out=rlast[:, pp, 1:2], in_=Rim[:, NCH - 1:NCH]), reads=["Rim", ("rl", pp)],
                      writes=[("rl", pp)])

        def stageC(idx):
            blk, half, a = iters[idx]
            s_ = idx % 2
            pp = 4 * blk + a
            ukey = ("uT", blk, half)
            usl = uT[:, blk, half * 1024:(half + 1) * 1024]
            qbuf = qbufs[s_]

            def cp(e):
                if a == 0:
                    e.matmul(ps[:, 5, :], lhsT=K1blk[:, blk, :], rhs=eo(usl, 1), start=True, stop=False)
                e.matmul(ps[:, 4, :], lhsT=CTr[:, pp, :], rhs=qbuf[0][:, :], start=(a == 0), stop=False)
                e.matmul(ps[:, 4, :], lhsT=CTr[:, pp, :], rhs=qbuf[1][:, :], start=False, stop=False)
                e.matmul(ps[:, 4, :], lhsT=CTi[:, pp, :], rhs=qbuf[2][:, :], start=False, stop=False)
                e.matmul(ps[:, 4, :], lhsT=CTi[:, pp, :], rhs=qbuf[3][:, :], start=False, stop=(a == 3))
                e.matmul(ps[:, 5, :], lhsT=CIr[:, pp, :], rhs=qbuf[0][:, :], start=False, stop=False)
                e.matmul(ps[:, 5, :], lhsT=CIr[:, pp, :], rhs=qbuf[1][:, :], start=False, stop=False)
                e.matmul(ps[:, 5, :], lhsT=CIi[:, pp, :], rhs=qbuf[2][:, :], start=False, stop=False)
                return e.matmul(ps[:, 5, :], lhsT=CIi[:, pp, :], rhs=qbuf[3][:, :], start=False, stop=(a == 3))
            P.pe(cp, reads=[("q", s_, k) for k in range(4)] + [ukey], writes=[("ps", 4), ("ps", 5)])
            if a != 3:
                return
            for which, bank in ((1, 4), (0, 5)):
                P.dve(lambda e, which=which, bank=bank: e.scalar_tensor_tensor(
                    out=eo(usl, which), in0=eo(usl, which), scalar=sm[:, C_DSK + blk:C_DSK + blk + 1], in1=ps[:, bank, :],
                    op0=ALU.mult, op1=ALU.add), reads=[ukey, ("ps", bank)], writes=[ukey])

        stageA0(0)
        stageA0(1)
        stageA(0)
        for idx in range(len(iters)):
            if idx + 2 < len(iters):
                stageA0(idx + 2)
            if idx + 1 < len(iters):
                stageA(idx + 1)
            stageB(idx)
            if idx >= 1:
                stageC(idx - 1)
        stageC(len(iters) - 1)
        gpieces = [(blk, half) for blk in range(8) for half in range(2)]

        def gel_a(gi):
            blk, half = gpieces[gi]
            tsl = slice(half * 1024, (half + 1) * 1024)
            ukey = ("uT", blk, half)
            g_ = (ysb, gtmp)[gi % 2]
            gs = gsig[gi % 2]
            gk, gsk = ("gel", gi % 2), ("gsig", gi % 2)
            P.dve(lambda e: e.tensor_tensor(out=g_[:, :], in0=uT[:, blk, tsl], in1=uT[:, blk, tsl], op=ALU.mult),
                  reads=[ukey], writes=[gk])
            P.dve(lambda e: e.tensor_scalar(out=g_[:, :], in0=g_[:, :], scalar1=0.044715, scalar2=1.0, op0=ALU.mult,
                                            op1=ALU.add), reads=[gk], writes=[gk])
            P.dve(lambda e: e.tensor_tensor(out=g_[:, :], in0=g_[:, :], in1=uT[:, blk, tsl], op=ALU.mult),
                  reads=[gk, ukey], writes=[gk])
            P.act(lambda e: e.activation(out=gs[:, :], in_=g_[:, :], func=AF.Sigmoid, scale=GELU_C), reads=[gk],
                  writes=[gsk])

        def gel_b(gi):
            blk, half = gpieces[gi]
            tsl = slice(half * 1024, (half + 1) * 1024)
            ukey = ("uT", blk, half)
            gs = gsig[gi % 2]
            P.dve(lambda e: e.tensor_tensor(out=uT[:, blk, tsl], in0=gs[:, :], in1=uT[:, blk, tsl], op=ALU.mult),
                  reads=[("gsig", gi % 2), ukey], writes=[ukey])

        gel_a(0)
        for gi in range(len(gpieces)):
            if gi + 1 < len(gpieces):
                gel_a(gi + 1)
            gel_b(gi)
        P.barrier()
        if STOP == 7:
            P.emit()
            return nc

        wg = sb(SCR + 24576, [128, 8, 1024], BF16)
        sg = [sb(SCR + 2048 * i, [128, 512], F32) for i in range(2)]
        ssm = [sb(SCR + 4096 + 2048 * i, [128, 512], F32) for i in range(2)]
        sqb = [sb(SCR + 8192 + 1024 * i, [128, 512], BF16) for i in range(2)]
        rbc = sb(SCR + 12288, [128, 2048], F32)
        ones128 = sb(SCR + 20480, [128, 128], BF16)
        Wo = sb(R0, [128, 16, 2048], BF16)
        w_o_v = w_o.rearrange("(c p) n -> p c n", p=128)
        P.dma("gpsimd", wg[:, :, :], w_glu.rearrange("(c p) n -> p c n", p=128), writes=["wg"])
        for q4 in range(4):
            P.dma("gpsimd", Wo[:, 4 * q4:4 * q4 + 4, :], w_o_v[:, 4 * q4:4 * q4 + 4, :], writes=[("Wo", q4)])
        P.dve(lambda e: e.memset(ones128[:, :], 1.0), writes=["ones128"])
        glu_it = [(e8, tb) for e8 in range(8) for tb in range(4)]

        def glu_mm(i):
            e8, tb = glu_it[i]
            bk = i % 4

            def mg(e):
                last = None
                for c in range(8):
                    last = e.matmul(ps[:, bk, :], lhsT=wg[:, c, e8 * 128:(e8 + 1) * 128],
                                    rhs=uT[:, c, tb * 512:(tb + 1) * 512], start=(c == 0), stop=(c == 7))
                return last
            P.pe(mg, reads=["wg"], writes=[("ps", bk)])

        def glu_ew(i):
            e8, tb = glu_it[i]
            bk = i % 4
            b2 = i % 2
            P.act(lambda e: e.activation(out=sg[b2][:, :], in_=ps[:, bk, :], func=AF.Sigmoid,
                                         bias=sm[:, C_BGLU + e8:C_BGLU + e8 + 1], scale=1.0),
                  reads=[("ps", bk)], writes=[("sg", b2)])
            P.dve(lambda e: e.tensor_tensor(out=ssm[b2][:, :], in0=uT[:, e8, tb * 512:(tb + 1) * 512], in1=sg[b2][:, :],
                                            op=ALU.mult), reads=[("sg", b2)], writes=[("ssm", b2)])
            P.dve(lambda e: e.tensor_tensor(out=sqb[b2][:, :], in0=ssm[b2][:, :], in1=ssm[b2][:, :], op=ALU.mult),
                  reads=[("ssm", b2)], writes=[("sqb", b2)])
            P.dve(lambda e: e.tensor_scalar(out=actT[:, 8 + e8, tb * 512:(tb + 1) * 512], in0=ssm[b2][:, :],
                                            scalar1=sm[:, C_GSSM + e8:C_GSSM + e8 + 1], scalar2=None, op0=ALU.mult),
                  reads=[("ssm", b2)], writes=[("mixS", e8)])

        def glu_sq(i):
            e8, tb = glu_it[i]
            b2 = i % 2
            P.pe(lambda e: e.matmul(ps[:, 4 + tb, :], lhsT=ones128[:, :], rhs=sqb[b2][:, :], start=(e8 == 0), stop=(e8 == 7)),
                 reads=[("sqb", b2), "ones128"], writes=[("ps", 4 + tb)])

        glu_mm(0)
        for i in range(len(glu_it)):
            glu_ew(i)
            if i + 1 < len(glu_it):
                glu_mm(i + 1)
            glu_sq(i)
        P.dve(lambda e: e.tensor_scalar(out=rbc[:, :].rearrange("p (a b) -> p a b", a=4), in0=ps[:, 4:8, :], scalar1=1.0 / 1024,
                                        scalar2=EPS, op0=ALU.mult, op1=ALU.add), reads=[("ps", 4 + t) for t in range(4)],
              writes=["rbc"])
        P.act(lambda e: e.activation(out=rbc[:, :], in_=rbc[:, :], func=AF.Sqrt), reads=["rbc"], writes=["rbc"])
        P.dve(lambda e: e.reciprocal(out=rbc[:, :], in_=rbc[:, :]), reads=["rbc"], writes=["rbc"])
        for e8 in range(8):
            P.dve(lambda e, e8=e8: e.tensor_tensor(out=actT[:, 8 + e8, :], in0=actT[:, 8 + e8, :], in1=rbc[:, :], op=ALU.mult),
                  reads=["rbc", ("mixS", e8)], writes=[("mixS", e8)])
        P.barrier()
        if STOP == 8:
            P.emit()
            return nc

        gpm = sb(65536, [128, 2048], F32)
        gpf = sb(65536 + 8192, [128, 2048], F32)
        xt2 = [sb(65536 + 16384 + 8192 * i, [128, 2048], F32) for i in range(2)]
        Abuf = [sb(SCR + 8192 * i, [128, 2048], F32) for i in range(2)]
        hnb = [sb(SCR + 16384 + 4096 * i, [128, 2048], BF16) for i in range(2)]
        ojb = sb(SCR + 24576, [128, 2048], BF16)
        P.dma("sync", gpm[:, :], gvecs[1:2, :].partition_broadcast(128), writes=["gpm"])
        P.dma("sync", gpf[:, :], gvecs[2:3, :].partition_broadcast(128), writes=["gpf"])

        def v4(ap_):
            return ap_.rearrange("p (a b) -> p a b", a=4)

        def wo_mm(tt):
            tsl = slice(tt * 128, (tt + 1) * 128)
            bset = 4 * (tt % 2)

            def mo(e):
                last = None
                for cbk in range(4):
                    for c in range(16):
                        last = e.matmul(ps[:, bset + cbk, :], lhsT=actT[:, c, tsl], rhs=Wo[:, c, cbk * 512:(cbk + 1) * 512],
                                        start=(c == 0), stop=(c == 15))
                return last
            P.pe(mo, reads=[("act", tt)], writes=[("ps", bset + i) for i in range(4)])

        def wo_post(tt):
            tsl = slice(tt * 128, (tt + 1) * 128)
            s_ = tt % 2
            bset = 4 * s_
            c0 = 16 + 8 * s_
            A = Abuf[s_]
            pk = [("ps", bset + i) for i in range(4)]
            acc = ps[:, bset:bset + 4, :]
            P.dma("sync", xt2[s_][:, :], x[tsl, :], writes=[("xt2", s_)])
            P.act(lambda e: e.activation(out=v4(A[:, :]), in_=acc, func=AF.Copy), reads=pk, writes=[("A", s_)])
            P.dve(lambda e: e.scalar_tensor_tensor(out=ojb[:, :], in0=A[:, :], scalar=1.0, in1=A[:, :], op0=ALU.mult,
                                                   op1=ALU.mult, accum_out=stat[:, c0:c0 + 1]),
                  reads=[("A", s_)], writes=["oj", ("st", c0)])
            rstd_from(("st", c0), stat[:, c0:c0 + 1], stat[:, c0 + 2:c0 + 3], ("st", c0 + 2), D, stat[:, c0 + 1:c0 + 2],
                      ("st", c0 + 1))
            P.dve(lambda e: e.scalar_tensor_tensor(out=A[:, :], in0=A[:, :], scalar=stat[:, c0 + 2:c0 + 3], in1=gpm[:, :],
                                                   op0=ALU.mult, op1=ALU.mult), reads=[("A", s_), ("st", c0 + 2), "gpm"],
                  writes=[("A", s_)])
            P.dve(lambda e: e.tensor_tensor(out=A[:, :], in0=A[:, :], in1=xt2[s_][:, :], op=ALU.add),
                  reads=[("A", s_), ("xt2", s_)], writes=[("A", s_)])
            P.dma("sync", hscr[tsl, :], A[:, :], reads=[("A", s_)], writes=[("hscr", tt)])
            P.dve(lambda e: e.scalar_tensor_tensor(out=ojb[:, :], in0=A[:, :], scalar=1.0, in1=A[:, :], op0=ALU.mult,
                                                   op1=ALU.mult, accum_out=stat[:, c0 + 3:c0 + 4]),
                  reads=[("A", s_)], writes=["oj", ("st", c0 + 3)])
            rstd_from(("st", c0 + 3), stat[:, c0 + 3:c0 + 4], stat[:, c0 + 5:c0 + 6], ("st", c0 + 5), D,
                      stat[:, c0 + 4:c0 + 5], ("st", c0 + 4))
            P.dve(lambda e: e.scalar_tensor_tensor(out=hnb[s_][:, :], in0=A[:, :], scalar=stat[:, c0 + 5:c0 + 6], in1=gpf[:, :],
                                                   op0=ALU.mult, op1=ALU.mult), reads=[("A", s_), ("st", c0 + 5), "gpf"],
                  writes=[("hnb", s_)])

            def trh(e):
                last = None
                for c in range(16):
                    last = e.transpose(out=psb[:, bset + c // 8, (c % 8) * 128:(c % 8) * 128 + 128],
                                       in_=hnb[s_][:, c * 128:(c + 1) * 128], identity=identb[:, :])
                return last
            P.pe(trh, reads=[("hnb", s_)], writes=[("ps", bset), ("ps", bset + 1)])
            P.act(lambda e: e.activation(out=actT[:, 0:8, tsl], in_=psb[:, bset, :].rearrange("p (a b) -> p a b", a=8),
                                         func=AF.Copy), reads=[("ps", bset)], writes=[("act", tt)])
            P.dve(lambda e: e.tensor_copy(out=actT[:, 8:16, tsl], in_=psb[:, bset + 1, :].rearrange("p (a b) -> p a b", a=8)),
                  reads=[("ps", bset + 1), ("act", tt)], writes=[("act", tt)])

        wo_mm(0)
        for tt in range(NT):
            if tt + 1 < NT:
                wo_mm(tt + 1)
            wo_post(tt)
        P.barrier()
        if STOP == 9:
            P.emit()
            return nc

        hidT = sb(65536, [128, NFC, 512], BF16)
        ff = sb(110592, [128, 4, 2048], F32)
        wpool = [sb(143360 + 4096 * i, [128, 4, 512], BF16) for i in range(8)]
        ht = sb(176128, [128, 2048], F32)
        gpo = sb(184320, [128, 2048], F32)
        sgf = [sb(192512 + 2048 * i, [128, 512], F32) for i in range(2)]
        fj = sb(196608, [128, 2048], BF16)
        P.dma("sync", gpo[:, :], gvecs[3:4, :].partition_broadcast(128), writes=["gpo"])
        wg_v = w_gate.rearrange("(c p) n -> p c n", p=128)
        wu_v = w_up.rearrange("(c p) n -> p c n", p=128)
        wd_v = w_down.rearrange("(f p) n -> p f n", p=128)
        nld = 0
        pending_epi = []

        def ffn_epi(tb, t4):
            tt = tb * 4 + t4
            fk = [("ff", t4, db) for db in range(4)]
            P.dma("sync", ht[:, :], hscr[tt * 128:(tt + 1) * 128, :], reads=[("hscr", tt)], writes=["ht"])
            P.dve(lambda e: e.scalar_tensor_tensor(out=fj[:, :], in0=ff[:, t4, :], scalar=1.0, in1=ff[:, t4, :],
                                                   op0=ALU.mult, op1=ALU.mult, accum_out=stat[:, 14:15]),
                  reads=fk, writes=["fj", "st14"])
            rstd_from("st14", stat[:, 14:15], stat[:, 3:4], "st3", D, stat[:, 15:16], "st15")
            P.dve(lambda e: e.scalar_tensor_tensor(out=ff[:, t4, :], in0=ff[:, t4, :], scalar=stat[:, 3:4], in1=gpo[:, :],
                                                   op0=ALU.mult, op1=ALU.mult), reads=fk + ["st3", "gpo"], writes=fk)
            P.dve(lambda e: e.tensor_tensor(out=ff[:, t4, :], in0=ff[:, t4, :], in1=ht[:, :], op=ALU.add),
                  reads=fk + ["ht"], writes=fk)
            P.dma("sync", out[tt * 128:(tt + 1) * 128, :], ff[:, t4, :], reads=fk, writes=[("out", tt)])

        for tb in range(4):
            tsl = slice(tb * 512, (tb + 1) * 512)
            for blk in range(11):
                if blk in (1, 3, 5, 7) and pending_epi:
                    ffn_epi(*pending_epi.pop(0))
                for cq in range(4):
                    gb = wpool[nld % 8]
                    gk = ("wp", nld % 8)
                    nld += 1
                    ub = wpool[nld % 8]
                    uk = ("wp", nld % 8)
                    nld += 1
                    P.dma("gpsimd", gb[:, :, :], wg_v[:, 4 * cq:4 * cq + 4, blk * 512:(blk + 1) * 512], writes=[gk])
                    P.dma("gpsimd", ub[:, :, :], wu_v[:, 4 * cq:4 * cq + 4, blk * 512:(blk + 1) * 512], writes=[uk])
                    for fcl in range(4):
                        def mgu(e, gb=gb, ub=ub, fcl=fcl, cq=cq, tsl=tsl):
                            last = None
                            for c4 in range(4):
                                c = 4 * cq + c4
                                e.matmul(ps[:, fcl, :], lhsT=gb[:, c4, fcl * 128:(fcl + 1) * 128], rhs=actT[:, c, tsl],
                                         start=(c == 0), stop=(c == 15))
                                last = e.matmul(ps[:, 4 + fcl, :], lhsT=ub[:, c4, fcl * 128:(fcl + 1) * 128],
                                                rhs=actT[:, c, tsl], start=(c == 0), stop=(c == 15))
                            return last
                        P.pe(mgu, reads=[gk, uk], writes=[("ps", fcl), ("ps", 4 + fcl)])
                        if cq == 3:
                            fc = 4 * blk + fcl
                            b2 = fc % 2
                            P.act(lambda e, fcl=fcl, b2=b2: e.activation(out=sgf[b2][:, :], in_=ps[:, fcl, :], func=AF.Silu),
                                  reads=[("ps", fcl)], writes=[("sgf", b2)])
                            P.dve(lambda e, fcl=fcl, b2=b2, fc=fc: e.tensor_tensor(out=hidT[:, fc, :], in0=sgf[b2][:, :],
                                                                                   in1=ps[:, 4 + fcl, :], op=ALU.mult),
                                  reads=[("sgf", b2), ("ps", 4 + fcl)], writes=[("hid", fc)])
            for db in range(4):
                bs = 4 * (db % 2)
                for fq in range(11):
                    wdb = wpool[nld % 8]
                    wk = ("wp", nld % 8)
                    nld += 1
                    P.dma("gpsimd", wdb[:, :, :], wd_v[:, 4 * fq:4 * fq + 4, db * 512:(db + 1) * 512], writes=[wk])

                    def md(e, wdb=wdb, fq=fq, bs=bs):
                        last = None
                        for f4 in range(4):
                            fc = fq * 4 + f4
                            for t4 in range(4):
                                last = e.matmul(ps[:, bs + t4, :], lhsT=hidT[:, fc, t4 * 128:(t4 + 1) * 128], rhs=wdb[:, f4, :],
                                                start=(fc == 0), stop=(fc == NFC - 1))
                        return last
                    P.pe(md, reads=[wk] + [("hid", fq * 4 + f4) for f4 in range(4)], writes=[("ps", bs + t4) for t4 in range(4)])
                for t4 in range(4):
                    if t4 % 2 == 0:
                        P.act(lambda e, t4=t4, db=db, bs=bs: e.activation(out=ff[:, t4, db * 512:(db + 1) * 512],
                                                                          in_=ps[:, bs + t4, :], func=AF.Copy),
                              reads=[("ps", bs + t4)], writes=[("ff", t4, db)])
                    else:
                        P.dve(lambda e, t4=t4, db=db, bs=bs: e.tensor_copy(out=ff[:, t4, db * 512:(db + 1) * 512],
                                                                           in_=ps[:, bs + t4, :]),
                              reads=[("ps", bs + t4)], writes=[("ff", t4, db)])
            pending_epi = [(tb, t4) for t4 in range(4)]
        for (tb_, t4_) in pending_epi:
            ffn_epi(tb_, t4_)
        P.emit()
        print('sig counts', P.sig_counts, 'dma cum', max(P.dma_cum))
    return nc


def _host_layouts(inp):
    f32 = np.float32
    G, N, Pp = 64, 64, 16
    sm = np.zeros((128, NSM), f32)
    sm[:, C_ID:C_ID + 128] = np.eye(128, dtype=f32)
    kk = np.arange(128)[:, None]
    qq = np.arange(128)[None, :]
    sm[:, C_MASK:C_MASK + 128] = (kk > qq).astype(f32)
    sm[:, C_MASK + 128:C_MASK + 256] = (kk <= qq).astype(f32)
    half = 32
    inv_freq = (np.float32(10000.0) ** (-np.arange(half, dtype=f32) / np.float32(half))).astype(f32)
    sm[:, C_INVF:C_INVF + 32] = inv_freq[None, :]
    sm[:, C_SINK:C_SINK + 16] = inp["sinks"][0][None, :]
    sm[:, C_GSSM:C_GSSM + 8] = inp["g_ssm_out"][0].reshape(8, 128).T
    sm[:, C_BGLU:C_BGLU + 8] = inp["b_glu"][0].reshape(8, 128).T
    sm[:, C_DSK:C_DSK + 8] = inp["d_skip"][0].reshape(8, 8, 16).reshape(8, 128).T
    a_re, a_im, ldt = inp["a_re"][0], inp["a_im"][0], inp["log_dt"][0]
    for b in range(2):
        sm[64 * b:64 * b + 64, C_ARA:C_ARA + 32] = a_re[b::2, :].T
        sm[64 * b:64 * b + 64, C_AIA:C_AIA + 32] = a_im[b::2, :].T
        sm[64 * b:64 * b + 64, C_LDA:C_LDA + 32] = np.broadcast_to(ldt[b::2][None, :], (64, 32))
    lb3 = np.zeros((128, 3, 8, 2, 64), f32)
    for gq in range(8):
        rows = slice(16 * gq, 16 * gq + 16)
        for blk in range(8):
            g = 8 * blk + gq
            lb3[rows, 0, blk, :, :] = a_re[g][None, None, :]
            lb3[rows, 1, blk, :, :] = a_im[g][None, None, :]
            lb3[rows, 2, blk, :, :] = ldt[g]
    lb3 = lb3.reshape(128, 3, 1024)
    bexp = np.zeros((128, 2, 8, 2, 64), f32)
    cexp = np.zeros((128, 2, 32, 8, 16), f32)
    maska = np.zeros((128, 4), f32)
    b_re, b_im, c_re, c_im = inp["b_re"][0], inp["b_im"][0], inp["c_re"][0], inp["c_im"][0]
    for g in range(G):
        blk, gq = divmod(g, 8)
        a, b = divmod(gq, 2)
        rows = slice(16 * gq, 16 * gq + 16)
        bexp[rows, 0, blk, b, :] = b_re[g].T
        bexp[rows, 1, blk, b, :] = b_im[g].T
        maska[rows, a] = 1.0
        pp = g // 2
        cexp[64 * b:64 * b + 64, 0, pp, gq, :] = c_re[g].T
        cexp[64 * b:64 * b + 64, 1, pp, gq, :] = c_im[g].T
    bexp = bexp.reshape(128, 2, 1024)
    cexp = cexp.reshape(128, 2, 4096)
    gv = np.zeros((5, D), f32)
    gv[0] = inp["g_pre_mix"][0]
    gv[1] = inp["g_post_mix"][0]
    gv[2] = inp["g_pre_ffn"][0]
    gv[3] = inp["g_post_ffn"][0]
    gv[4, :1024] = inp["g_attn_out"][0]
    tokc = np.broadcast_to(np.arange(1, 2049, dtype=f32)[None, :], (128, 2048)).copy()
    shared = {
        "smalls": sm, "gvecs": gv, "lb3": lb3, "bexp": bexp, "cexp": cexp, "tokc": tokc, "maska": maska,
        "w_in": np.ascontiguousarray(inp["w_in"][0]), "w_glu": np.ascontiguousarray(inp["w_glu"][0]),
        "w_o": np.ascontiguousarray(inp["w_o"][0]), "w_gate": np.ascontiguousarray(inp["w_gate"][0]),
        "w_up": np.ascontiguousarray(inp["w_up"][0]), "w_down": np.ascontiguousarray(inp["w_down"][0]),
    }
    return shared


def kernel(**inputs):
    inp = {k: np.asarray(v) for k, v in inputs.items()}
    shared = _host_layouts(inp)
    nc = build_nc()
    in_maps = []
    for c in range(8):
        m = dict(shared)
        m["x"] = np.ascontiguousarray(inp["x"][c])
        m["pos"] = np.ascontiguousarray(inp["positions"][c].astype(np.int32).reshape(16, 128).T)
        in_maps.append(m)
    res = run_bass_kernel_spmd(nc, in_maps, core_ids=list(range(8)))
    return np.stack([np.asarray(r["out"], dtype=np.float32) for r in res.results], axis=0)
```

```python
import math
import numpy as np
from contextlib import ExitStack
import concourse.bass as bass
import concourse.mybir as mybir
from concourse.bass_utils import run_bass_kernel_spmd

F32 = mybir.dt.float32
BF16 = mybir.dt.bfloat16
I32 = mybir.dt.int32
AF = mybir.ActivationFunctionType
ALU = mybir.AluOpType
AX = mybir.AxisListType

ENGS = ["tensor", "vector", "scalar", "gpsimd", "sync"]


class Op:
    __slots__ = ("eng", "fn", "deps", "signal", "semval", "dma", "dsem", "dval", "prev_on_sem")

    def __init__(self, eng, fn, dma):
        self.eng = eng
        self.fn = fn
        self.deps = []
        self.signal = False
        self.semval = 0
        self.dma = dma
        self.dsem = None
        self.dval = 0
        self.prev_on_sem = None


class Prog:
    def __init__(self, nc, n_dma_sems=32):
        self.nc = nc
        self.ops = {e: [] for e in ENGS}
        self.res = {}
        self.n_dma_sems = n_dma_sems
        self.dma_rr = 0
        self.dma_last = [None] * n_dma_sems
        self.dma_cum = [0] * n_dma_sems

    def add(self, eng, fn, reads=(), writes=(), dma=False):
        op = Op(eng, fn, dma)
        deps = {}
        for k in reads:
            st = self.res.get(k)
            if st is not None and st[0] is not None:
                deps[id(st[0])] = st[0]
        for k in writes:
            st = self.res.get(k)
            if st is not None:
                if st[0] is not None:
                    deps[id(st[0])] = st[0]
                for r in st[1]:
                    deps[id(r)] = r
        for k in reads:
            st = self.res.get(k)
            if st is None:
                self.res[k] = [None, [op]]
            else:
                st[1].append(op)
        for k in writes:
            self.res[k] = [op, []]
        for d in deps.values():
            if d is op:
                continue
            if (not d.dma) and d.eng == eng and eng == "tensor":
                continue
            op.deps.append(d)
            d.signal = True
        if dma:
            s = self.dma_rr
            self.dma_rr = (self.dma_rr + 1) % self.n_dma_sems
            op.dsem = s
            self.dma_cum[s] += 16
            op.dval = self.dma_cum[s]
            op.prev_on_sem = self.dma_last[s]
            self.dma_last[s] = op
        self.ops[eng].append(op)
        return op

    def pe(self, fn, reads=(), writes=()):
        return self.add("tensor", fn, reads, writes)

    def dve(self, fn, reads=(), writes=()):
        return self.add("vector", fn, reads, writes)

    def act(self, fn, reads=(), writes=()):
        return self.add("scalar", fn, reads, writes)

    def pool(self, fn, reads=(), writes=()):
        return self.add("vector" if POOL_AS_DVE else "gpsimd", fn, reads, writes)

    def dma(self, eng, out, in_, reads=(), writes=(), **kw):
        return self.add(eng, lambda e: e.dma_start(out=out, in_=in_, **kw), reads, writes, dma=True)

    def barrier(self):
        lasts = []
        for e in ENGS:
            for op in reversed(self.ops[e]):
                if (not op.dma) and op.fn is not None:
                    lasts.append(op)
                    break
        dl = [d for d in self.dma_last if d is not None]
        for e in ENGS:
            op = Op(e, None, False)
            for d in lasts:
                if d.eng != e:
                    op.deps.append(d)
                    d.signal = True
            op.deps.extend(dl)
            self.ops[e].append(op)
        self.res = {}

    def emit(self):
        nc = self.nc
        self.barrier()
        for e in ENGS:
            cum = 0
            for op in self.ops[e]:
                if op.dma:
                    continue
                if op.signal:
                    cum += 1
                    op.semval = cum
            self.sig_counts = getattr(self, "sig_counts", {})
            self.sig_counts[e] = (cum, len(self.ops[e]))
        with ExitStack() as st:
            esem = {e: st.enter_context(nc.semaphore("es_" + e)) for e in ENGS}
            dsem = [st.enter_context(nc.semaphore("ds_%d" % i)) for i in range(self.n_dma_sems)]
            block = st.enter_context(nc.Block())

            def run(eng, e):
                waited = {}

                def wait_for(d):
                    if d.dma:
                        key, sem, val = ("d", d.dsem), dsem[d.dsem], d.dval
                    else:
                        key, sem, val = ("e", d.eng), esem[d.eng], d.semval
                    if waited.get(key, 0) < val:
                        eng.wait_ge(sem, val)
                        waited[key] = val

                for op in self.ops[e]:
                    for d in op.deps:
                        wait_for(d)
                    if op.dma and op.prev_on_sem is not None:
                        wait_for(op.prev_on_sem)
                    if op.fn is None:
                        continue
                    inst = op.fn(eng)
                    if op.dma:
                        inst.then_inc(dsem[op.dsem], 16)
                    elif op.signal:
                        inst.then_inc(esem[e], 1)

            for e in ENGS:
                getattr(block, e)(lambda eng, e=e: run(eng, e))


D = 2048
L = 2048
NT = 16
DFF = 5632
NFC = 44
EPS = 1e-6
BASE = 17408
STOP = -1
POOL_AS_DVE = True
ATT_LEVEL = 99
ATT_TILES = 16
INV2PI = 1.0 / (2.0 * math.pi)
TWO_PI = 2.0 * math.pi * (1.0 - 2e-7)
MAGIC = 12582912.0
GELU_C = 2.0 * math.sqrt(2.0 / math.pi)

C_ID = 0
C_MASK = 128
C_INVF = 384
C_SINK = 416
C_GSSM = 432
C_BGLU = 440
C_DSK = 448
C_ARA = 456
C_AIA = 488
C_LDA = 520
NSM = 552


def build_nc():
    nc = bass.Bass("TRN2", target_bir_lowering=False)

    def din(name, shape, dt=F32):
        return nc.dram_tensor(name, list(shape), dt, kind="ExternalInput").ap()

    x = din("x", [L, D])
    pos = din("pos", [128, NT], I32)
    smalls = din("smalls", [128, NSM])
    gvecs = din("gvecs", [5, D])
    w_in = din("w_in", [D, 2560])
    w_glu = din("w_glu", [1024, 1024])
    w_o = din("w_o", [D, D])
    w_gate = din("w_gate", [D, DFF])
    w_up = din("w_up", [D, DFF])
    w_down = din("w_down", [DFF, D])
    lb3 = din("lb3", [128, 3, 1024])
    bexp = din("bexp", [128, 2, 1024])
    maska = din("maska", [128, 4])
    cexp = din("cexp", [128, 2, 4096])
    tokc = din("tokc", [128, 2048])
    out = nc.dram_tensor("out", [L, D], F32, kind="ExternalOutput").ap()
    hscr = nc.dram_tensor("hscr", [L, D], F32, kind="Internal").ap()

    cnt = [0]

    def sb(off, shape, dt):
        cnt[0] += 1
        return nc.alloc_sbuf_tensor_at("t%d" % cnt[0], list(shape), dt, offset=BASE + off)

    def cap(t, off, dims):
        return bass.AP(tensor=t, offset=off, ap=[list(d) for d in dims])

    P = Prog(nc)
    with ExitStack() as st:
        ps = st.enter_context(nc.psum_tensor("ps", [128, 8, 512], F32))
        psb = ps[:, :, :].bitcast(BF16)

        actT = sb(0, [128, 16, 2048], BF16)
        uT = sb(65536, [128, 8, 2048], BF16)
        qT = sb(98304, [128, 8, 2048], BF16)
        kT2 = sb(131072, [128, 4, 2176], BF16)
        v1 = sb(148480, [128, 17, 4, 65], BF16)
        cosT = sb(157696, [128, 16, 32], F32)
        sinT = sb(157696 + 2048, [128, 16, 32], F32)
        nsinT = sb(157696 + 4096, [128, 16, 32], F32)
        SCR = 163840
        CONST = 207872
        sm = sb(CONST, [128, NSM], F32)
        identb = sb(CONST + 2208, [128, 128], BF16)
        maskb = sb(CONST + 2464, [128, 2, 128], BF16)
        stat = sb(CONST + 2976, [128, 64], F32)
        ngs = sb(CONST + 3232, [128, 4], F32)
        onesb = sb(CONST + 3264, [128, 2], BF16)
        rstd_s = sb(CONST + 3296, [128, 16], F32)
        posi = sb(CONST + 3360, [128, 16], I32)
        posf = sb(CONST + 3424, [128, 16], F32)
        thp = sb(CONST + 3488, [128, 32], F32)
        rho = sb(CONST + 3616, [128, 32], F32)
        rlast = sb(CONST + 3744, [128, 32, 2], F32)
        identf = sm[:, C_ID:C_ID + 128]
        magp = sb(CONST + 4000, [128, 4], F32)
        maskA = sb(CONST + 4032, [128, 4], F32)

        P.dma("sync", sm[:, :], smalls, writes=["sm"])
        P.dma("sync", posi[:, :], pos, writes=["posi"])
        P.dma("sync", maskA[:, :], maska, writes=["maskA"])
        P.dve(lambda e: e.tensor_copy(out=identb[:, :], in_=sm[:, C_ID:C_ID + 128]), reads=["sm"], writes=["identb"])
        P.dve(lambda e: e.tensor_copy(out=maskb[:, :, :], in_=sm[:, C_MASK:C_MASK + 256].rearrange("p (a b) -> p a b", a=2)),
              reads=["sm"], writes=["maskb"])
        P.dve(lambda e: e.memset(onesb[:, :], 1.0), writes=["onesb"])
        P.dve(lambda e: e.memset(magp[:, 0:1], MAGIC), writes=["magp0"])
        P.dve(lambda e: e.memset(magp[:, 1:2], -MAGIC), writes=["magp1"])
        P.dve(lambda e: e.memset(magp[:, 2:3], math.pi / 2), writes=["magp2"])
        P.dve(lambda e: e.tensor_reduce(out=ngs[:, :], in_=sm[:, C_SINK:C_SINK + 16].rearrange("p (a b) -> p a b", a=4),
                                        axis=AX.X, op=ALU.max, negate=True), reads=["sm"], writes=["ngs"])
        P.dve(lambda e: e.memset(kT2[:, :, 0:128], 0.0), writes=["kpad"])
        P.dve(lambda e: e.memset(v1[:, 0, :, :], 0.0), writes=["vpad"])
        P.dve(lambda e: e.memset(v1[:, 1:17, :, 64:65], 1.0), writes=["vones"])
        P.dve(lambda e: e.tensor_copy(out=posf[:, :], in_=posi[:, :]), reads=["posi"], writes=["posf"])

        rt = [sb(65536 + 2048 * i, [128, 16, 32], F32) for i in range(4)]
        P.dve(lambda e: e.tensor_tensor(out=rt[0][:, :, :], in0=cap(posf, 0, [[16, 128], [1, 16], [0, 32]]),
                                        in1=cap(sm, C_INVF, [[NSM, 128], [0, 16], [1, 32]]), op=ALU.mult),
              reads=["posf", "sm"], writes=["rt0"])
        P.dve(lambda e: e.tensor_scalar(out=rt[0][:, :, :], in0=rt[0][:, :, :], scalar1=INV2PI, scalar2=None, op0=ALU.mult),
              reads=["rt0"], writes=["rt0"])
        P.dve(lambda e: e.tensor_scalar(out=rt[1][:, :, :], in0=rt[0][:, :, :], scalar1=MAGIC, scalar2=MAGIC, op0=ALU.add,
                                        op1=ALU.subtract), reads=["rt0"], writes=["rt1"])
        P.dve(lambda e: e.tensor_tensor(out=rt[2][:, :, :], in0=rt[0][:, :, :], in1=rt[1][:, :, :], op=ALU.subtract),
              reads=["rt0", "rt1"], writes=["rt2"])
        P.dve(lambda e: e.scalar_tensor_tensor(out=rt[3][:, :, :], in0=rt[2][:, :, :], scalar=-1.0, in1=rt[2][:, :, :],
                                               op0=ALU.mult, op1=ALU.max), reads=["rt2"], writes=["rt3"])
        P.act(lambda e: e.activation(out=sinT[:, :, :], in_=rt[2][:, :, :], func=AF.Sin, scale=TWO_PI), reads=["rt2"], writes=["sinT"])
        P.act(lambda e: e.activation(out=nsinT[:, :, :], in_=rt[2][:, :, :], func=AF.Sin, scale=-TWO_PI), reads=["rt2"], writes=["nsinT"])
        P.act(lambda e: e.activation(out=cosT[:, :, :], in_=rt[3][:, :, :], func=AF.Sin, scale=-TWO_PI, bias=math.pi / 2),
              reads=["rt3"], writes=["cosT"])
        if STOP == 0:
            P.emit()
            return nc

        def rstd_from(ss_key, ss_ap, dst_ap, dst_key, n, tmp_ap, tmp_key):
            P.dve(lambda e: e.tensor_scalar(out=tmp_ap, in0=ss_ap, scalar1=1.0 / n, scalar2=EPS, op0=ALU.mult, op1=ALU.add),
                  reads=[ss_key], writes=[tmp_key])
            P.act(lambda e: e.activation(out=tmp_ap, in_=tmp_ap, func=AF.Ln), reads=[tmp_key], writes=[tmp_key])
            P.act(lambda e: e.activation(out=dst_ap, in_=tmp_ap, func=AF.Exp, scale=-0.5), reads=[tmp_key], writes=[dst_key])

        xt = [sb(SCR + 8192 * i, [128, 2048], F32) for i in range(2)]
        xs = [sb(SCR + 16384 + 4096 * i, [128, 2048], BF16) for i in range(2)]
        gbc = sb(SCR + 24576, [128, 2048], F32)
        junk = sb(SCR + 32768, [128, 2048], BF16)
        P.dma("sync", gbc[:, :], gvecs[0:1, :].partition_broadcast(128), writes=["gbc"])

        def a1_pre(tt):
            b = tt % 2
            c0 = 4 * b
            P.dma("sync", xt[b][:, :], x[tt * 128:(tt + 1) * 128, :], writes=[("xt", b)])
            P.dve(lambda e: e.scalar_tensor_tensor(out=junk[:, :], in0=xt[b][:, :], scalar=1.0, in1=xt[b][:, :],
                                                   op0=ALU.mult, op1=ALU.mult, accum_out=stat[:, c0:c0 + 1]),
                  reads=[("xt", b)], writes=["junk", ("st", c0)])
            rstd_from(("st", c0), stat[:, c0:c0 + 1], stat[:, c0 + 2:c0 + 3], ("st", c0 + 2), D, stat[:, c0 + 1:c0 + 2],
                      ("st", c0 + 1))
            P.dve(lambda e: e.scalar_tensor_tensor(out=xs[b][:, :], in0=xt[b][:, :], scalar=stat[:, c0 + 2:c0 + 3],
                                                   in1=gbc[:, :], op0=ALU.mult, op1=ALU.mult),
                  reads=[("xt", b), ("st", c0 + 2), "gbc"], writes=[("xs", b)])

        def a1_post(tt):
            b = tt % 2
            bk = 2 * b

            def tr(e):
                last = None
                for c in range(16):
                    last = e.transpose(out=psb[:, bk + c // 8, (c % 8) * 128:(c % 8) * 128 + 128],
                                       in_=xs[b][:, c * 128:(c + 1) * 128], identity=identb[:, :])
                return last
            P.pe(tr, reads=[("xs", b), "identb"], writes=[("ps", bk), ("ps", bk + 1)])
            P.act(lambda e: e.activation(out=actT[:, 0:8, tt * 128:(tt + 1) * 128],
                                         in_=psb[:, bk, :].rearrange("p (a b) -> p a b", a=8), func=AF.Copy),
                  reads=[("ps", bk)], writes=[("actT", tt, 0)])
            P.act(lambda e: e.activation(out=actT[:, 8:16, tt * 128:(tt + 1) * 128],
                                         in_=psb[:, bk + 1, :].rearrange("p (a b) -> p a b", a=8), func=AF.Copy),
                  reads=[("ps", bk + 1)], writes=[("actT", tt, 1)])

        a1_pre(0)
        for tt in range(NT):
            if tt + 1 < NT:
                a1_pre(tt + 1)
            a1_post(tt)
        P.barrier()
        if STOP == 1:
            P.emit()
            return nc

        wb = [sb(SCR + 8192 * i, [128, 16, 256], BF16) for i in range(2)]
        rAs = [sb(SCR + 16384 + 1024 * i, [128, 256], F32) for i in range(2)]
        rBs = [sb(SCR + 18432 + 1024 * i, [128, 256], F32) for i in range(2)]
        qrs = [sb(SCR + 20480 + 512 * i, [128, 256], BF16) for i in range(2)]
        kds = [sb(SCR + 21504 + 1024 * i, [128, 4, 2, 64], BF16) for i in range(2)]
        w_in_v = w_in.rearrange("(c p) n -> p c n", p=128)
        jobs = []
        for cb in range(6):
            for tt in range(NT):
                jobs.append((cb, tt, len(jobs) % 4, len(jobs) % 2))
        loaded = set()

        def a2_load(cb):
            if cb in loaded or cb >= 10:
                return
            loaded.add(cb)
            P.dma("gpsimd", wb[cb % 2][:, :, :], w_in_v[:, :, cb * 256:(cb + 1) * 256], writes=[("wb", cb % 2)])

        def a2_M(job):
            cb, tt, bk, par = job
            a2_load(cb)
            wbuf = wb[cb % 2]

            def mm(e):
                last = None
                for c in range(16):
                    last = e.matmul(ps[:, bk, 0:256], lhsT=actT[:, c, tt * 128:(tt + 1) * 128], rhs=wbuf[:, c, :],
                                    start=(c == 0), stop=(c == 15))
                return last
            P.pe(mm, reads=[("wb", cb % 2), ("actT", tt, 0), ("actT", tt, 1)], writes=[("ps", bk)])

        def a2_post(job):
            cb, tt, bk, par = job
            if cb == 5:
                P.act(lambda e: e.activation(out=v1[:, tt + 1, :, 0:64], in_=ps[:, bk, 0:256].rearrange("p (a b) -> p a b", a=4),
                                             func=AF.Copy), reads=[("ps", bk)], writes=[("v1", tt)])
                return
            rA, rB, qr, kd = rAs[par], rBs[par], qrs[par], kds[par]
            pv = ps[:, bk, 0:256].rearrange("p (h t d) -> p h t d", h=4, t=2)
            rBv = rB[:, :].rearrange("p (h t d) -> p h t d", h=4, t=2)
            P.dve(lambda e: e.tensor_tensor(out=rA[:, :].rearrange("p (h t d) -> p h t d", h=4, t=2), in0=pv,
                                            in1=cap(cosT, tt * 32, [[512, 128], [0, 4], [0, 2], [1, 32]]), op=ALU.mult),
                  reads=[("ps", bk), "cosT"], writes=[("rA", par)])
            P.dve(lambda e: e.tensor_tensor(out=rBv[:, :, 0, :], in0=pv[:, :, 1, :],
                                            in1=cap(nsinT, tt * 32, [[512, 128], [0, 4], [1, 32]]), op=ALU.mult),
                  reads=[("ps", bk), "nsinT"], writes=[("rB0", par)])
            P.dve(lambda e: e.tensor_tensor(out=rBv[:, :, 1, :], in0=pv[:, :, 0, :],
                                            in1=cap(sinT, tt * 32, [[512, 128], [0, 4], [1, 32]]), op=ALU.mult),
                  reads=[("ps", bk), "sinT"], writes=[("rB1", par)])
            rk = [("rA", par), ("rB0", par), ("rB1", par)]
            if cb < 4:
                P.dve(lambda e: e.tensor_tensor(out=qr[:, :], in0=rA[:, :], in1=rB[:, :], op=ALU.add), reads=rk,
                      writes=[("qr", par)])
                tb2 = 4 + par

                def trq(e):
                    last = None
                    for j in range(2):
                        last = e.transpose(out=psb[:, tb2, j * 128:(j + 1) * 128], in_=qr[:, j * 128:(j + 1) * 128],
                                           identity=identb[:, :])
                    return last
                P.pe(trq, reads=[("qr", par)], writes=[("ps", tb2)])
                P.act(lambda e: e.activation(out=qT[:, 2 * cb:2 * cb + 2, tt * 128:(tt + 1) * 128],
                                             in_=psb[:, tb2, 0:256].rearrange("p (a b) -> p a b", a=2), func=AF.Copy),
                      reads=[("ps", tb2)], writes=[("qT", cb, tt)])
            else:
                for dup in range(2):
                    P.dve(lambda e, dup=dup: e.tensor_tensor(out=kd[:, :, dup, :], in0=rA[:, :].rearrange("p (h d) -> p h d", h=4),
                                                             in1=rB[:, :].rearrange("p (h d) -> p h d", h=4), op=ALU.add),
                          reads=rk, writes=[("kd", par, dup)])
                tb2 = 6 + par

                def trk(e):
                    last = None
                    for j in range(4):
                        last = e.transpose(out=psb[:, tb2, j * 128:(j + 1) * 128],
                                           in_=kd[:, j, :, :].rearrange("p a b -> p (a b)"), identity=identb[:, :])
                    return last
                P.pe(trk, reads=[("kd", par, 0), ("kd", par, 1)], writes=[("ps", tb2)])
                P.act(lambda e: e.activation(out=kT2[:, :, 128 + tt * 128:128 + (tt + 1) * 128],
                                             in_=psb[:, tb2, 0:512].rearrange("p (a b) -> p a b", a=4), func=AF.Copy),
                      reads=[("ps", tb2)], writes=[("kT2", tt)])

        a2_load(0)
        a2_load(1)
        a2_M(jobs[0])
        for ji in range(len(jobs)):
            if ji + 1 < len(jobs):
                a2_M(jobs[ji + 1])
            a2_post(jobs[ji])
            if jobs[ji][1] == NT - 1:
                a2_load(jobs[ji][0] + 2)
        pc = 0
        for cb in range(6, 10):
            a2_load(cb)
            wbuf = wb[cb % 2]
            for j in range(2):
                uc = (cb - 6) * 2 + j
                for tb in range(4):
                    bk = pc % 4
                    pc += 1

                    def mmu(e, j=j, tb=tb, bk=bk, wbuf=wbuf):
                        last = None
                        for c in range(16):
                            last = e.matmul(ps[:, bk, :], lhsT=wbuf[:, c, j * 128:(j + 1) * 128],
                                            rhs=actT[:, c, tb * 512:(tb + 1) * 512], start=(c == 0), stop=(c == 15))
                        return last
                    P.pe(mmu, reads=[("wb", cb % 2)], writes=[("ps", bk)])
                    if (tb % 2) == 0:
                        P.act(lambda e, uc=uc, tb=tb, bk=bk: e.activation(out=uT[:, uc, tb * 512:(tb + 1) * 512],
                                                                          in_=ps[:, bk, :], func=AF.Copy),
                              reads=[("ps", bk)], writes=[("uT", uc, tb // 2)])
                    else:
                        P.dve(lambda e, uc=uc, tb=tb, bk=bk: e.tensor_copy(out=uT[:, uc, tb * 512:(tb + 1) * 512],
                                                                           in_=ps[:, bk, :]),
                              reads=[("ps", bk)], writes=[("uT", uc, tb // 2)])
            a2_load(cb + 2)
        P.barrier()
        if STOP == 2:
            P.emit()
            return nc

        Pbs = [sb(SCR + 2048 * i, [128, 1024], BF16) for i in range(2)]
        PTs = [sb(SCR + 4096 + 2048 * i, [128, 4, 2, 128], BF16) for i in range(2)]
        attn = [sb(SCR + 8192 + 4096 * i, [128, 1024], F32) for i in range(2)]
        anbs = [sb(SCR + 16384 + 2048 * i, [128, 1024], BF16) for i in range(2)]
        gat = sb(SCR + 20480, [128, 1024], F32)
        ajunk = sb(SCR + 24576, [128, 1024], BF16)
        asts = [sb(SCR + 26624 + 64 * i, [128, 16], F32) for i in range(4)]
        P.dma("sync", gat[:, :], gvecs[4:5, 0:1024].partition_broadcast(128), writes=["gat"])
        aits = [(tt, j) for tt in range(min(NT, ATT_TILES)) for j in range(4)]

        def att_X(i):
            tt, j = aits[i]
            sbk = 2 * (i % 2)
            ast = asts[i % 4]
            Pb = Pbs[i % 2]
            ka = ("ast", i % 4)

            def qk(e):
                last = None
                for hh in range(4):
                    h = 4 * j + hh
                    pb = (h % 2) * 64
                    last = e.matmul(ps[:, sbk + hh % 2, (hh // 2) * 256:(hh // 2) * 256 + 256],
                                    lhsT=qT[pb:pb + 64, h // 2, tt * 128:(tt + 1) * 128],
                                    rhs=kT2[pb:pb + 64, j, tt * 128:tt * 128 + 256], start=True, stop=True)
                return last
            P.pe(qk, reads=[], writes=[("ps", sbk), ("ps", sbk + 1)])

        def att_X2(i):
            tt, j = aits[i]
            sbk = 2 * (i % 2)
            ast = asts[i % 4]
            Pb = Pbs[i % 2]
            ka = ("ast", i % 4)
            sc = ps[:, sbk:sbk + 2, :]
            P.dve(lambda e: e.tensor_reduce(out=ast[:, 0:1], in_=sc, axis=AX.XY, op=ALU.max, negate=True),
                  reads=[("ps", sbk), ("ps", sbk + 1)], writes=[ka])
            P.dve(lambda e: e.tensor_scalar(out=ast[:, 1:2], in0=ast[:, 0:1], scalar1=0.125, scalar2=ngs[:, j:j + 1],
                                            op0=ALU.mult, op1=ALU.min), reads=[ka, "ngs"], writes=[ka])
            P.act(lambda e: e.activation(out=Pb[:, :].rearrange("p (a b) -> p a b", a=2), in_=sc, func=AF.Exp,
                                         bias=ast[:, 1:2], scale=0.125),
                  reads=[("ps", sbk), ("ps", sbk + 1), ka], writes=[("Pb", i % 2)])
            P.act(lambda e: e.activation(out=ast[:, 4:8], in_=sm[:, C_SINK + 4 * j:C_SINK + 4 * j + 4], func=AF.Exp,
                                         bias=ast[:, 1:2], scale=1.0), reads=[ka], writes=[("ase", i % 4)])
            Pv = Pb[:, :].rearrange("p (h b k) -> p h b k", h=4, b=2)
            P.add("gpsimd", lambda e: e.affine_select(out=Pv[:, :, 0, :], in_=Pv[:, :, 0, :], pattern=[[0, 4], [1, 128]],
                                                      compare_op=ALU.is_gt, fill=0.0, base=0, channel_multiplier=-1),
                  reads=[("Pb", i % 2)], writes=[("Pb", i % 2)])
            P.add("gpsimd", lambda e: e.affine_select(out=Pv[:, :, 1, :], in_=Pv[:, :, 1, :], pattern=[[0, 4], [-1, 128]],
                                                      compare_op=ALU.is_ge, fill=0.0, base=0, channel_multiplier=1),
                  reads=[("Pb", i % 2)], writes=[("Pb", i % 2)])

        def att_Y1(i):
            Pb = Pbs[i % 2]
            PT = PTs[i % 2]
            tbk = 4 if i % 2 == 0 else 7

            def trp(e):
                last = None
                for k in range(8):
                    last = e.transpose(out=psb[:, tbk, k * 128:(k + 1) * 128], in_=Pb[:, k * 128:(k + 1) * 128],
                                       identity=identb[:, :])
                return last
            P.pe(trp, reads=[("Pb", i % 2)], writes=[("ps", tbk)])
            if i % 2 == 0:
                P.act(lambda e: e.activation(out=PT[:, :, :, :].rearrange("p h k q -> p (h k q)"), in_=psb[:, tbk, :],
                                             func=AF.Copy), reads=[("ps", tbk)], writes=[("PT", i % 2)])
            else:
                P.dve(lambda e: e.tensor_copy(out=PT[:, :, :, :].rearrange("p h k q -> p (h k q)"), in_=psb[:, tbk, :]),
                      reads=[("ps", tbk)], writes=[("PT", i % 2)])

        def att_Y2(i):
            tt, j = aits[i]
            ab = tt % 2
            PT = PTs[i % 2]
            ast = asts[i % 4]
            obk = 5 if i % 2 == 0 else 6
            tbk = 4 if i % 2 == 0 else 7

            def pv_(e):
                last = None
                for i4 in range(4):
                    hh = (0, 2, 1, 3)[i4]
                    for kb in range(2):
                        last = e.matmul(ps[:, obk, hh * 65:hh * 65 + 65], lhsT=PT[:, i4, kb, :], rhs=v1[:, tt + kb, j, :],
                                        start=(kb == 0), stop=(kb == 1))
                return last
            P.pe(pv_, reads=[("PT", i % 2)], writes=[("ps", obk)])
            po = ps[:, obk, 0:260].rearrange("p (h d) -> p h d", h=4)
            P.dve(lambda e: e.tensor_tensor(out=ast[:, 8:12], in0=po[:, :, 64], in1=ast[:, 4:8], op=ALU.add),
                  reads=[("ps", obk), ("ase", i % 4)], writes=[("aden", i % 4)])
            P.dve(lambda e: e.reciprocal(out=ast[:, 12:16], in_=ast[:, 8:12]), reads=[("aden", i % 4)], writes=[("ard", i % 4)])
            P.dve(lambda e: e.tensor_tensor(out=attn[ab][:, j * 256:(j + 1) * 256].rearrange("p (h d) -> p h d", h=4),
                                            in0=po[:, :, 0:64], in1=cap(ast, 12, [[16, 128], [1, 4], [0, 64]]), op=ALU.mult),
                  reads=[("ps", obk), ("ard", i % 4)], writes=[("attn", ab, j)])
            if j != 3:
                return
            ak = [("attn", ab, jj) for jj in range(4)]
            c0 = 32 + 8 * ab
            anb = anbs[ab]
            P.dve(lambda e: e.scalar_tensor_tensor(out=ajunk[:, :], in0=attn[ab][:, :], scalar=1.0, in1=attn[ab][:, :],
                                                   op0=ALU.mult, op1=ALU.mult, accum_out=stat[:, c0:c0 + 1]),
                  reads=ak, writes=["ajunk", ("st", c0)])
            rstd_from(("st", c0), stat[:, c0:c0 + 1], stat[:, c0 + 2:c0 + 3], ("st", c0 + 2), 1024, stat[:, c0 + 1:c0 + 2],
                      ("st", c0 + 1))
            P.dve(lambda e: e.scalar_tensor_tensor(out=anb[:, :], in0=attn[ab][:, :], scalar=stat[:, c0 + 2:c0 + 3],
                                                   in1=gat[:, :], op0=ALU.mult, op1=ALU.mult),
                  reads=ak + [("st", c0 + 2), "gat"], writes=[("anb", ab)])

            def tra(e):
                last = None
                for c in range(8):
                    last = e.transpose(out=psb[:, tbk, c * 128:(c + 1) * 128], in_=anb[:, c * 128:(c + 1) * 128],
                                       identity=identb[:, :])
                return last
            P.pe(tra, reads=[("anb", ab)], writes=[("ps", tbk)])
            P.act(lambda e: e.activation(out=actT[:, 0:8, tt * 128:(tt + 1) * 128],
                                         in_=psb[:, tbk, :].rearrange("p (a b) -> p a b", a=8), func=AF.Copy),
                  reads=[("ps", tbk)], writes=[("mixT", tt)])

        na = len(aits)
        att_X(0)
        att_X(1)
        att_X2(0)
        att_X(2)
        att_X2(1)
        att_Y1(0)
        for i in range(na):
            if i + 3 < na:
                att_X(i + 3)
            if i + 2 < na:
                att_X2(i + 2)
            if i + 1 < na:
                att_Y1(i + 1)
            att_Y2(i)
        P.barrier()
        if STOP == 3:
            P.emit()
            return nc

        R0 = 98304
        BbTr = sb(R0, [128, 8, 4, 128], BF16)
        BbTi = sb(R0 + 8192, [128, 8, 4, 128], BF16)
        CTr = sb(R0 + 16384, [128, 32, 128], BF16)
        CTi = sb(R0 + 24576, [128, 32, 128], BF16)
        tok = sb(R0 + 32768, [128, 2048], F32)
        fre = sb(49152, [128, 1024], F32)
        fim = sb(53248, [128, 1024], F32)
        lrB = sb(57344, [128, 1024], F32)
        liB = sb(61440, [128, 1024], F32)
        LBr = sb(R0 + 40960, [128, 8, 4, 128], BF16)
        LBi = sb(R0 + 49152, [128, 8, 4, 128], BF16)
        K1blk = sb(R0 + 57344, [128, 8, 128], BF16)
        CIr = sb(32768, [128, 32, 128], BF16)
        CIi = sb(40960, [128, 32, 128], BF16)
        S_ = [sb(SCR + 4096 * i, [128, 1024], F32) for i in range(10)]
        P.dma("sync", tok[:, :], tokc, writes=["tok"])
        P.act(lambda e: e.activation(out=stat[:, 32:64], in_=sm[:, C_LDA:C_LDA + 32], func=AF.Exp), reads=[], writes=["dtA"])
        P.dve(lambda e: e.tensor_tensor(out=rho[:, :], in0=sm[:, C_ARA:C_ARA + 32], in1=stat[:, 32:64], op=ALU.mult),
              reads=["dtA"], writes=["rho0"])
        P.act(lambda e: e.activation(out=rho[:, :], in_=rho[:, :], func=AF.Exp), reads=["rho0"], writes=["rho"])
        P.dve(lambda e: e.scalar_tensor_tensor(out=thp[:, :], in0=sm[:, C_AIA:C_AIA + 32], scalar=INV2PI, in1=stat[:, 32:64],
                                               op0=ALU.mult, op1=ALU.mult), reads=["dtA"], writes=["thp"])
        AR, AI, LD = S_[0], S_[1], S_[2]
        for i, t_ in enumerate((AR, AI, LD)):
            P.dma("sync", t_[:, :], lb3[:, i, :], writes=[("S", i)])
        P.act(lambda e: e.activation(out=LD[:, :], in_=LD[:, :], func=AF.Exp), reads=[("S", 2)], writes=[("S", 2)])
        P.dve(lambda e: e.tensor_tensor(out=S_[3][:, :], in0=AR[:, :], in1=LD[:, :], op=ALU.mult), reads=[("S", 0), ("S", 2)],
              writes=[("S", 3)])
        P.act(lambda e: e.activation(out=S_[3][:, :], in_=S_[3][:, :], func=AF.Exp), reads=[("S", 3)], writes=[("S", 3)])
        P.dve(lambda e: e.scalar_tensor_tensor(out=S_[4][:, :], in0=AI[:, :], scalar=INV2PI, in1=LD[:, :], op0=ALU.mult,
                                               op1=ALU.mult), reads=[("S", 1), ("S", 2)], writes=[("S", 4)])
        P.dve(lambda e: e.tensor_scalar(out=S_[5][:, :], in0=S_[4][:, :], scalar1=MAGIC, scalar2=MAGIC, op0=ALU.add,
                                        op1=ALU.subtract), reads=[("S", 4)], writes=[("S", 5)])
        P.dve(lambda e: e.tensor_tensor(out=S_[4][:, :], in0=S_[4][:, :], in1=S_[5][:, :], op=ALU.subtract),
              reads=[("S", 4), ("S", 5)], writes=[("S", 4)])
        P.dve(lambda e: e.scalar_tensor_tensor(out=S_[5][:, :], in0=S_[4][:, :], scalar=-1.0, in1=S_[4][:, :], op0=ALU.mult,
                                               op1=ALU.max), reads=[("S", 4)], writes=[("S", 5)])
        P.act(lambda e: e.activation(out=S_[6][:, :], in_=S_[4][:, :], func=AF.Sin, scale=TWO_PI), reads=[("S", 4)],
              writes=[("S", 6)])
        P.act(lambda e: e.activation(out=S_[7][:, :], in_=S_[5][:, :], func=AF.Sin, scale=-TWO_PI, bias=math.pi / 2),
              reads=[("S", 5)], writes=[("S", 7)])
        P.dve(lambda e: e.tensor_tensor(out=S_[6][:, :], in0=S_[6][:, :], in1=S_[3][:, :], op=ALU.mult),
              reads=[("S", 6), ("S", 3)], writes=[("S", 6)])
        P.dve(lambda e: e.tensor_tensor(out=S_[7][:, :], in0=S_[7][:, :], in1=S_[3][:, :], op=ALU.mult),
              reads=[("S", 7), ("S", 3)], writes=[("S", 7)])
        P.act(lambda e: e.activation(out=lrB[:, :], in_=S_[7][:, :], func=AF.Copy), reads=[("S", 7)], writes=["lrB"])
        P.act(lambda e: e.activation(out=liB[:, :], in_=S_[6][:, :], func=AF.Copy), reads=[("S", 6)], writes=["liB"])
        P.dve(lambda e: e.tensor_scalar(out=S_[7][:, :], in0=S_[7][:, :], scalar1=-1.0, scalar2=None, op0=ALU.add),
              reads=[("S", 7), "lrB"], writes=[("S", 7)])
        P.dve(lambda e: e.tensor_tensor(out=S_[3][:, :], in0=AR[:, :], in1=AR[:, :], op=ALU.mult), reads=[("S", 0)],
              writes=[("S", 3)])
        P.dve(lambda e: e.tensor_tensor(out=S_[4][:, :], in0=AI[:, :], in1=AI[:, :], op=ALU.mult), reads=[("S", 1)],
              writes=[("S", 4)])
        P.dve(lambda e: e.tensor_tensor(out=S_[3][:, :], in0=S_[3][:, :], in1=S_[4][:, :], op=ALU.add),
              reads=[("S", 3), ("S", 4)], writes=[("S", 3)])
        P.dve(lambda e: e.reciprocal(out=S_[3][:, :], in_=S_[3][:, :]), reads=[("S", 3)], writes=[("S", 3)])
        P.dve(lambda e: e.tensor_tensor(out=S_[4][:, :], in0=S_[7][:, :], in1=AR[:, :], op=ALU.mult),
              reads=[("S", 7), ("S", 0)], writes=[("S", 4)])
        P.dve(lambda e: e.tensor_tensor(out=S_[5][:, :], in0=S_[6][:, :], in1=AI[:, :], op=ALU.mult),
              reads=[("S", 6), ("S", 1)], writes=[("S", 5)])
        P.dve(lambda e: e.tensor_tensor(out=S_[4][:, :], in0=S_[4][:, :], in1=S_[5][:, :], op=ALU.add),
              reads=[("S", 4), ("S", 5)], writes=[("S", 4)])
        P.dve(lambda e: e.tensor_tensor(out=fre[:, :], in0=S_[4][:, :], in1=S_[3][:, :], op=ALU.mult),
              reads=[("S", 4), ("S", 3)], writes=["fre"])
        P.dve(lambda e: e.tensor_tensor(out=S_[4][:, :], in0=S_[6][:, :], in1=AR[:, :], op=ALU.mult),
              reads=[("S", 6), ("S", 0)], writes=[("S", 4)])
        P.dve(lambda e: e.tensor_tensor(out=S_[5][:, :], in0=S_[7][:, :], in1=AI[:, :], op=ALU.mult),
              reads=[("S", 7), ("S", 1)], writes=[("S", 5)])
        P.dve(lambda e: e.tensor_tensor(out=S_[4][:, :], in0=S_[4][:, :], in1=S_[5][:, :], op=ALU.subtract),
              reads=[("S", 4), ("S", 5)], writes=[("S", 4)])
        P.dve(lambda e: e.tensor_tensor(out=fim[:, :], in0=S_[4][:, :], in1=S_[3][:, :], op=ALU.mult),
              reads=[("S", 4), ("S", 3)], writes=["fim"])
        P.barrier()
        if STOP == 4:
            P.emit()
            return nc
        Bq = [sb(SCR + 4096 * i, [128, 1024], F32) for i in range(8)]
        Bcr, Bci, T1, T2, Bbr_, Bbi_, Lr_, Li_ = Bq
        P.dma("sync", Bcr[:, :], bexp[:, 0, :], writes=["Bcr"])
        P.dma("sync", Bci[:, :], bexp[:, 1, :], writes=["Bci"])

        def cmul(outr, outi, ar, ai, br, bi, kr, ki):
            P.dve(lambda e: e.tensor_tensor(out=T1[:, :], in0=ar[:, :], in1=br[:, :], op=ALU.mult), reads=kr, writes=["T1"])
            P.dve(lambda e: e.tensor_tensor(out=T2[:, :], in0=ai[:, :], in1=bi[:, :], op=ALU.mult), reads=kr, writes=["T2"])
            P.dve(lambda e: e.tensor_tensor(out=outr[:, :], in0=T1[:, :], in1=T2[:, :], op=ALU.subtract), reads=["T1", "T2"],
                  writes=[ki + "r"])
            P.dve(lambda e: e.tensor_tensor(out=T1[:, :], in0=ar[:, :], in1=bi[:, :], op=ALU.mult), reads=kr + [ki + "r"],
                  writes=["T1"])
            P.dve(lambda e: e.tensor_tensor(out=T2[:, :], in0=ai[:, :], in1=br[:, :], op=ALU.mult), reads=kr + [ki + "r"],
                  writes=["T2"])
            P.dve(lambda e: e.tensor_tensor(out=outi[:, :], in0=T1[:, :], in1=T2[:, :], op=ALU.add), reads=["T1", "T2"],
                  writes=[ki + "i"])
        cmul(Bbr_, Bbi_, fre, fim, Bcr, Bci, ["Bcr", "Bci", "fre", "fim"], "Bb")
        cmul(Lr_, Li_, lrB, liB, Bbr_, Bbi_, ["Bbr", "Bbi", "lrB", "liB"], "L")
        for src, dst, k in ((Bbr_, BbTr, "Bbr"), (Bbi_, BbTi, "Bbi"), (Lr_, LBr, "Lr"), (Li_, LBi, "Li")):
            for a in range(4):
                P.dve(lambda e, src=src, dst=dst, a=a: e.tensor_scalar(
                    out=dst[:, :, a, :], in0=src[:, :].rearrange("p (k n) -> p k n", k=8), scalar1=maskA[:, a:a + 1],
                    scalar2=None, op0=ALU.mult), reads=[k, "maskA"], writes=[("exp", k, a)])
        P.barrier()
        if STOP == 5:
            P.emit()
            return nc
        cA = sb(SCR + 40960, [128, 32], F32)
        sA = sb(SCR + 40960 + 128, [128, 32], F32)
        tA = sb(SCR + 40960 + 256, [128, 32], F32)
        uA = sb(SCR + 40960 + 384, [128, 32], F32)
        rho2 = stat[:, 32:64]
        P.dve(lambda e: e.tensor_scalar(out=tA[:, :], in0=thp[:, :], scalar1=MAGIC, scalar2=MAGIC, op0=ALU.add,
                                        op1=ALU.subtract), reads=[], writes=["tA"])
        P.dve(lambda e: e.tensor_tensor(out=tA[:, :], in0=thp[:, :], in1=tA[:, :], op=ALU.subtract), reads=["tA"],
              writes=["tA"])
        P.dve(lambda e: e.scalar_tensor_tensor(out=uA[:, :], in0=tA[:, :], scalar=-1.0, in1=tA[:, :], op0=ALU.mult,
                                               op1=ALU.max), reads=["tA"], writes=["uA"])
        P.act(lambda e: e.activation(out=sA[:, :], in_=tA[:, :], func=AF.Sin, scale=TWO_PI), reads=["tA"], writes=["sA"])
        P.act(lambda e: e.activation(out=cA[:, :], in_=uA[:, :], func=AF.Sin, scale=-TWO_PI, bias=magp[:, 2:3]),
              reads=["uA"], writes=["cA"])
        P.dve(lambda e: e.reciprocal(out=uA[:, :], in_=rho[:, :]), reads=["cA"], writes=["uA"])
        P.dve(lambda e: e.tensor_tensor(out=cA[:, :], in0=cA[:, :], in1=uA[:, :], op=ALU.mult), reads=["cA", "uA"],
              writes=["cA"])
        P.dve(lambda e: e.tensor_tensor(out=sA[:, :], in0=sA[:, :], in1=uA[:, :], op=ALU.mult), reads=["sA", "uA"],
              writes=["sA"])
        P.dve(lambda e: e.tensor_tensor(out=rho2, in0=rho[:, :], in1=rho[:, :], op=ALU.mult), reads=[], writes=["rho2"])
        Cre = sb(SCR, [128, 16, 128], F32)
        Cim = sb(SCR + 8192, [128, 16, 128], F32)
        U1 = sb(SCR + 16384, [128, 16, 128], F32)
        U2 = sb(SCR + 24576, [128, 16, 128], F32)
        for hp in range(2):
            psl = slice(16 * hp, 16 * hp + 16)
            P.dma("sync", Cre[:, :, :], cexp[:, 0, hp * 2048:(hp + 1) * 2048].rearrange("p (a b) -> p a b", a=16),
                  writes=["Cre"])
            P.dma("sync", Cim[:, :, :], cexp[:, 1, hp * 2048:(hp + 1) * 2048].rearrange("p (a b) -> p a b", a=16),
                  writes=["Cim"])
            P.act(lambda e, psl=psl: e.activation(out=CTr[:, psl, :], in_=Cre[:, :, :], func=AF.Copy), reads=["Cre"],
                  writes=["CTr"])
            P.act(lambda e, psl=psl: e.activation(out=CTi[:, psl, :], in_=Cim[:, :, :], func=AF.Copy, scale=-1.0),
                  reads=["Cim"], writes=["CTi"])
            cAb = cap(cA, 16 * hp, [[32, 128], [1, 16], [0, 128]])
            sAb = cap(sA, 16 * hp, [[32, 128], [1, 16], [0, 128]])
            P.dve(lambda e, cAb=cAb: e.tensor_tensor(out=U1[:, :, :], in0=Cre[:, :, :], in1=cAb, op=ALU.mult),
                  reads=["Cre", "cA"], writes=["U1"])
            P.dve(lambda e, sAb=sAb: e.tensor_tensor(out=U2[:, :, :], in0=Cim[:, :, :], in1=sAb, op=ALU.mult),
                  reads=["Cim", "sA"], writes=["U2"])
            P.dve(lambda e, psl=psl: e.tensor_tensor(out=CIr[:, psl, :], in0=U1[:, :, :], in1=U2[:, :, :], op=ALU.add),
                  reads=["U1", "U2"], writes=["CIr"])
            P.dve(lambda e, sAb=sAb: e.tensor_tensor(out=U1[:, :, :], in0=Cre[:, :, :], in1=sAb, op=ALU.mult),
                  reads=["Cre", "sA", "CIr"], writes=["U1"])
            P.dve(lambda e, cAb=cAb: e.tensor_tensor(out=U2[:, :, :], in0=Cim[:, :, :], in1=cAb, op=ALU.mult),
                  reads=["Cim", "cA", "CIr"], writes=["U2"])
            P.dve(lambda e, psl=psl: e.tensor_tensor(out=CIi[:, psl, :], in0=U1[:, :, :], in1=U2[:, :, :], op=ALU.subtract),
                  reads=["U1", "U2"], writes=["CIi"])
        Xs = [sb(SCR + 32768 + 2048 * i, [128, 8, 128], BF16) for i in range(2)]
        for blk in range(8):
            xb = Xs[blk % 2]
            tbk = 2 * (blk % 2)

            def trx(e, blk=blk, tbk=tbk):
                last = None
                for a in range(4):
                    for ri, Bt in enumerate((BbTr, BbTi)):
                        k = 2 * a + ri
                        last = e.transpose(out=psb[:, tbk, k * 128:(k + 1) * 128], in_=Bt[:, blk, a, :], identity=identb[:, :])
                return last
            P.pe(trx, reads=["BbTr", "BbTi"], writes=[("ps", tbk)])
            P.act(lambda e, xb=xb, tbk=tbk: e.activation(out=xb[:, :, :], in_=psb[:, tbk, :].rearrange("p (a b) -> p a b", a=8),
                                                         func=AF.Copy), reads=[("ps", tbk)], writes=[("Xs", blk % 2)])

            def mk1(e, blk=blk, xb=xb, tbk=tbk):
                last = None
                for a in range(4):
                    pp = 4 * blk + a
                    for ri, Ct in enumerate((CIr, CIi)):
                        k = 2 * a + ri
                        last = e.matmul(ps[:, tbk + 1, 0:128], lhsT=xb[:, k, :], rhs=Ct[:, pp, :], start=(k == 0), stop=(k == 7))
                return last
            P.pe(mk1, reads=[("Xs", blk % 2), "CIr", "CIi"], writes=[("ps", tbk + 1)])
            P.act(lambda e, blk=blk, tbk=tbk: e.activation(out=K1blk[:, blk, :], in_=ps[:, tbk + 1, 0:128], func=AF.Copy,
                                                           scale=-1.0), reads=[("ps", tbk + 1)], writes=["K1blk"])
        P.barrier()
        if STOP == 6:
            P.emit()
            return nc

        NCH = 512

        def sl2(off, n, dt):
            return [sb(SCR + off + n * i, [128, NCH], dt) for i in range(2)]
        yqs = sl2(0, 2048, F32)
        kfqs = sl2(4096, 2048, F32)
        SINfs = sl2(8192, 2048, F32)
        COSfs = sl2(12288, 2048, F32)
        tb16 = [[sb(SCR + 16384 + 1024 * (3 * s_ + k), [128, NCH], BF16) for k in range(3)] for s_ in range(2)]
        pbuf = [[sb(SCR + 22528 + 1024 * (4 * s_ + k), [128, NCH], BF16) for k in range(4)] for s_ in range(2)]
        Rre = sb(SCR + 30720, [128, NCH], BF16)
        Rim = sb(SCR + 31744, [128, NCH], BF16)
        qbufs = [[sb(SCR + 32768 + 1024 * (4 * s_ + k), [128, NCH], BF16) for k in range(4)] for s_ in range(2)]
        ysb = sb(49152, [128, 1024], F32)
        gtmp = sb(53248, [128, 1024], F32)
        gsig = [sb(57344 + 2048 * i, [128, 1024], BF16) for i in range(2)]

        iters = [(blk, half, a) for blk in range(8) for half in range(2) for a in range(4)]

        def eo(ap_, which):
            return ap_.rearrange("p (c t) -> p c t", t=2)[:, :, which]

        def stageA0(idx):
            blk, half, a = iters[idx]
            s_ = idx % 2
            pp = 4 * blk + a
            yq, kfq = yqs[s_], kfqs[s_]
            ky, kk = ("yq", s_), ("kfq", s_)
            tokv = eo(tok[:, half * 1024:(half + 1) * 1024], 1)
            P.act(lambda e: e.activation(out=yq[:, :], in_=tokv, func=AF.Copy, scale=thp[:, pp:pp + 1]),
                  reads=["tok", "thp"], writes=[ky])
            P.act(lambda e: e.activation(out=kfq[:, :], in_=yq[:, :], func=AF.Identity, bias=magp[:, 0:1], scale=1.0),
                  reads=[ky, "magp"], writes=[kk])
            P.act(lambda e: e.activation(out=kfq[:, :], in_=kfq[:, :], func=AF.Identity, bias=magp[:, 1:2], scale=1.0),
                  reads=[kk, "magp"], writes=[kk])
            P.add("gpsimd", lambda e: e.tensor_tensor(out=yq[:, :], in0=yq[:, :], in1=kfq[:, :], op=ALU.subtract),
                  reads=[ky, kk], writes=[ky])

        def stageA(idx):
            blk, half, a = iters[idx]
            s_ = idx % 2
            pp = 4 * blk + a
            ukey = ("uT", blk, half)
            SINb, NSINb, COSb = tb16[s_]
            pb = pbuf[s_]
            yq, kfq, SINf, COSf = yqs[s_], kfqs[s_], SINfs[s_], COSfs[s_]
            ky, kk, ksf, kcf = ("yq", s_), ("kfq", s_), ("SINf", s_), ("COSf", s_)
            b0 = 2 * s_
            ue = eo(uT[:, blk, half * 1024:(half + 1) * 1024], 0)
            uo = eo(uT[:, blk, half * 1024:(half + 1) * 1024], 1)

            def bu(e):
                last = None
                for ri, (Lt, Bt) in enumerate(((LBr, BbTr), (LBi, BbTi))):
                    e.matmul(ps[:, b0 + ri, :], lhsT=Lt[:, blk, a, :], rhs=ue, start=True, stop=False)
                    last = e.matmul(ps[:, b0 + ri, :], lhsT=Bt[:, blk, a, :], rhs=uo, start=False, stop=True)
                return last
            P.pe(bu, reads=[ukey], writes=[("ps", b0), ("ps", b0 + 1)])
            P.act(lambda e: e.activation(out=kfq[:, :], in_=yq[:, :], func=AF.Abs), reads=[ky], writes=[kk])
            P.act(lambda e: e.activation(out=SINf[:, :], in_=yq[:, :], func=AF.Sin, scale=TWO_PI), reads=[ky], writes=[ksf])
            P.act(lambda e: e.activation(out=COSf[:, :], in_=kfq[:, :], func=AF.Sin, scale=-TWO_PI, bias=magp[:, 2:3]),
                  reads=[kk, "magp"], writes=[kcf])
            P.act(lambda e: e.activation(out=SINb[:, :], in_=yq[:, :], func=AF.Sin, scale=TWO_PI), reads=[ky],
                  writes=[("SINb", s_)])
            P.act(lambda e: e.activation(out=NSINb[:, :], in_=yq[:, :], func=AF.Sin, scale=-TWO_PI), reads=[ky],
                  writes=[("NSINb", s_)])
            P.act(lambda e: e.activation(out=COSb[:, :], in_=kfq[:, :], func=AF.Sin, scale=-TWO_PI, bias=magp[:, 2:3]),
                  reads=[kk, "magp"], writes=[("COSb", s_)])
            bre = ps[:, b0, :]
            bim = ps[:, b0 + 1, :]
            P.dve(lambda e: e.tensor_tensor(out=pb[0][:, :], in0=bre, in1=COSf[:, :], op=ALU.mult),
                  reads=[("ps", b0), kcf], writes=[("p", s_, 0)])
            P.dve(lambda e: e.tensor_tensor(out=pb[1][:, :], in0=bim, in1=SINf[:, :], op=ALU.mult),
                  reads=[("ps", b0 + 1), ksf], writes=[("p", s_, 1)])
            P.dve(lambda e: e.tensor_tensor(out=pb[2][:, :], in0=bim, in1=COSf[:, :], op=ALU.mult),
                  reads=[("ps", b0 + 1), kcf], writes=[("p", s_, 2)])
            P.dve(lambda e: e.scalar_tensor_tensor(out=pb[3][:, :], in0=bre, scalar=-1.0, in1=SINf[:, :], op0=ALU.mult,
                                                   op1=ALU.mult), reads=[("ps", b0), ksf], writes=[("p", s_, 3)])

        def stageB(idx):
            blk, half, a = iters[idx]
            s_ = idx % 2
            pp = 4 * blk + a
            SINb, NSINb, COSb = tb16[s_]
            pb = pbuf[s_]
            qbuf = qbufs[s_]
            rb = cap(stat, 32 + pp, [[64, 128], [0, NCH]])
            i0 = rlast[:, pp, 0:1] if half == 1 else 0.0
            i1 = rlast[:, pp, 1:2] if half == 1 else 0.0

            def addE(k0, bank):
                def f(e):
                    e.matmul(ps[:, bank, :], lhsT=identb[:, :], rhs=pb[k0][:, :], start=True, stop=False)
                    return e.matmul(ps[:, bank, :], lhsT=identb[:, :], rhs=pb[k0 + 1][:, :], start=False, stop=True)
                return f
            P.pe(addE(0, 6), reads=[("p", s_, 0), ("p", s_, 1)], writes=[("ps", 6)])
            P.pe(addE(2, 7), reads=[("p", s_, 2), ("p", s_, 3)], writes=[("ps", 7)])
            P.dve(lambda e: e.tensor_tensor_scan(out=Rre[:, :], data0=rb, data1=ps[:, 6, :], initial=i0, op0=ALU.mult,
                                                 op1=ALU.add), reads=[("ps", 6), "rho2", ("rl", pp)], writes=["Rre"])
            P.dve(lambda e: e.tensor_tensor(out=qbuf[0][:, :], in0=Rre[:, :], in1=COSb[:, :], op=ALU.mult),
                  reads=["Rre", ("COSb", s_)], writes=[("q", s_, 0)])
            P.dve(lambda e: e.tensor_tensor(out=qbuf[3][:, :], in0=Rre[:, :], in1=SINb[:, :], op=ALU.mult),
                  reads=["Rre", ("SINb", s_)], writes=[("q", s_, 3)])
            P.dve(lambda e: e.tensor_tensor_scan(out=Rim[:, :], data0=rb, data1=ps[:, 7, :], initial=i1, op0=ALU.mult,
                                                 op1=ALU.add), reads=[("ps", 7), "rho2", ("rl", pp)], writes=["Rim"])
            P.dve(lambda e: e.tensor_tensor(out=qbuf[1][:, :], in0=Rim[:, :], in1=NSINb[:, :], op=ALU.mult),
                  reads=["Rim", ("NSINb", s_)], writes=[("q", s_, 1)])
            P.dve(lambda e: e.tensor_tensor(out=qbuf[2][:, :], in0=Rim[:, :], in1=COSb[:, :], op=ALU.mult),
                  reads=["Rim", ("COSb", s_)], writes=[("q", s_, 2)])
            if half == 0:
                P.dve(lambda e: e.tensor_copy(out=rlast[:, pp, 0:1], in_=Rre[:, NCH - 1:NCH]), reads=["Rre"],
                      writes=[("rl", pp)])
                P.dve(lambda e: e.tensor_copy(out=rlast[:, pp, 1:2], in_=Rim[:, NCH - 1:NCH]), reads=["Rim", ("rl", pp)],
                      writes=[("rl", pp)])

        def stageC(idx):
            blk, half, a = iters[idx]
            s_ = idx % 2
            pp = 4 * blk + a
            ukey = ("uT", blk, half)
            usl = uT[:, blk, half * 1024:(half + 1) * 1024]
            qbuf = qbufs[s_]

            def cp(e):
                if a == 0:
                    e.matmul(ps[:, 5, :], lhsT=K1blk[:, blk, :], rhs=eo(usl, 1), start=True, stop=False)
                e.matmul(ps[:, 4, :], lhsT=CTr[:, pp, :], rhs=qbuf[0][:, :], start=(a == 0), stop=False)
                e.matmul(ps[:, 4, :], lhsT=CTr[:, pp, :], rhs=qbuf[1][:, :], start=False, stop=False)
                e.matmul(ps[:, 4, :], lhsT=CTi[:, pp, :], rhs=qbuf[2][:, :], start=False, stop=False)
                e.matmul(ps[:, 4, :], lhsT=CTi[:, pp, :], rhs=qbuf[3][:, :], start=False, stop=(a == 3))
                e.matmul(ps[:, 5, :], lhsT=CIr[:, pp, :], rhs=qbuf[0][:, :], start=False, stop=False)
                e.matmul(ps[:, 5, :], lhsT=CIr[:, pp, :], rhs=qbuf[1][:, :], start=False, stop=False)
                e.matmul(ps[:, 5, :], lhsT=CIi[:, pp, :], rhs=qbuf[2][:, :], start=False, stop=False)
                return e.matmul(ps[:, 5, :], lhsT=CIi[:, pp, :], rhs=qbuf[3][:, :], start=False, stop=(a == 3))
            P.pe(cp, reads=[("q", s_, k) for k in range(4)] + [ukey], writes=[("ps", 4), ("ps", 5)])
            if a != 3:
                return
            for which, bank in ((1, 4), (0, 5)):
                P.dve(lambda e, which=which, bank=bank: e.scalar_tensor_tensor(
                    out=eo(usl, which), in0=eo(usl, which), scalar=sm[:, C_DSK + blk:C_DSK + blk + 1], in1=ps[:, bank, :],
                    op0=ALU.mult, op1=ALU.add), reads=[ukey, ("ps", bank)], writes=[ukey])

        stageA0(0)
        stageA0(1)
        stageA(0)
        for idx in range(len(iters)):
            if idx + 2 < len(iters):
                stageA0(idx + 2)
            if idx + 1 < len(iters):
                stageA(idx + 1)
            stageB(idx)
            if idx >= 1:
                stageC(idx - 1)
        stageC(len(iters) - 1)
        gpieces = [(blk, half) for blk in range(8) for half in range(2)]

        def gel_a(gi):
            blk, half = gpieces[gi]
            tsl = slice(half * 1024, (half + 1) * 1024)
            ukey = ("uT", blk, half)
            g_ = (ysb, gtmp)[gi % 2]
            gs = gsig[gi % 2]
            gk, gsk = ("gel", gi % 2), ("gsig", gi % 2)
            P.dve(lambda e: e.tensor_tensor(out=g_[:, :], in0=uT[:, blk, tsl], in1=uT[:, blk, tsl], op=ALU.mult),
                  reads=[ukey], writes=[gk])
            P.dve(lambda e: e.tensor_scalar(out=g_[:, :], in0=g_[:, :], scalar1=0.044715, scalar2=1.0, op0=ALU.mult,
                                            op1=ALU.add), reads=[gk], writes=[gk])
            P.dve(lambda e: e.tensor_tensor(out=g_[:, :], in0=g_[:, :], in1=uT[:, blk, tsl], op=ALU.mult),
                  reads=[gk, ukey], writes=[gk])
            P.act(lambda e: e.activation(out=gs[:, :], in_=g_[:, :], func=AF.Sigmoid, scale=GELU_C), reads=[gk],
                  writes=[gsk])

        def gel_b(gi):
            blk, half = gpieces[gi]
            tsl = slice(half * 1024, (half + 1) * 1024)
            ukey = ("uT", blk, half)
            gs = gsig[gi % 2]
            P.dve(lambda e: e.tensor_tensor(out=uT[:, blk, tsl], in0=gs[:, :], in1=uT[:, blk, tsl], op=ALU.mult),
                  reads=[("gsig", gi % 2), ukey], writes=[ukey])

        gel_a(0)
        for gi in range(len(gpieces)):
            if gi + 1 < len(gpieces):
                gel_a(gi + 1)
            gel_b(gi)
        P.barrier()
        if STOP == 7:
            P.emit()
            return nc

        wg = sb(SCR + 24576, [128, 8, 1024], BF16)
        sg = [sb(SCR + 2048 * i, [128, 512], F32) for i in range(2)]
        ssm = [sb(SCR + 4096 + 2048 * i, [128, 512], F32) for i in range(2)]
        sqb = [sb(SCR + 8192 + 1024 * i, [128, 512], BF16) for i in range(2)]
        rbc = sb(SCR + 12288, [128, 2048], F32)
        ones128 = sb(SCR + 20480, [128, 128], BF16)
        Wo = sb(R0, [128, 16, 2048], BF16)
        w_o_v = w_o.rearrange("(c p) n -> p c n", p=128)
        P.dma("gpsimd", wg[:, :, :], w_glu.rearrange("(c p) n -> p c n", p=128), writes=["wg"])
        for q4 in range(4):
            P.dma("gpsimd", Wo[:, 4 * q4:4 * q4 + 4, :], w_o_v[:, 4 * q4:4 * q4 + 4, :], writes=[("Wo", q4)])
        P.dve(lambda e: e.memset(ones128[:, :], 1.0), writes=["ones128"])
        glu_it = [(e8, tb) for e8 in range(8) for tb in range(4)]

        def glu_mm(i):
            e8, tb = glu_it[i]
            bk = i % 4

            def mg(e):
                last = None
                for c in range(8):
                    last = e.matmul(ps[:, bk, :], lhsT=wg[:, c, e8 * 128:(e8 + 1) * 128],
                                    rhs=uT[:, c, tb * 512:(tb + 1) * 512], start=(c == 0), stop=(c == 7))
                return last
            P.pe(mg, reads=["wg"], writes=[("ps", bk)])

        def glu_ew(i):
            e8, tb = glu_it[i]
            bk = i % 4
            b2 = i % 2
            P.act(lambda e: e.activation(out=sg[b2][:, :], in_=ps[:, bk, :], func=AF.Sigmoid,
                                         bias=sm[:, C_BGLU + e8:C_BGLU + e8 + 1], scale=1.0),
                  reads=[("ps", bk)], writes=[("sg", b2)])
            P.dve(lambda e: e.tensor_tensor(out=ssm[b2][:, :], in0=uT[:, e8, tb * 512:(tb + 1) * 512], in1=sg[b2][:, :],
                                            op=ALU.mult), reads=[("sg", b2)], writes=[("ssm", b2)])
            P.dve(lambda e: e.tensor_tensor(out=sqb[b2][:, :], in0=ssm[b2][:, :], in1=ssm[b2][:, :], op=ALU.mult),
                  reads=[("ssm", b2)], writes=[("sqb", b2)])
            P.dve(lambda e: e.tensor_scalar(out=actT[:, 8 + e8, tb * 512:(tb + 1) * 512], in0=ssm[b2][:, :],
                                            scalar1=sm[:, C_GSSM + e8:C_GSSM + e8 + 1], scalar2=None, op0=ALU.mult),
                  reads=[("ssm", b2)], writes=[("mixS", e8)])

        def glu_sq(i):
            e8, tb = glu_it[i]
            b2 = i % 2
            P.pe(lambda e: e.matmul(ps[:, 4 + tb, :], lhsT=ones128[:, :], rhs=sqb[b2][:, :], start=(e8 == 0), stop=(e8 == 7)),
                 reads=[("sqb", b2), "ones128"], writes=[("ps", 4 + tb)])

        glu_mm(0)
        for i in range(len(glu_it)):
            glu_ew(i)
            if i + 1 < len(glu_it):
                glu_mm(i + 1)
            glu_sq(i)
        P.dve(lambda e: e.tensor_scalar(out=rbc[:, :].rearrange("p (a b) -> p a b", a=4), in0=ps[:, 4:8, :], scalar1=1.0 / 1024,
                                        scalar2=EPS, op0=ALU.mult, op1=ALU.add), reads=[("ps", 4 + t) for t in range(4)],
              writes=["rbc"])
        P.act(lambda e: e.activation(out=rbc[:, :], in_=rbc[:, :], func=AF.Sqrt), reads=["rbc"], writes=["rbc"])
        P.dve(lambda e: e.reciprocal(out=rbc[:, :], in_=rbc[:, :]), reads=["rbc"], writes=["rbc"])
        for e8 in range(8):
            P.dve(lambda e, e8=e8: e.tensor_tensor(out=actT[:, 8 + e8, :], in0=actT[:, 8 + e8, :], in1=rbc[:, :], op=ALU.mult),
                  reads=["rbc", ("mixS", e8)], writes=[("mixS", e8)])
        P.barrier()
        if STOP == 8:
            P.emit()
            return nc

        gpm = sb(65536, [128, 2048], F32)
        gpf = sb(65536 + 8192, [128, 2048], F32)
        xt2 = [sb(65536 + 16384 + 8192 * i, [128, 2048], F32) for i in range(2)]
        Abuf = [sb(SCR + 8192 * i, [128, 2048], F32) for i in range(2)]
        hnb = [sb(SCR + 16384 + 4096 * i, [128, 2048], BF16) for i in range(2)]
        ojb = sb(SCR + 24576, [128, 2048], BF16)
        P.dma("sync", gpm[:, :], gvecs[1:2, :].partition_broadcast(128), writes=["gpm"])
        P.dma("sync", gpf[:, :], gvecs[2:3, :].partition_broadcast(128), writes=["gpf"])

        def v4(ap_):
            return ap_.rearrange("p (a b) -> p a b", a=4)

        def wo_mm(tt):
            tsl = slice(tt * 128, (tt + 1) * 128)
            bset = 4 * (tt % 2)

            def mo(e):
                last = None
                for cbk in range(4):
                    for c in range(16):
                        last = e.matmul(ps[:, bset + cbk, :], lhsT=actT[:, c, tsl], rhs=Wo[:, c, cbk * 512:(cbk + 1) * 512],
                                        start=(c == 0), stop=(c == 15))
                return last
            P.pe(mo, reads=[("act", tt)], writes=[("ps", bset + i) for i in range(4)])

        def wo_post(tt):
            tsl = slice(tt * 128, (tt + 1) * 128)
            s_ = tt % 2
            bset = 4 * s_
            c0 = 16 + 8 * s_
            A = Abuf[s_]
            pk = [("ps", bset + i) for i in range(4)]
            acc = ps[:, bset:bset + 4, :]
            P.dma("sync", xt2[s_][:, :], x[tsl, :], writes=[("xt2", s_)])
            P.act(lambda e: e.activation(out=v4(A[:, :]), in_=acc, func=AF.Copy), reads=pk, writes=[("A", s_)])
            P.dve(lambda e: e.scalar_tensor_tensor(out=ojb[:, :], in0=A[:, :], scalar=1.0, in1=A[:, :], op0=ALU.mult,
                                                   op1=ALU.mult, accum_out=stat[:, c0:c0 + 1]),
                  reads=[("A", s_)], writes=["oj", ("st", c0)])
            rstd_from(("st", c0), stat[:, c0:c0 + 1], stat[:, c0 + 2:c0 + 3], ("st", c0 + 2), D, stat[:, c0 + 1:c0 + 2],
                      ("st", c0 + 1))
            P.dve(lambda e: e.scalar_tensor_tensor(out=A[:, :], in0=A[:, :], scalar=stat[:, c0 + 2:c0 + 3], in1=gpm[:, :],
                                                   op0=ALU.mult, op1=ALU.mult), reads=[("A", s_), ("st", c0 + 2), "gpm"],
                  writes=[("A", s_)])
            P.dve(lambda e: e.tensor_tensor(out=A[:, :], in0=A[:, :], in1=xt2[s_][:, :], op=ALU.add),
                  reads=[("A", s_), ("xt2", s_)], writes=[("A", s_)])
            P.dma("sync", hscr[tsl, :], A[:, :], reads=[("A", s_)], writes=[("hscr", tt)])
            P.dve(lambda e: e.scalar_tensor_tensor(out=ojb[:, :], in0=A[:, :], scalar=1.0, in1=A[:, :], op0=ALU.mult,
                                                   op1=ALU.mult, accum_out=stat[:, c0 + 3:c0 + 4]),
                  reads=[("A", s_)], writes=["oj", ("st", c0 + 3)])
            rstd_from(("st", c0 + 3), stat[:, c0 + 3:c0 + 4], stat[:, c0 + 5:c0 + 6], ("st", c0 + 5), D,
                      stat[:, c0 + 4:c0 + 5], ("st", c0 + 4))
            P.dve(lambda e: e.scalar_tensor_tensor(out=hnb[s_][:, :], in0=A[:, :], scalar=stat[:, c0 + 5:c0 + 6], in1=gpf[:, :],
                                                   op0=ALU.mult, op1=ALU.mult), reads=[("A", s_), ("st", c0 + 5), "gpf"],
                  writes=[("hnb", s_)])

            def trh(e):
                last = None
                for c in range(16):
                    last = e.transpose(out=psb[:, bset + c // 8, (c % 8) * 128:(c % 8) * 128 + 128],
                                       in_=hnb[s_][:, c * 128:(c + 1) * 128], identity=identb[:, :])
                return last
            P.pe(trh, reads=[("hnb", s_)], writes=[("ps", bset), ("ps", bset + 1)])
            P.act(lambda e: e.activation(out=actT[:, 0:8, tsl], in_=psb[:, bset, :].rearrange("p (a b) -> p a b", a=8),
                                         func=AF.Copy), reads=[("ps", bset)], writes=[("act", tt)])
            P.dve(lambda e: e.tensor_copy(out=actT[:, 8:16, tsl], in_=psb[:, bset + 1, :].rearrange("p (a b) -> p a b", a=8)),
                  reads=[("ps", bset + 1), ("act", tt)], writes=[("act", tt)])

        wo_mm(0)
        for tt in range(NT):
            if tt + 1 < NT:
                wo_mm(tt + 1)
            wo_post(tt)
        P.barrier()
        if STOP == 9:
            P.emit()
            return nc

        hidT = sb(65536, [128, NFC, 512], BF16)
        ff = sb(110592, [128, 4, 2048], F32)
        wpool = [sb(143360 + 4096 * i, [128, 4, 512], BF16) for i in range(8)]
        ht = sb(176128, [128, 2048], F32)
        gpo = sb(184320, [128, 2048], F32)
        sgf = [sb(192512 + 2048 * i, [128, 512], F32) for i in range(2)]
        fj = sb(196608, [128, 2048], BF16)
        P.dma("sync", gpo[:, :], gvecs[3:4, :].partition_broadcast(128), writes=["gpo"])
        wg_v = w_gate.rearrange("(c p) n -> p c n", p=128)
        wu_v = w_up.rearrange("(c p) n -> p c n", p=128)
        wd_v = w_down.rearrange("(f p) n -> p f n", p=128)
        nld = 0
        pending_epi = []

        def ffn_epi(tb, t4):
            tt = tb * 4 + t4
            fk = [("ff", t4, db) for db in range(4)]
            P.dma("sync", ht[:, :], hscr[tt * 128:(tt + 1) * 128, :], reads=[("hscr", tt)], writes=["ht"])
            P.dve(lambda e: e.scalar_tensor_tensor(out=fj[:, :], in0=ff[:, t4, :], scalar=1.0, in1=ff[:, t4, :],
                                                   op0=ALU.mult, op1=ALU.mult, accum_out=stat[:, 14:15]),
                  reads=fk, writes=["fj", "st14"])
            rstd_from("st14", stat[:, 14:15], stat[:, 3:4], "st3", D, stat[:, 15:16], "st15")
            P.dve(lambda e: e.scalar_tensor_tensor(out=ff[:, t4, :], in0=ff[:, t4, :], scalar=stat[:, 3:4], in1=gpo[:, :],
                                                   op0=ALU.mult, op1=ALU.mult), reads=fk + ["st3", "gpo"], writes=fk)
            P.dve(lambda e: e.tensor_tensor(out=ff[:, t4, :], in0=ff[:, t4, :], in1=ht[:, :], op=ALU.add),
                  reads=fk + ["ht"], writes=fk)
            P.dma("sync", out[tt * 128:(tt + 1) * 128, :], ff[:, t4, :], reads=fk, writes=[("out", tt)])

        for tb in range(4):
            tsl = slice(tb * 512, (tb + 1) * 512)
            for blk in range(11):
                if blk in (1, 3, 5, 7) and pending_epi:
                    ffn_epi(*pending_epi.pop(0))
                for cq in range(4):
                    gb = wpool[nld % 8]
                    gk = ("wp", nld % 8)
                    nld += 1
                    ub = wpool[nld % 8]
                    uk = ("wp", nld % 8)
                    nld += 1
                    P.dma("gpsimd", gb[:, :, :], wg_v[:, 4 * cq:4 * cq + 4, blk * 512:(blk + 1) * 512], writes=[gk])
                    P.dma("gpsimd", ub[:, :, :], wu_v[:, 4 * cq:4 * cq + 4, blk * 512:(blk + 1) * 512], writes=[uk])
                    for fcl in range(4):
                        def mgu(e, gb=gb, ub=ub, fcl=fcl, cq=cq, tsl=tsl):
                            last = None
                            for c4 in range(4):
                                c = 4 * cq + c4
                                e.matmul(ps[:, fcl, :], lhsT=gb[:, c4, fcl * 128:(fcl + 1) * 128], rhs=actT[:, c, tsl],
                                         start=(c == 0), stop=(c == 15))
                                last = e.matmul(ps[:, 4 + fcl, :], lhsT=ub[:, c4, fcl * 128:(fcl + 1) * 128],
                                                rhs=actT[:, c, tsl], start=(c == 0), stop=(c == 15))
                            return last
                        P.pe(mgu, reads=[gk, uk], writes=[("ps", fcl), ("ps", 4 + fcl)])
                        if cq == 3:
                            fc = 4 * blk + fcl
                            b2 = fc % 2
                            P.act(lambda e, fcl=fcl, b2=b2: e.activation(out=sgf[b2][:, :], in_=ps[:, fcl, :], func=AF.Silu),
                                  reads=[("ps", fcl)], writes=[("sgf", b2)])
                            P.dve(lambda e, fcl=fcl, b2=b2, fc=fc: e.tensor_tensor(out=hidT[:, fc, :], in0=sgf[b2][:, :],
                                                                                   in1=ps[:, 4 + fcl, :], op=ALU.mult),
                                  reads=[("sgf", b2), ("ps", 4 + fcl)], writes=[("hid", fc)])
            for db in range(4):
                bs = 4 * (db % 2)
                for fq in range(11):
                    wdb = wpool[nld % 8]
                    wk = ("wp", nld % 8)
                    nld += 1
                    P.dma("gpsimd", wdb[:, :, :], wd_v[:, 4 * fq:4 * fq + 4, db * 512:(db + 1) * 512], writes=[wk])

                    def md(e, wdb=wdb, fq=fq, bs=bs):
                        last = None
                        for f4 in range(4):
                            fc = fq * 4 + f4
                            for t4 in range(4):
                                last = e.matmul(ps[:, bs + t4, :], lhsT=hidT[:, fc, t4 * 128:(t4 + 1) * 128], rhs=wdb[:, f4, :],
                                                start=(fc == 0), stop=(fc == NFC - 1))
                        return last
                    P.pe(md, reads=[wk] + [("hid", fq * 4 + f4) for f4 in range(4)], writes=[("ps", bs + t4) for t4 in range(4)])
                for t4 in range(4):
                    if t4 % 2 == 0:
                        P.act(lambda e, t4=t4, db=db, bs=bs: e.activation(out=ff[:, t4, db * 512:(db + 1) * 512],
                                                                          in_=ps[:, bs + t4, :], func=AF.Copy),
                              reads=[("ps", bs + t4)], writes=[("ff", t4, db)])
                    else:
                        P.dve(lambda e, t4=t4, db=db, bs=bs: e.tensor_copy(out=ff[:, t4, db * 512:(db + 1) * 512],
                                                                           in_=ps[:, bs + t4, :]),
                              reads=[("ps", bs + t4)], writes=[("ff", t4, db)])
            pending_epi = [(tb, t4) for t4 in range(4)]
        for (tb_, t4_) in pending_epi:
            ffn_epi(tb_, t4_)
        P.emit()
        print('sig counts', P.sig_counts, 'dma cum', max(P.dma_cum))
    return nc


def _host_layouts(inp):
    f32 = np.float32
    G, N, Pp = 64, 64, 16
    sm = np.zeros((128, NSM), f32)
    sm[:, C_ID:C_ID + 128] = np.eye(128, dtype=f32)
    kk = np.arange(128)[:, None]
    qq = np.arange(128)[None, :]
    sm[:, C_MASK:C_MASK + 128] = (kk > qq).astype(f32)
    sm[:, C_MASK + 128:C_MASK + 256] = (kk <= qq).astype(f32)
    half = 32
    inv_freq = (np.float32(10000.0) ** (-np.arange(half, dtype=f32) / np.float32(half))).astype(f32)
    sm[:, C_INVF:C_INVF + 32] = inv_freq[None, :]
    sm[:, C_SINK:C_SINK + 16] = inp["sinks"][0][None, :]
    sm[:, C_GSSM:C_GSSM + 8] = inp["g_ssm_out"][0].reshape(8, 128).T
    sm[:, C_BGLU:C_BGLU + 8] = inp["b_glu"][0].reshape(8, 128).T
    sm[:, C_DSK:C_DSK + 8] = inp["d_skip"][0].reshape(8, 8, 16).reshape(8, 128).T
    a_re, a_im, ldt = inp["a_re"][0], inp["a_im"][0], inp["log_dt"][0]
    for b in range(2):
        sm[64 * b:64 * b + 64, C_ARA:C_ARA + 32] = a_re[b::2, :].T
        sm[64 * b:64 * b + 64, C_AIA:C_AIA + 32] = a_im[b::2, :].T
        sm[64 * b:64 * b + 64, C_LDA:C_LDA + 32] = np.broadcast_to(ldt[b::2][None, :], (64, 32))
    lb3 = np.zeros((128, 3, 8, 2, 64), f32)
    for gq in range(8):
        rows = slice(16 * gq, 16 * gq + 16)
        for blk in range(8):
            g = 8 * blk + gq
            lb3[rows, 0, blk, :, :] = a_re[g][None, None, :]
            lb3[rows, 1, blk, :, :] = a_im[g][None, None, :]
            lb3[rows, 2, blk, :, :] = ldt[g]
    lb3 = lb3.reshape(128, 3, 1024)
    bexp = np.zeros((128, 2, 8, 2, 64), f32)
    cexp = np.zeros((128, 2, 32, 8, 16), f32)
    maska = np.zeros((128, 4), f32)
    b_re, b_im, c_re, c_im = inp["b_re"][0], inp["b_im"][0], inp["c_re"][0], inp["c_im"][0]
    for g in range(G):
        blk, gq = divmod(g, 8)
        a, b = divmod(gq, 2)
        rows = slice(16 * gq, 16 * gq + 16)
        bexp[rows, 0, blk, b, :] = b_re[g].T
        bexp[rows, 1, blk, b, :] = b_im[g].T
        maska[rows, a] = 1.0
        pp = g // 2
        cexp[64 * b:64 * b + 64, 0, pp, gq, :] = c_re[g].T
        cexp[64 * b:64 * b + 64, 1, pp, gq, :] = c_im[g].T
    bexp = bexp.reshape(128, 2, 1024)
    cexp = cexp.reshape(128, 2, 4096)
    gv = np.zeros((5, D), f32)
    gv[0] = inp["g_pre_mix"][0]
    gv[1] = inp["g_post_mix"][0]
    gv[2] = inp["g_pre_ffn"][0]
    gv[3] = inp["g_post_ffn"][0]
    gv[4, :1024] = inp["g_attn_out"][0]
    tokc = np.broadcast_to(np.arange(1, 2049, dtype=f32)[None, :], (128, 2048)).copy()
    shared = {
        "smalls": sm, "gvecs": gv, "lb3": lb3, "bexp": bexp, "cexp": cexp, "tokc": tokc, "maska": maska,
        "w_in": np.ascontiguousarray(inp["w_in"][0]), "w_glu": np.ascontiguousarray(inp["w_glu"][0]),
        "w_o": np.ascontiguousarray(inp["w_o"][0]), "w_gate": np.ascontiguousarray(inp["w_gate"][0]),
        "w_up": np.ascontiguousarray(inp["w_up"][0]), "w_down": np.ascontiguousarray(inp["w_down"][0]),
    }
    return shared


def kernel(**inputs):
    inp = {k: np.asarray(v) for k, v in inputs.items()}
    shared = _host_layouts(inp)
    nc = build_nc()
    in_maps = []
    for c in range(8):
        m = dict(shared)
        m["x"] = np.ascontiguousarray(inp["x"][c])
        m["pos"] = np.ascontiguousarray(inp["positions"][c].astype(np.int32).reshape(16, 128).T)
        in_maps.append(m)
    res = run_bass_kernel_spmd(nc, in_maps, core_ids=list(range(8)))
    return np.stack([np.asarray(r["out"], dtype=np.float32) for r in res.results], axis=0)
```

```python
import math
import numpy as np
from contextlib import ExitStack
import concourse.bass as bass
import concourse.mybir as mybir
from concourse.bass_utils import run_bass_kernel_spmd

F32 = mybir.dt.float32
BF16 = mybir.dt.bfloat16
I32 = mybir.dt.int32
AF = mybir.ActivationFunctionType
ALU = mybir.AluOpType
AX = mybir.AxisListType

ENGS = ["tensor", "vector", "scalar", "gpsimd", "sync"]


class Op:
    __slots__ = ("eng", "fn", "deps", "signal", "semval", "dma", "dsem", "dval", "prev_on_sem")

    def __init__(self, eng, fn, dma):
        self.eng = eng
        self.fn = fn
        self.deps = []
        self.signal = False
        self.semval = 0
        self.dma = dma
        self.dsem = None
        self.dval = 0
        self.prev_on_sem = None


class Prog:
    def __init__(self, nc, n_dma_sems=32):
        self.nc = nc
        self.ops = {e: [] for e in ENGS}
        self.res = {}
        self.n_dma_sems = n_dma_sems
        self.dma_rr = 0
        self.dma_last = [None] * n_dma_sems
        self.dma_cum = [0] * n_dma_sems

    def add(self, eng, fn, reads=(), writes=(), dma=False):
        op = Op(eng, fn, dma)
        deps = {}
        for k in reads:
            st = self.res.get(k)
            if st is not None and st[0] is not None:
                deps[id(st[0])] = st[0]
        for k in writes:
            st = self.res.get(k)
            if st is not None:
                if st[0] is not None:
                    deps[id(st[0])] = st[0]
                for r in st[1]:
                    deps[id(r)] = r
        for k in reads:
            st = self.res.get(k)
            if st is None:
                self.res[k] = [None, [op]]
            else:
                st[1].append(op)
        for k in writes:
            self.res[k] = [op, []]
        for d in deps.values():
            if d is op:
                continue
            if (not d.dma) and d.eng == eng and eng == "tensor":
                continue
            op.deps.append(d)
            d.signal = True
        if dma:
            s = self.dma_rr
            self.dma_rr = (self.dma_rr + 1) % self.n_dma_sems
            op.dsem = s
            self.dma_cum[s] += 16
            op.dval = self.dma_cum[s]
            op.prev_on_sem = self.dma_last[s]
            self.dma_last[s] = op
        self.ops[eng].append(op)
        return op

    def pe(self, fn, reads=(), writes=()):
        return self.add("tensor", fn, reads, writes)

    def dve(self, fn, reads=(), writes=()):
        return self.add("vector", fn, reads, writes)

    def act(self, fn, reads=(), writes=()):
        return self.add("scalar", fn, reads, writes)

    def pool(self, fn, reads=(), writes=()):
        return self.add("vector" if POOL_AS_DVE else "gpsimd", fn, reads, writes)

    def dma(self, eng, out, in_, reads=(), writes=(), **kw):
        return self.add(eng, lambda e: e.dma_start(out=out, in_=in_, **kw), reads, writes, dma=True)

    def barrier(self):
        lasts = []
        for e in ENGS:
            for op in reversed(self.ops[e]):
                if (not op.dma) and op.fn is not None:
                    lasts.append(op)
                    break
        dl = [d for d in self.dma_last if d is not None]
        for e in ENGS:
            op = Op(e, None, False)
            for d in lasts:
                if d.eng != e:
                    op.deps.append(d)
                    d.signal = True
            op.deps.extend(dl)
            self.ops[e].append(op)
        self.res = {}

    def emit(self):
        nc = self.nc
        self.barrier()
        for e in ENGS:
            cum = 0
            for op in self.ops[e]:
                if op.dma:
                    continue
                if op.signal:
                    cum += 1
                    op.semval = cum
            self.sig_counts = getattr(self, "sig_counts", {})
            self.sig_counts[e] = (cum, len(self.ops[e]))
        with ExitStack() as st:
            esem = {e: st.enter_context(nc.semaphore("es_" + e)) for e in ENGS}
            dsem = [st.enter_context(nc.semaphore("ds_%d" % i)) for i in range(self.n_dma_sems)]
            block = st.enter_context(nc.Block())

            def run(eng, e):
                waited = {}

                def wait_for(d):
                    if d.dma:
                        key, sem, val = ("d", d.dsem), dsem[d.dsem], d.dval
                    else:
                        key, sem, val = ("e", d.eng), esem[d.eng], d.semval
                    if waited.get(key, 0) < val:
                        eng.wait_ge(sem, val)
                        waited[key] = val

                for op in self.ops[e]:
                    for d in op.deps:
                        wait_for(d)
                    if op.dma and op.prev_on_sem is not None:
                        wait_for(op.prev_on_sem)
                    if op.fn is None:
                        continue
                    inst = op.fn(eng)
                    if op.dma:
                        inst.then_inc(dsem[op.dsem], 16)
                    elif op.signal:
                        inst.then_inc(esem[e], 1)

            for e in ENGS:
                getattr(block, e)(lambda eng, e=e: run(eng, e))


D = 2048
L = 2048
NT = 16
DFF = 5632
NFC = 44
EPS = 1e-6
BASE = 17408
STOP = -1
POOL_AS_DVE = True
ATT_LEVEL = 99
ATT_TILES = 16
INV2PI = 1.0 / (2.0 * math.pi)
TWO_PI = 2.0 * math.pi * (1.0 - 2e-7)
MAGIC = 12582912.0
GELU_C = 2.0 * math.sqrt(2.0 / math.pi)

C_ID = 0
C_MASK = 128
C_INVF = 384
C_SINK = 416
C_GSSM = 432
C_BGLU = 440
C_DSK = 448
C_ARA = 456
C_AIA = 488
C_LDA = 520
NSM = 552


def build_nc():
    nc = bass.Bass("TRN2", target_bir_lowering=False)

    def din(name, shape, dt=F32):
        return nc.dram_tensor(name, list(shape), dt, kind="ExternalInput").ap()

    x = din("x", [L, D])
    pos = din("pos", [128, NT], I32)
    smalls = din("smalls", [128, NSM])
    gvecs = din("gvecs", [5, D])
    w_in = din("w_in", [D, 2560])
    w_glu = din("w_glu", [1024, 1024])
    w_o = din("w_o", [D, D])
    w_gate = din("w_gate", [D, DFF])
    w_up = din("w_up", [D, DFF])
    w_down = din("w_down", [DFF, D])
    lb3 = din("lb3", [128, 3, 1024])
    bexp = din("bexp", [128, 2, 1024])
    maska = din("maska", [128, 4])
    cexp = din("cexp", [128, 2, 4096])
    tokc = din("tokc", [128, 2048])
    out = nc.dram_tensor("out", [L, D], F32, kind="ExternalOutput").ap()
    hscr = nc.dram_tensor("hscr", [L, D], F32, kind="Internal").ap()

    cnt = [0]

    def sb(off, shape, dt):
        cnt[0] += 1
        return nc.alloc_sbuf_tensor_at("t%d" % cnt[0], list(shape), dt, offset=BASE + off)

    def cap(t, off, dims):
        return bass.AP(tensor=t, offset=off, ap=[list(d) for d in dims])

    P = Prog(nc)
    with ExitStack() as st:
        ps = st.enter_context(nc.psum_tensor("ps", [128, 8, 512], F32))
        psb = ps[:, :, :].bitcast(BF16)

        actT = sb(0, [128, 16, 2048], BF16)
        uT = sb(65536, [128, 8, 2048], BF16)
        qT = sb(98304, [128, 8, 2048], BF16)
        kT2 = sb(131072, [128, 4, 2176], BF16)
        v1 = sb(148480, [128, 17, 4, 65], BF16)
        cosT = sb(157696, [128, 16, 32], F32)
        sinT = sb(157696 + 2048, [128, 16, 32], F32)
        nsinT = sb(157696 + 4096, [128, 16, 32], F32)
        SCR = 163840
        CONST = 207872
        sm = sb(CONST, [128, NSM], F32)
        identb = sb(CONST + 2208, [128, 128], BF16)
        maskb = sb(CONST + 2464, [128, 2, 128], BF16)
        stat = sb(CONST + 2976, [128, 64], F32)
        ngs = sb(CONST + 3232, [128, 4], F32)
        onesb = sb(CONST + 3264, [128, 2], BF16)
        rstd_s = sb(CONST + 3296, [128, 16], F32)
        posi = sb(CONST + 3360, [128, 16], I32)
        posf = sb(CONST + 3424, [128, 16], F32)
        thp = sb(CONST + 3488, [128, 32], F32)
        rho = sb(CONST + 3616, [128, 32], F32)
        rlast = sb(CONST + 3744, [128, 32, 2], F32)
        identf = sm[:, C_ID:C_ID + 128]
        magp = sb(CONST + 4000, [128, 4], F32)
        maskA = sb(CONST + 4032, [128, 4], F32)

        P.dma("sync", sm[:, :], smalls, writes=["sm"])
        P.dma("sync", posi[:, :], pos, writes=["posi"])
        P.dma("sync", maskA[:, :], maska, writes=["maskA"])
        P.dve(lambda e: e.tensor_copy(out=identb[:, :], in_=sm[:, C_ID:C_ID + 128]), reads=["sm"], writes=["identb"])
        P.dve(lambda e: e.tensor_copy(out=maskb[:, :, :], in_=sm[:, C_MASK:C_MASK + 256].rearrange("p (a b) -> p a b", a=2)),
              reads=["sm"], writes=["maskb"])
        P.dve(lambda e: e.memset(onesb[:, :], 1.0), writes=["onesb"])
        P.dve(lambda e: e.memset(magp[:, 0:1], MAGIC), writes=["magp0"])
        P.dve(lambda e: e.memset(magp[:, 1:2], -MAGIC), writes=["magp1"])
        P.dve(lambda e: e.memset(magp[:, 2:3], math.pi / 2), writes=["magp2"])
        P.dve(lambda e: e.tensor_reduce(out=ngs[:, :], in_=sm[:, C_SINK:C_SINK + 16].rearrange("p (a b) -> p a b", a=4),
                                        axis=AX.X, op=ALU.max, negate=True), reads=["sm"], writes=["ngs"])
        P.dve(lambda e: e.memset(kT2[:, :, 0:128], 0.0), writes=["kpad"])
        P.dve(lambda e: e.memset(v1[:, 0, :, :], 0.0), writes=["vpad"])
        P.dve(lambda e: e.memset(v1[:, 1:17, :, 64:65], 1.0), writes=["vones"])
        P.dve(lambda e: e.tensor_copy(out=posf[:, :], in_=posi[:, :]), reads=["posi"], writes=["posf"])

        rt = [sb(65536 + 2048 * i, [128, 16, 32], F32) for i in range(4)]
        P.dve(lambda e: e.tensor_tensor(out=rt[0][:, :, :], in0=cap(posf, 0, [[16, 128], [1, 16], [0, 32]]),
                                        in1=cap(sm, C_INVF, [[NSM, 128], [0, 16], [1, 32]]), op=ALU.mult),
              reads=["posf", "sm"], writes=["rt0"])
        P.dve(lambda e: e.tensor_scalar(out=rt[0][:, :, :], in0=rt[0][:, :, :], scalar1=INV2PI, scalar2=None, op0=ALU.mult),
              reads=["rt0"], writes=["rt0"])
        P.dve(lambda e: e.tensor_scalar(out=rt[1][:, :, :], in0=rt[0][:, :, :], scalar1=MAGIC, scalar2=MAGIC, op0=ALU.add,
                                        op1=ALU.subtract), reads=["rt0"], writes=["rt1"])
        P.dve(lambda e: e.tensor_tensor(out=rt[2][:, :, :], in0=rt[0][:, :, :], in1=rt[1][:, :, :], op=ALU.subtract),
              reads=["rt0", "rt1"], writes=["rt2"])
        P.dve(lambda e: e.scalar_tensor_tensor(out=rt[3][:, :, :], in0=rt[2][:, :, :], scalar=-1.0, in1=rt[2][:, :, :],
                                               op0=ALU.mult, op1=ALU.max), reads=["rt2"], writes=["rt3"])
        P.act(lambda e: e.activation(out=sinT[:, :, :], in_=rt[2][:, :, :], func=AF.Sin, scale=TWO_PI), reads=["rt2"], writes=["sinT"])
        P.act(lambda e: e.activation(out=nsinT[:, :, :], in_=rt[2][:, :, :], func=AF.Sin, scale=-TWO_PI), reads=["rt2"], writes=["nsinT"])
        P.act(lambda e: e.activation(out=cosT[:, :, :], in_=rt[3][:, :, :], func=AF.Sin, scale=-TWO_PI, bias=math.pi / 2),
              reads=["rt3"], writes=["cosT"])
        if STOP == 0:
            P.emit()
            return nc

        def rstd_from(ss_key, ss_ap, dst_ap, dst_key, n, tmp_ap, tmp_key):
            P.dve(lambda e: e.tensor_scalar(out=tmp_ap, in0=ss_ap, scalar1=1.0 / n, scalar2=EPS, op0=ALU.mult, op1=ALU.add),
                  reads=[ss_key], writes=[tmp_key])
            P.act(lambda e: e.activation(out=tmp_ap, in_=tmp_ap, func=AF.Ln), reads=[tmp_key], writes=[tmp_key])
            P.act(lambda e: e.activation(out=dst_ap, in_=tmp_ap, func=AF.Exp, scale=-0.5), reads=[tmp_key], writes=[dst_key])

        xt = [sb(SCR + 8192 * i, [128, 2048], F32) for i in range(2)]
        xs = [sb(SCR + 16384 + 4096 * i, [128, 2048], BF16) for i in range(2)]
        gbc = sb(SCR + 24576, [128, 2048], F32)
        junk = sb(SCR + 32768, [128, 2048], BF16)
        P.dma("sync", gbc[:, :], gvecs[0:1, :].partition_broadcast(128), writes=["gbc"])

        def a1_pre(tt):
            b = tt % 2
            c0 = 4 * b
            P.dma("sync", xt[b][:, :], x[tt * 128:(tt + 1) * 128, :], writes=[("xt", b)])
            P.dve(lambda e: e.scalar_tensor_tensor(out=junk[:, :], in0=xt[b][:, :], scalar=1.0, in1=xt[b][:, :],
                                                   op0=ALU.mult, op1=ALU.mult, accum_out=stat[:, c0:c0 + 1]),
                  reads=[("xt", b)], writes=["junk", ("st", c0)])
            rstd_from(("st", c0), stat[:, c0:c0 + 1], stat[:, c0 + 2:c0 + 3], ("st", c0 + 2), D, stat[:, c0 + 1:c0 + 2],
                      ("st", c0 + 1))
            P.dve(lambda e: e.scalar_tensor_tensor(out=xs[b][:, :], in0=xt[b][:, :], scalar=stat[:, c0 + 2:c0 + 3],
                                                   in1=gbc[:, :], op0=ALU.mult, op1=ALU.mult),
                  reads=[("xt", b), ("st", c0 + 2), "gbc"], writes=[("xs", b)])

        def a1_post(tt):
            b = tt % 2
            bk = 2 * b

            def tr(e):
                last = None
                for c in range(16):
                    last = e.transpose(out=psb[:, bk + c // 8, (c % 8) * 128:(c % 8) * 128 + 128],
                                       in_=xs[b][:, c * 128:(c + 1) * 128], identity=identb[:, :])
                return last
            P.pe(tr, reads=[("xs", b), "identb"], writes=[("ps", bk), ("ps", bk + 1)])
            P.act(lambda e: e.activation(out=actT[:, 0:8, tt * 128:(tt + 1) * 128],
                                         in_=psb[:, bk, :].rearrange("p (a b) -> p a b", a=8), func=AF.Copy),
                  reads=[("ps", bk)], writes=[("actT", tt, 0)])
            P.act(lambda e: e.activation(out=actT[:, 8:16, tt * 128:(tt + 1) * 128],
                                         in_=psb[:, bk + 1, :].rearrange("p (a b) -> p a b", a=8), func=AF.Copy),
                  reads=[("ps", bk + 1)], writes=[("actT", tt, 1)])

        a1_pre(0)
        for tt in range(NT):
            if tt + 1 < NT:
                a1_pre(tt + 1)
            a1_post(tt)
        P.barrier()
        if STOP == 1:
            P.emit()
            return nc

        wb = [sb(SCR + 8192 * i, [128, 16, 256], BF16) for i in range(2)]
        rAs = [sb(SCR + 16384 + 1024 * i, [128, 256], F32) for i in range(2)]
        rBs = [sb(SCR + 18432 + 1024 * i, [128, 256], F32) for i in range(2)]
        qrs = [sb(SCR + 20480 + 512 * i, [128, 256], BF16) for i in range(2)]
        kds = [sb(SCR + 21504 + 1024 * i, [128, 4, 2, 64], BF16) for i in range(2)]
        w_in_v = w_in.rearrange("(c p) n -> p c n", p=128)
        jobs = []
        for cb in range(6):
            for tt in range(NT):
                jobs.append((cb, tt, len(jobs) % 4, len(jobs) % 2))
        loaded = set()

        def a2_load(cb):
            if cb in loaded or cb >= 10:
                return
            loaded.add(cb)
            P.dma("gpsimd", wb[cb % 2][:, :, :], w_in_v[:, :, cb * 256:(cb + 1) * 256], writes=[("wb", cb % 2)])

        def a2_M(job):
            cb, tt, bk, par = job
            a2_load(cb)
            wbuf = wb[cb % 2]

            def mm(e):
                last = None
                for c in range(16):
                    last = e.matmul(ps[:, bk, 0:256], lhsT=actT[:, c, tt * 128:(tt + 1) * 128], rhs=wbuf[:, c, :],
                                    start=(c == 0), stop=(c == 15))
                return last
            P.pe(mm, reads=[("wb", cb % 2), ("actT", tt, 0), ("actT", tt, 1)], writes=[("ps", bk)])

        def a2_post(job):
            cb, tt, bk, par = job
            if cb == 5:
                P.act(lambda e: e.activation(out=v1[:, tt + 1, :, 0:64], in_=ps[:, bk, 0:256].rearrange("p (a b) -> p a b", a=4),
                                             func=AF.Copy), reads=[("ps", bk)], writes=[("v1", tt)])
                return
            rA, rB, qr, kd = rAs[par], rBs[par], qrs[par], kds[par]
            pv = ps[:, bk, 0:256].rearrange("p (h t d) -> p h t d", h=4, t=2)
            rBv = rB[:, :].rearrange("p (h t d) -> p h t d", h=4, t=2)
            P.dve(lambda e: e.tensor_tensor(out=rA[:, :].rearrange("p (h t d) -> p h t d", h=4, t=2), in0=pv,
                                            in1=cap(cosT, tt * 32, [[512, 128], [0, 4], [0, 2], [1, 32]]), op=ALU.mult),
                  reads=[("ps", bk), "cosT"], writes=[("rA", par)])
            P.dve(lambda e: e.tensor_tensor(out=rBv[:, :, 0, :], in0=pv[:, :, 1, :],
                                            in1=cap(nsinT, tt * 32, [[512, 128], [0, 4], [1, 32]]), op=ALU.mult),
                  reads=[("ps", bk), "nsinT"], writes=[("rB0", par)])
            P.dve(lambda e: e.tensor_tensor(out=rBv[:, :, 1, :], in0=pv[:, :, 0, :],
                                            in1=cap(sinT, tt * 32, [[512, 128], [0, 4], [1, 32]]), op=ALU.mult),
                  reads=[("ps", bk), "sinT"], writes=[("rB1", par)])
            rk = [("rA", par), ("rB0", par), ("rB1", par)]
            if cb < 4:
                P.dve(lambda e: e.tensor_tensor(out=qr[:, :], in0=rA[:, :], in1=rB[:, :], op=ALU.add), reads=rk,
                      writes=[("qr", par)])
                tb2 = 4 + par

                def trq(e):
                    last = None
                    for j in range(2):
                        last = e.transpose(out=psb[:, tb2, j * 128:(j + 1) * 128], in_=qr[:, j * 128:(j + 1) * 128],
                                           identity=identb[:, :])
                    return last
                P.pe(trq, reads=[("qr", par)], writes=[("ps", tb2)])
                P.act(lambda e: e.activation(out=qT[:, 2 * cb:2 * cb + 2, tt * 128:(tt + 1) * 128],
                                             in_=psb[:, tb2, 0:256].rearrange("p (a b) -> p a b", a=2), func=AF.Copy),
                      reads=[("ps", tb2)], writes=[("qT", cb, tt)])
            else:
                for dup in range(2):
                    P.dve(lambda e, dup=dup: e.tensor_tensor(out=kd[:, :, dup, :], in0=rA[:, :].rearrange("p (h d) -> p h d", h=4),
                                                             in1=rB[:, :].rearrange("p (h d) -> p h d", h=4), op=ALU.add),
                          reads=rk, writes=[("kd", par, dup)])
                tb2 = 6 + par

                def trk(e):
                    last = None
                    for j in range(4):
                        last = e.transpose(out=psb[:, tb2, j * 128:(j + 1) * 128],
                                           in_=kd[:, j, :, :].rearrange("p a b -> p (a b)"), identity=identb[:, :])
                    return last
                P.pe(trk, reads=[("kd", par, 0), ("kd", par, 1)], writes=[("ps", tb2)])
                P.act(lambda e: e.activation(out=kT2[:, :, 128 + tt * 128:128 + (tt + 1) * 128],
                                             in_=psb[:, tb2, 0:512].rearrange("p (a b) -> p a b", a=4), func=AF.Copy),
                      reads=[("ps", tb2)], writes=[("kT2", tt)])

        a2_load(0)
        a2_load(1)
        a2_M(jobs[0])
        for ji in range(len(jobs)):
            if ji + 1 < len(jobs):
                a2_M(jobs[ji + 1])
            a2_post(jobs[ji])
            if jobs[ji][1] == NT - 1:
                a2_load(jobs[ji][0] + 2)
        pc = 0
        for cb in range(6, 10):
            a2_load(cb)
            wbuf = wb[cb % 2]
            for j in range(2):
                uc = (cb - 6) * 2 + j
                for tb in range(4):
                    bk = pc % 4
                    pc += 1

                    def mmu(e, j=j, tb=tb, bk=bk, wbuf=wbuf):
                        last = None
                        for c in range(16):
                            last = e.matmul(ps[:, bk, :], lhsT=wbuf[:, c, j * 128:(j + 1) * 128],
                                            rhs=actT[:, c, tb * 512:(tb + 1) * 512], start=(c == 0), stop=(c == 15))
                        return last
                    P.pe(mmu, reads=[("wb", cb % 2)], writes=[("ps", bk)])
                    if (tb % 2) == 0:
                        P.act(lambda e, uc=uc, tb=tb, bk=bk: e.activation(out=uT[:, uc, tb * 512:(tb + 1) * 512],
                                                                          in_=ps[:, bk, :], func=AF.Copy),
                              reads=[("ps", bk)], writes=[("uT", uc, tb // 2)])
                    else:
                        P.dve(lambda e, uc=uc, tb=tb, bk=bk: e.tensor_copy(out=uT[:, uc, tb * 512:(tb + 1) * 512],
                                                                           in_=ps[:, bk, :]),
                              reads=[("ps", bk)], writes=[("uT", uc, tb // 2)])
            a2_load(cb + 2)
        P.barrier()
        if STOP == 2:
            P.emit()
            return nc

        Pbs = [sb(SCR + 2048 * i, [128, 1024], BF16) for i in range(2)]
        PTs = [sb(SCR + 4096 + 2048 * i, [128, 4, 2, 128], BF16) for i in range(2)]
        attn = [sb(SCR + 8192 + 4096 * i, [128, 1024], F32) for i in range(2)]
        anbs = [sb(SCR + 16384 + 2048 * i, [128, 1024], BF16) for i in range(2)]
        gat = sb(SCR + 20480, [128, 1024], F32)
        ajunk = sb(SCR + 24576, [128, 1024], BF16)
        asts = [sb(SCR + 26624 + 64 * i, [128, 16], F32) for i in range(4)]
        P.dma("sync", gat[:, :], gvecs[4:5, 0:1024].partition_broadcast(128), writes=["gat"])
        aits = [(tt, j) for tt in range(min(NT, ATT_TILES)) for j in range(4)]

        def att_X(i):
            tt, j = aits[i]
            sbk = 2 * (i % 2)
            ast = asts[i % 4]
            Pb = Pbs[i % 2]
            ka = ("ast", i % 4)

            def qk(e):
                last = None
                for hh in range(4):
                    h = 4 * j + hh
                    pb = (h % 2) * 64
                    last = e.matmul(ps[:, sbk + hh % 2, (hh // 2) * 256:(hh // 2) * 256 + 256],
                                    lhsT=qT[pb:pb + 64, h // 2, tt * 128:(tt + 1) * 128],
                                    rhs=kT2[pb:pb + 64, j, tt * 128:tt * 128 + 256], start=True, stop=True)
                return last
            P.pe(qk, reads=[], writes=[("ps", sbk), ("ps", sbk + 1)])

        def att_X2(i):
            tt, j = aits[i]
            sbk = 2 * (i % 2)
            ast = asts[i % 4]
            Pb = Pbs[i % 2]
            ka = ("ast", i % 4)
            sc = ps[:, sbk:sbk + 2, :]
            P.dve(lambda e: e.tensor_reduce(out=ast[:, 0:1], in_=sc, axis=AX.XY, op=ALU.max, negate=True),
                  reads=[("ps", sbk), ("ps", sbk + 1)], writes=[ka])
            P.dve(lambda e: e.tensor_scalar(out=ast[:, 1:2], in0=ast[:, 0:1], scalar1=0.125, scalar2=ngs[:, j:j + 1],
                                            op0=ALU.mult, op1=ALU.min), reads=[ka, "ngs"], writes=[ka])
            P.act(lambda e: e.activation(out=Pb[:, :].rearrange("p (a b) -> p a b", a=2), in_=sc, func=AF.Exp,
                                         bias=ast[:, 1:2], scale=0.125),
                  reads=[("ps", sbk), ("ps", sbk + 1), ka], writes=[("Pb", i % 2)])
            P.act(lambda e: e.activation(out=ast[:, 4:8], in_=sm[:, C_SINK + 4 * j:C_SINK + 4 * j + 4], func=AF.Exp,
                                         bias=ast[:, 1:2], scale=1.0), reads=[ka], writes=[("ase", i % 4)])
            Pv = Pb[:, :].rearrange("p (h b k) -> p h b k", h=4, b=2)
            P.add("gpsimd", lambda e: e.affine_select(out=Pv[:, :, 0, :], in_=Pv[:, :, 0, :], pattern=[[0, 4], [1, 128]],
                                                      compare_op=ALU.is_gt, fill=0.0, base=0, channel_multiplier=-1),
                  reads=[("Pb", i % 2)], writes=[("Pb", i % 2)])
            P.add("gpsimd", lambda e: e.affine_select(out=Pv[:, :, 1, :], in_=Pv[:, :, 1, :], pattern=[[0, 4], [-1, 128]],
                                                      compare_op=ALU.is_ge, fill=0.0, base=0, channel_multiplier=1),
                  reads=[("Pb", i % 2)], writes=[("Pb", i % 2)])

        def att_Y1(i):
            Pb = Pbs[i % 2]
            PT = PTs[i % 2]
            tbk = 4 if i % 2 == 0 else 7

            def trp(e):
                last = None
                for k in range(8):
                    last = e.transpose(out=psb[:, tbk, k * 128:(k + 1) * 128], in_=Pb[:, k * 128:(k + 1) * 128],
                                       identity=identb[:, :])
                return last
            P.pe(trp, reads=[("Pb", i % 2)], writes=[("ps", tbk)])
            if i % 2 == 0:
                P.act(lambda e: e.activation(out=PT[:, :, :, :].rearrange("p h k q -> p (h k q)"), in_=psb[:, tbk, :],
                                             func=AF.Copy), reads=[("ps", tbk)], writes=[("PT", i % 2)])
            else:
                P.dve(lambda e: e.tensor_copy(out=PT[:, :, :, :].rearrange("p h k q -> p (h k q)"), in_=psb[:, tbk, :]),
                      reads=[("ps", tbk)], writes=[("PT", i % 2)])

        def att_Y2(i):
            tt, j = aits[i]
            ab = tt % 2
            PT = PTs[i % 2]
            ast = asts[i % 4]
            obk = 5 if i % 2 == 0 else 6
            tbk = 4 if i % 2 == 0 else 7

            def pv_(e):
                last = None
                for i4 in range(4):
                    hh = (0, 2, 1, 3)[i4]
                    for kb in range(2):
                        last = e.matmul(ps[:, obk, hh * 65:hh * 65 + 65], lhsT=PT[:, i4, kb, :], rhs=v1[:, tt + kb, j, :],
                                        start=(kb == 0), stop=(kb == 1))
                return last
            P.pe(pv_, reads=[("PT", i % 2)], writes=[("ps", obk)])
            po = ps[:, obk, 0:260].rearrange("p (h d) -> p h d", h=4)
            P.dve(lambda e: e.tensor_tensor(out=ast[:, 8:12], in0=po[:, :, 64], in1=ast[:, 4:8], op=ALU.add),
                  reads=[("ps", obk), ("ase", i % 4)], writes=[("aden", i % 4)])
            P.dve(lambda e: e.reciprocal(out=ast[:, 12:16], in_=ast[:, 8:12]), reads=[("aden", i % 4)], writes=[("ard", i % 4)])
            P.dve(lambda e: e.tensor_tensor(out=attn[ab][:, j * 256:(j + 1) * 256].rearrange("p (h d) -> p h d", h=4),
                                            in0=po[:, :, 0:64], in1=cap(ast, 12, [[16, 128], [1, 4], [0, 64]]), op=ALU.mult),
                  reads=[("ps", obk), ("ard", i % 4)], writes=[("attn", ab, j)])
            if j != 3:
                return
            ak = [("attn", ab, jj) for jj in range(4)]
            c0 = 32 + 8 * ab
            anb = anbs[ab]
            P.dve(lambda e: e.scalar_tensor_tensor(out=ajunk[:, :], in0=attn[ab][:, :], scalar=1.0, in1=attn[ab][:, :],
                                                   op0=ALU.mult, op1=ALU.mult, accum_out=stat[:, c0:c0 + 1]),
                  reads=ak, writes=["ajunk", ("st", c0)])
            rstd_from(("st", c0), stat[:, c0:c0 + 1], stat[:, c0 + 2:c0 + 3], ("st", c0 + 2), 1024, stat[:, c0 + 1:c0 + 2],
                      ("st", c0 + 1))
            P.dve(lambda e: e.scalar_tensor_tensor(out=anb[:, :], in0=attn[ab][:, :], scalar=stat[:, c0 + 2:c0 + 3],
                                                   in1=gat[:, :], op0=ALU.mult, op1=ALU.mult),
                  reads=ak + [("st", c0 + 2), "gat"], writes=[("anb", ab)])

            def tra(e):
                last = None
                for c in range(8):
                    last = e.transpose(out=psb[:, tbk, c * 128:(c + 1) * 128], in_=anb[:, c * 128:(c + 1) * 128],
                                       identity=identb[:, :])
                return last
            P.pe(tra, reads=[("anb", ab)], writes=[("ps", tbk)])
            P.act(lambda e: e.activation(out=actT[:, 0:8, tt * 128:(tt + 1) * 128],
                                         in_=psb[:, tbk, :].rearrange("p (a b) -> p a b", a=8), func=AF.Copy),
                  reads=[("ps", tbk)], writes=[("mixT", tt)])

        na = len(aits)
        att_X(0)
        att_X(1)
        att_X2(0)
        att_X(2)
        att_X2(1)
        att_Y1(0)
        for i in range(na):
            if i + 3 < na:
                att_X(i + 3)
            if i + 2 < na:
                att_X2(i + 2)
            if i + 1 < na:
                att_Y1(i + 1)
            att_Y2(i)
        P.barrier()
        if STOP == 3:
            P.emit()
            return nc

        R0 = 98304
        BbTr = sb(R0, [128, 8, 4, 128], BF16)
        BbTi = sb(R0 + 8192, [128, 8, 4, 128], BF16)
        CTr = sb(R0 + 16384, [128, 32, 128], BF16)
        CTi = sb(R0 + 24576, [128, 32, 128], BF16)
        tok = sb(R0 + 32768, [128, 2048], F32)
        fre = sb(49152, [128, 1024], F32)
        fim = sb(53248, [128, 1024], F32)
        lrB = sb(57344, [128, 1024], F32)
        liB = sb(61440, [128, 1024], F32)
        LBr = sb(R0 + 40960, [128, 8, 4, 128], BF16)
        LBi = sb(R0 + 49152, [128, 8, 4, 128], BF16)
        K1blk = sb(R0 + 57344, [128, 8, 128], BF16)
        CIr = sb(32768, [128, 32, 128], BF16)
        CIi = sb(40960, [128, 32, 128], BF16)
        S_ = [sb(SCR + 4096 * i, [128, 1024], F32) for i in range(10)]
        P.dma("sync", tok[:, :], tokc, writes=["tok"])
        P.act(lambda e: e.activation(out=stat[:, 32:64], in_=sm[:, C_LDA:C_LDA + 32], func=AF.Exp), reads=[], writes=["dtA"])
        P.dve(lambda e: e.tensor_tensor(out=rho[:, :], in0=sm[:, C_ARA:C_ARA + 32], in1=stat[:, 32:64], op=ALU.mult),
              reads=["dtA"], writes=["rho0"])
        P.act(lambda e: e.activation(out=rho[:, :], in_=rho[:, :], func=AF.Exp), reads=["rho0"], writes=["rho"])
        P.dve(lambda e: e.scalar_tensor_tensor(out=thp[:, :], in0=sm[:, C_AIA:C_AIA + 32], scalar=INV2PI, in1=stat[:, 32:64],
                                               op0=ALU.mult, op1=ALU.mult), reads=["dtA"], writes=["thp"])
        AR, AI, LD = S_[0], S_[1], S_[2]
        for i, t_ in enumerate((AR, AI, LD)):
            P.dma("sync", t_[:, :], lb3[:, i, :], writes=[("S", i)])
        P.act(lambda e: e.activation(out=LD[:, :], in_=LD[:, :], func=AF.Exp), reads=[("S", 2)], writes=[("S", 2)])
        P.dve(lambda e: e.tensor_tensor(out=S_[3][:, :], in0=AR[:, :], in1=LD[:, :], op=ALU.mult), reads=[("S", 0), ("S", 2)],
              writes=[("S", 3)])
        P.act(lambda e: e.activation(out=S_[3][:, :], in_=S_[3][:, :], func=AF.Exp), reads=[("S", 3)], writes=[("S", 3)])
        P.dve(lambda e: e.scalar_tensor_tensor(out=S_[4][:, :], in0=AI[:, :], scalar=INV2PI, in1=LD[:, :], op0=ALU.mult,
                                               op1=ALU.mult), reads=[("S", 1), ("S", 2)], writes=[("S", 4)])
        P.dve(lambda e: e.tensor_scalar(out=S_[5][:, :], in0=S_[4][:, :], scalar1=MAGIC, scalar2=MAGIC, op0=ALU.add,
                                        op1=ALU.subtract), reads=[("S", 4)], writes=[("S", 5)])
        P.dve(lambda e: e.tensor_tensor(out=S_[4][:, :], in0=S_[4][:, :], in1=S_[5][:, :], op=ALU.subtract),
              reads=[("S", 4), ("S", 5)], writes=[("S", 4)])
        P.dve(lambda e: e.scalar_tensor_tensor(out=S_[5][:, :], in0=S_[4][:, :], scalar=-1.0, in1=S_[4][:, :], op0=ALU.mult,
                                               op1=ALU.max), reads=[("S", 4)], writes=[("S", 5)])
        P.act(lambda e: e.activation(out=S_[6][:, :], in_=S_[4][:, :], func=AF.Sin, scale=TWO_PI), reads=[("S", 4)],
              writes=[("S", 6)])
        P.act(lambda e: e.activation(out=S_[7][:, :], in_=S_[5][:, :], func=AF.Sin, scale=-TWO_PI, bias=math.pi / 2),
              reads=[("S", 5)], writes=[("S", 7)])
        P.dve(lambda e: e.tensor_tensor(out=S_[6][:, :], in0=S_[6][:, :], in1=S_[3][:, :], op=ALU.mult),
              reads=[("S", 6), ("S", 3)], writes=[("S", 6)])
        P.dve(lambda e: e.tensor_tensor(out=S_[7][:, :], in0=S_[7][:, :], in1=S_[3][:, :], op=ALU.mult),
              reads=[("S", 7), ("S", 3)], writes=[("S", 7)])
        P.act(lambda e: e.activation(out=lrB[:, :], in_=S_[7][:, :], func=AF.Copy), reads=[("S", 7)], writes=["lrB"])
        P.act(lambda e: e.activation(out=liB[:, :], in_=S_[6][:, :], func=AF.Copy), reads=[("S", 6)], writes=["liB"])
        P.dve(lambda e: e.tensor_scalar(out=S_[7][:, :], in0=S_[7][:, :], scalar1=-1.0, scalar2=None, op0=ALU.add),
              reads=[("S", 7), "lrB"], writes=[("S", 7)])
        P.dve(lambda e: e.tensor_tensor(out=S_[3][:, :], in0=AR[:, :], in1=AR[:, :], op=ALU.mult), reads=[("S", 0)],
              writes=[("S", 3)])
        P.dve(lambda e: e.tensor_tensor(out=S_[4][:, :], in0=AI[:, :], in1=AI[:, :], op=ALU.mult), reads=[("S", 1)],
              writes=[("S", 4)])
        P.dve(lambda e: e.tensor_tensor(out=S_[3][:, :], in0=S_[3][:, :], in1=S_[4][:, :], op=ALU.add),
              reads=[("S", 3), ("S", 4)], writes=[("S", 3)])
        P.dve(lambda e: e.reciprocal(out=S_[3][:, :], in_=S_[3][:, :]), reads=[("S", 3)], writes=[("S", 3)])
        P.dve(lambda e: e.tensor_tensor(out=S_[4][:, :], in0=S_[7][:, :], in1=AR[:, :], op=ALU.mult),
              reads=[("S", 7), ("S", 0)], writes=[("S", 4)])
        P.dve(lambda e: e.tensor_tensor(out=S_[5][:, :], in0=S_[6][:, :], in1=AI[:, :], op=ALU.mult),
              reads=[("S", 6), ("S", 1)], writes=[("S", 5)])
        P.dve(lambda e: e.tensor_tensor(out=S_[4][:, :], in0=S_[4][:, :], in1=S_[5][:, :], op=ALU.add),
              reads=[("S", 4), ("S", 5)], writes=[("S", 4)])
        P.dve(lambda e: e.tensor_tensor(out=fre[:, :], in0=S_[4][:, :], in1=S_[3][:, :], op=ALU.mult),
              reads=[("S", 4), ("S", 3)], writes=["fre"])
        P.dve(lambda e: e.tensor_tensor(out=S_[4][:, :], in0=S_[6][:, :], in1=AR[:, :], op=ALU.mult),
              reads=[("S", 6), ("S", 0)], writes=[("S", 4)])
        P.dve(lambda e: e.tensor_tensor(out=S_[5][:, :], in0=S_[7][:, :], in1=AI[:, :], op=ALU.mult),
              reads=[("S", 7), ("S", 1)], writes=[("S", 5)])
        P.dve(lambda e: e.tensor_tensor(out=S_[4][:, :], in0=S_[4][:, :], in1=S_[5][:, :], op=ALU.subtract),
              reads=[("S", 4), ("S", 5)], writes=[("S", 4)])
        P.dve(lambda e: e.tensor_tensor(out=fim[:, :], in0=S_[4][:, :], in1=S_[3][:, :], op=ALU.mult),
              reads=[("S", 4), ("S", 3)], writes=["fim"])
        P.barrier()
        if STOP == 4:
            P.emit()
            return nc
        Bq = [sb(SCR + 4096 * i, [128, 1024], F32) for i in range(8)]
        Bcr, Bci, T1, T2, Bbr_, Bbi_, Lr_, Li_ = Bq
        P.dma("sync", Bcr[:, :], bexp[:, 0, :], writes=["Bcr"])
        P.dma("sync", Bci[:, :], bexp[:, 1, :], writes=["Bci"])

        def cmul(outr, outi, ar, ai, br, bi, kr, ki):
            P.dve(lambda e: e.tensor_tensor(out=T1[:, :], in0=ar[:, :], in1=br[:, :], op=ALU.mult), reads=kr, writes=["T1"])
            P.dve(lambda e: e.tensor_tensor(out=T2[:, :], in0=ai[:, :], in1=bi[:, :], op=ALU.mult), reads=kr, writes=["T2"])
            P.dve(lambda e: e.tensor_tensor(out=outr[:, :], in0=T1[:, :], in1=T2[:, :], op=ALU.subtract), reads=["T1", "T2"],
                  writes=[ki + "r"])
            P.dve(lambda e: e.tensor_tensor(out=T1[:, :], in0=ar[:, :], in1=bi[:, :], op=ALU.mult), reads=kr + [ki + "r"],
                  writes=["T1"])
            P.dve(lambda e: e.tensor_tensor(out=T2[:, :], in0=ai[:, :], in1=br[:, :], op=ALU.mult), reads=kr + [ki + "r"],
                  writes=["T2"])
            P.dve(lambda e: e.tensor_tensor(out=outi[:, :], in0=T1[:, :], in1=T2[:, :], op=ALU.add), reads=["T1", "T2"],
                  writes=[ki + "i"])
        cmul(Bbr_, Bbi_, fre, fim, Bcr, Bci, ["Bcr", "Bci", "fre", "fim"], "Bb")
        cmul(Lr_, Li_, lrB, liB, Bbr_, Bbi_, ["Bbr", "Bbi", "lrB", "liB"], "L")
        for src, dst, k in ((Bbr_, BbTr, "Bbr"), (Bbi_, BbTi, "Bbi"), (Lr_, LBr, "Lr"), (Li_, LBi, "Li")):
            for a in range(4):
                P.dve(lambda e, src=src, dst=dst, a=a: e.tensor_scalar(
                    out=dst[:, :, a, :], in0=src[:, :].rearrange("p (k n) -> p k n", k=8), scalar1=maskA[:, a:a + 1],
                    scalar2=None, op0=ALU.mult), reads=[k, "maskA"], writes=[("exp", k, a)])
        P.barrier()
        if STOP == 5:
            P.emit()
            return nc
        cA = sb(SCR + 40960, [128, 32], F32)
        sA = sb(SCR + 40960 + 128, [128, 32], F32)
        tA = sb(SCR + 40960 + 256, [128, 32], F32)
        uA = sb(SCR + 40960 + 384, [128, 32], F32)
        rho2 = stat[:, 32:64]
        P.dve(lambda e: e.tensor_scalar(out=tA[:, :], in0=thp[:, :], scalar1=MAGIC, scalar2=MAGIC, op0=ALU.add,
                                        op1=ALU.subtract), reads=[], writes=["tA"])
        P.dve(lambda e: e.tensor_tensor(out=tA[:, :], in0=thp[:, :], in1=tA[:, :], op=ALU.subtract), reads=["tA"],
              writes=["tA"])
        P.dve(lambda e: e.scalar_tensor_tensor(out=uA[:, :], in0=tA[:, :], scalar=-1.0, in1=tA[:, :], op0=ALU.mult,
                                               op1=ALU.max), reads=["tA"], writes=["uA"])
        P.act(lambda e: e.activation(out=sA[:, :], in_=tA[:, :], func=AF.Sin, scale=TWO_PI), reads=["tA"], writes=["sA"])
        P.act(lambda e: e.activation(out=cA[:, :], in_=uA[:, :], func=AF.Sin, scale=-TWO_PI, bias=magp[:, 2:3]),
              reads=["uA"], writes=["cA"])
        P.dve(lambda e: e.reciprocal(out=uA[:, :], in_=rho[:, :]), reads=["cA"], writes=["uA"])
        P.dve(lambda e: e.tensor_tensor(out=cA[:, :], in0=cA[:, :], in1=uA[:, :], op=ALU.mult), reads=["cA", "uA"],
              writes=["cA"])
        P.dve(lambda e: e.tensor_tensor(out=sA[:, :], in0=sA[:, :], in1=uA[:, :], op=ALU.mult), reads=["sA", "uA"],
              writes=["sA"])
        P.dve(lambda e: e.tensor_tensor(out=rho2, in0=rho[:, :], in1=rho[:, :], op=ALU.mult), reads=[], writes=["rho2"])
        Cre = sb(SCR, [128, 16, 128], F32)
        Cim = sb(SCR + 8192, [128, 16, 128], F32)
        U1 = sb(SCR + 16384, [128, 16, 128], F32)
        U2 = sb(SCR + 24576, [128, 16, 128], F32)
        for hp in range(2):
            psl = slice(16 * hp, 16 * hp + 16)
            P.dma("sync", Cre[:, :, :], cexp[:, 0, hp * 2048:(hp + 1) * 2048].rearrange("p (a b) -> p a b", a=16),
                  writes=["Cre"])
            P.dma("sync", Cim[:, :, :], cexp[:, 1, hp * 2048:(hp + 1) * 2048].rearrange("p (a b) -> p a b", a=16),
                  writes=["Cim"])
            P.act(lambda e, psl=psl: e.activation(out=CTr[:, psl, :], in_=Cre[:, :, :], func=AF.Copy), reads=["Cre"],
                  writes=["CTr"])
            P.act(lambda e, psl=psl: e.activation(out=CTi[:, psl, :], in_=Cim[:, :, :], func=AF.Copy, scale=-1.0),
                  reads=["Cim"], writes=["CTi"])
            cAb = cap(cA, 16 * hp, [[32, 128], [1, 16], [0, 128]])
            sAb = cap(sA, 16 * hp, [[32, 128], [1, 16], [0, 128]])
            P.dve(lambda e, cAb=cAb: e.tensor_tensor(out=U1[:, :, :], in0=Cre[:, :, :], in1=cAb, op=ALU.mult),
                  reads=["Cre", "cA"], writes=["U1"])
            P.dve(lambda e, sAb=sAb: e.tensor_tensor(out=U2[:, :, :], in0=Cim[:, :, :], in1=sAb, op=ALU.mult),
                  reads=["Cim", "sA"], writes=["U2"])
            P.dve(lambda e, psl=psl: e.tensor_tensor(out=CIr[:, psl, :], in0=U1[:, :, :], in1=U2[:, :, :], op=ALU.add),
                  reads=["U1", "U2"], writes=["CIr"])
            P.dve(lambda e, sAb=sAb: e.tensor_tensor(out=U1[:, :, :], in0=Cre[:, :, :], in1=sAb, op=ALU.mult),
                  reads=["Cre", "sA", "CIr"], writes=["U1"])
            P.dve(lambda e, cAb=cAb: e.tensor_tensor(out=U2[:, :, :], in0=Cim[:, :, :], in1=cAb, op=ALU.mult),
                  reads=["Cim", "cA", "CIr"], writes=["U2"])
            P.dve(lambda e, psl=psl: e.tensor_tensor(out=CIi[:, psl, :], in0=U1[:, :, :], in1=U2[:, :, :], op=ALU.subtract),
                  reads=["U1", "U2"], writes=["CIi"])
        Xs = [sb(SCR + 32768 + 2048 * i, [128, 8, 128], BF16) for i in range(2)]
        for blk in range(8):
            xb = Xs[blk % 2]
            tbk = 2 * (blk % 2)

            def trx(e, blk=blk, tbk=tbk):
                last = None
                for a in range(4):
                    for ri, Bt in enumerate((BbTr, BbTi)):
                        k = 2 * a + ri
                        last = e.transpose(out=psb[:, tbk, k * 128:(k + 1) * 128], in_=Bt[:, blk, a, :], identity=identb[:, :])
                return last
            P.pe(trx, reads=["BbTr", "BbTi"], writes=[("ps", tbk)])
            P.act(lambda e, xb=xb, tbk=tbk: e.activation(out=xb[:, :, :], in_=psb[:, tbk, :].rearrange("p (a b) -> p a b", a=8),
                                                         func=AF.Copy), reads=[("ps", tbk)], writes=[("Xs", blk % 2)])

            def mk1(e, blk=blk, xb=xb, tbk=tbk):
                last = None
                for a in range(4):
                    pp = 4 * blk + a
                    for ri, Ct in enumerate((CIr, CIi)):
                        k = 2 * a + ri
                        last = e.matmul(ps[:, tbk + 1, 0:128], lhsT=xb[:, k, :], rhs=Ct[:, pp, :], start=(k == 0), stop=(k == 7))
                return last
            P.pe(mk1, reads=[("Xs", blk % 2), "CIr", "CIi"], writes=[("ps", tbk + 1)])
            P.act(lambda e, blk=blk, tbk=tbk: e.activation(out=K1blk[:, blk, :], in_=ps[:, tbk + 1, 0:128], func=AF.Copy,
                                                           scale=-1.0), reads=[("ps", tbk + 1)], writes=["K1blk"])
        P.barrier()
        if STOP == 6:
            P.emit()
            return nc

        NCH = 512

        def sl2(off, n, dt):
            return [sb(SCR + off + n * i, [128, NCH], dt) for i in range(2)]
        yqs = sl2(0, 2048, F32)
        kfqs = sl2(4096, 2048, F32)
        SINfs = sl2(8192, 2048, F32)
        COSfs = sl2(12288, 2048, F32)
        tb16 = [[sb(SCR + 16384 + 1024 * (3 * s_ + k), [128, NCH], BF16) for k in range(3)] for s_ in range(2)]
        pbuf = [[sb(SCR + 22528 + 1024 * (4 * s_ + k), [128, NCH], BF16) for k in range(4)] for s_ in range(2)]
        Rre = sb(SCR + 30720, [128, NCH], BF16)
        Rim = sb(SCR + 31744, [128, NCH], BF16)
        qbufs = [[sb(SCR + 32768 + 1024 * (4 * s_ + k), [128, NCH], BF16) for k in range(4)] for s_ in range(2)]
        ysb = sb(49152, [128, 1024], F32)
        gtmp = sb(53248, [128, 1024], F32)
        gsig = [sb(57344 + 2048 * i, [128, 1024], BF16) for i in range(2)]

        iters = [(blk, half, a) for blk in range(8) for half in range(2) for a in range(4)]

        def eo(ap_, which):
            return ap_.rearrange("p (c t) -> p c t", t=2)[:, :, which]

        def stageA0(idx):
            blk, half, a = iters[idx]
            s_ = idx % 2
            pp = 4 * blk + a
            yq, kfq = yqs[s_], kfqs[s_]
            ky, kk = ("yq", s_), ("kfq", s_)
            tokv = eo(tok[:, half * 1024:(half + 1) * 1024], 1)
            P.act(lambda e: e.activation(out=yq[:, :], in_=tokv, func=AF.Copy, scale=thp[:, pp:pp + 1]),
                  reads=["tok", "thp"], writes=[ky])
            P.act(lambda e: e.activation(out=kfq[:, :], in_=yq[:, :], func=AF.Identity, bias=magp[:, 0:1], scale=1.0),
                  reads=[ky, "magp"], writes=[kk])
            P.act(lambda e: e.activation(out=kfq[:, :], in_=kfq[:, :], func=AF.Identity, bias=magp[:, 1:2], scale=1.0),
                  reads=[kk, "magp"], writes=[kk])
            P.add("gpsimd", lambda e: e.tensor_tensor(out=yq[:, :], in0=yq[:, :], in1=kfq[:, :], op=ALU.subtract),
                  reads=[ky, kk], writes=[ky])

        def stageA(idx):
            blk, half, a = iters[idx]
            s_ = idx % 2
            pp = 4 * blk + a
            ukey = ("uT", blk, half)
            SINb, NSINb, COSb = tb16[s_]
            pb = pbuf[s_]
            yq, kfq, SINf, COSf = yqs[s_], kfqs[s_], SINfs[s_], COSfs[s_]
            ky, kk, ksf, kcf = ("yq", s_), ("kfq", s_), ("SINf", s_), ("COSf", s_)
            b0 = 2 * s_
            ue = eo(uT[:, blk, half * 1024:(half + 1) * 1024], 0)
            uo = eo(uT[:, blk, half * 1024:(half + 1) * 1024], 1)

            def bu(e):
                last = None
                for ri, (Lt, Bt) in enumerate(((LBr, BbTr), (LBi, BbTi))):
                    e.matmul(ps[:, b0 + ri, :], lhsT=Lt[:, blk, a, :], rhs=ue, start=True, stop=False)
                    last = e.matmul(ps[:, b0 + ri, :], lhsT=Bt[:, blk, a, :], rhs=uo, start=False, stop=True)
                return last
            P.pe(bu, reads=[ukey], writes=[("ps", b0), ("ps", b0 + 1)])
            P.act(lambda e: e.activation(out=kfq[:, :], in_=yq[:, :], func=AF.Abs), reads=[ky], writes=[kk])
            P.act(lambda e: e.activation(out=SINf[:, :], in_=yq[:, :], func=AF.Sin, scale=TWO_PI), reads=[ky], writes=[ksf])
            P.act(lambda e: e.activation(out=COSf[:, :], in_=kfq[:, :], func=AF.Sin, scale=-TWO_PI, bias=magp[:, 2:3]),
                  reads=[kk, "magp"], writes=[kcf])
            P.act(lambda e: e.activation(out=SINb[:, :], in_=yq[:, :], func=AF.Sin, scale=TWO_PI), reads=[ky],
                  writes=[("SINb", s_)])
            P.act(lambda e: e.activation(out=NSINb[:, :], in_=yq[:, :], func=AF.Sin, scale=-TWO_PI), reads=[ky],
                  writes=[("NSINb", s_)])
            P.act(lambda e: e.activation(out=COSb[:, :], in_=kfq[:, :], func=AF.Sin, scale=-TWO_PI, bias=magp[:, 2:3]),
                  reads=[kk, "magp"], writes=[("COSb", s_)])
            bre = ps[:, b0, :]
            bim = ps[:, b0 + 1, :]
            P.dve(lambda e: e.tensor_tensor(out=pb[0][:, :], in0=bre, in1=COSf[:, :], op=ALU.mult),
                  reads=[("ps", b0), kcf], writes=[("p", s_, 0)])
            P.dve(lambda e: e.tensor_tensor(out=pb[1][:, :], in0=bim, in1=SINf[:, :], op=ALU.mult),
                  reads=[("ps", b0 + 1), ksf], writes=[("p", s_, 1)])
            P.dve(lambda e: e.tensor_tensor(out=pb[2][:, :], in0=bim, in1=COSf[:, :], op=ALU.mult),
                  reads=[("ps", b0 + 1), kcf], writes=[("p", s_, 2)])
            P.dve(lambda e: e.scalar_tensor_tensor(out=pb[3][:, :], in0=bre, scalar=-1.0, in1=SINf[:, :], op0=ALU.mult,
                                                   op1=ALU.mult), reads=[("ps", b0), ksf], writes=[("p", s_, 3)])

        def stageB(idx):
            blk, half, a = iters[idx]
            s_ = idx % 2
            pp = 4 * blk + a
            SINb, NSINb, COSb = tb16[s_]
            pb = pbuf[s_]
            qbuf = qbufs[s_]
            rb = cap(stat, 32 + pp, [[64, 128], [0, NCH]])
            i0 = rlast[:, pp, 0:1] if half == 1 else 0.0
            i1 = rlast[:, pp, 1:2] if half == 1 else 0.0

            def addE(k0, bank):
                def f(e):
                    e.matmul(ps[:, bank, :], lhsT=identb[:, :], rhs=pb[k0][:, :], start=True, stop=False)
                    return e.matmul(ps[:, bank, :], lhsT=identb[:, :], rhs=pb[k0 + 1][:, :], start=False, stop=True)
                return f
            P.pe(addE(0, 6), reads=[("p", s_, 0), ("p", s_, 1)], writes=[("ps", 6)])
            P.pe(addE(2, 7), reads=[("p", s_, 2), ("p", s_, 3)], writes=[("ps", 7)])
            P.dve(lambda e: e.tensor_tensor_scan(out=Rre[:, :], data0=rb, data1=ps[:, 6, :], initial=i0, op0=ALU.mult,
                                                 op1=ALU.add), reads=[("ps", 6), "rho2", ("rl", pp)], writes=["Rre"])
            P.dve(lambda e: e.tensor_tensor(out=qbuf[0][:, :], in0=Rre[:, :], in1=COSb[:, :], op=ALU.mult),
                  reads=["Rre", ("COSb", s_)], writes=[("q", s_, 0)])
            P.dve(lambda e: e.tensor_tensor(out=qbuf[3][:, :], in0=Rre[:, :], in1=SINb[:, :], op=ALU.mult),
                  reads=["Rre", ("SINb", s_)], writes=[("q", s_, 3)])
            P.dve(lambda e: e.tensor_tensor_scan(out=Rim[:, :], data0=rb, data1=ps[:, 7, :], initial=i1, op0=ALU.mult,
                                                 op1=ALU.add), reads=[("ps", 7), "rho2", ("rl", pp)], writes=["Rim"])
            P.dve(lambda e: e.tensor_tensor(out=qbuf[1][:, :], in0=Rim[:, :], in1=NSINb[:, :], op=ALU.mult),
                  reads=["Rim", ("NSINb", s_)], writes=[("q", s_, 1)])
            P.dve(lambda e: e.tensor_tensor(out=qbuf[2][:, :], in0=Rim[:, :], in1=COSb[:, :], op=ALU.mult),
                  reads=["Rim", ("COSb", s_)], writes=[("q", s_, 2)])
            if half == 0:
                P.dve(lambda e: e.tensor_copy(out=rlast[:, pp, 0:1], in_=Rre[:, NCH - 1:NCH]), reads=["Rre"],
                      writes=[("rl", pp)])
                P.dve(lambda e: e.tensor_copy(out=rlast[:, pp, 1:2], in_=Rim[:, NCH - 1:NCH]), reads=["Rim", ("rl", pp)],
                      writes=[("rl", pp)])

        def stageC(idx):
            blk, half, a = iters[idx]
            s_ = idx % 2
            pp = 4 * blk + a
            ukey = ("uT", blk, half)
            usl = uT[:, blk, half * 1024:(half + 1) * 1024]
            qbuf = qbufs[s_]

            def cp(e):
                if a == 0:
                    e.matmul(ps[:, 5, :], lhsT=K1blk[:, blk, :], rhs=eo(usl, 1), start=True, stop=False)
                e.matmul(ps[:, 4, :], lhsT=CTr[:, pp, :], rhs=qbuf[0][:, :], start=(a == 0), stop=False)
                e.matmul(ps[:, 4, :], lhsT=CTr[:, pp, :], rhs=qbuf[1][:, :], start=False, stop=False)
                e.matmul(ps[:, 4, :], lhsT=CTi[:, pp, :], rhs=qbuf[2][:, :], start=False, stop=False)
                e.matmul(ps[:, 4, :], lhsT=CTi[:, pp, :], rhs=qbuf[3][:, :], start=False, stop=(a == 3))
                e.matmul(ps[:, 5, :], lhsT=CIr[:, pp, :], rhs=qbuf[0][:, :], start=False, stop=False)
                e.matmul(ps[:, 5, :], lhsT=CIr[:, pp, :], rhs=qbuf[1][:, :], start=False, stop=False)
                e.matmul(ps[:, 5, :], lhsT=CIi[:, pp, :], rhs=qbuf[2][:, :], start=False, stop=False)
                return e.matmul(ps[:, 5, :], lhsT=CIi[:, pp, :], rhs=qbuf[3][:, :], start=False, stop=(a == 3))
            P.pe(cp, reads=[("q", s_, k) for k in range(4)] + [ukey], writes=[("ps", 4), ("ps", 5)])
            if a != 3:
                return
            for which, bank in ((1, 4), (0, 5)):
                P.dve(lambda e, which=which, bank=bank: e.scalar_tensor_tensor(
                    out=eo(usl, which), in0=eo(usl, which), scalar=sm[:, C_DSK + blk:C_DSK + blk + 1], in1=ps[:, bank, :],
                    op0=ALU.mult, op1=ALU.add), reads=[ukey, ("ps", bank)], writes=[ukey])

        stageA0(0)
        stageA0(1)
        stageA(0)
        for idx in range(len(iters)):
            if idx + 2 < len(iters):
                stageA0(idx + 2)
            if idx + 1 < len(iters):
                stageA(idx + 1)
            stageB(idx)
            if idx >= 1:
                stageC(idx - 1)
        stageC(len(iters) - 1)
        wg = sb(SCR + 24576, [128, 8, 1024], BF16)
        P.dma("gpsimd", wg[:, :, :], w_glu.rearrange("(c p) n -> p c n", p=128),
              writes=["wg", "Rre", "Rim"] + [("p", s_, k) for s_ in range(2) for k in range(4)]
              + [("q", s_, k) for s_ in range(2) for k in range(4)])
        gpieces = [(blk, half) for blk in range(8) for half in range(2)]

        def gel_a(gi):
            blk, half = gpieces[gi]
            tsl = slice(half * 1024, (half + 1) * 1024)
            ukey = ("uT", blk, half)
            g_ = (ysb, gtmp)[gi % 2]
            gs = gsig[gi % 2]
            gk, gsk = ("gel", gi % 2), ("gsig", gi % 2)
            P.dve(lambda e: e.tensor_tensor(out=g_[:, :], in0=uT[:, blk, tsl], in1=uT[:, blk, tsl], op=ALU.mult),
                  reads=[ukey], writes=[gk])
            P.dve(lambda e: e.tensor_scalar(out=g_[:, :], in0=g_[:, :], scalar1=0.044715, scalar2=1.0, op0=ALU.mult,
                                            op1=ALU.add), reads=[gk], writes=[gk])
            P.dve(lambda e: e.tensor_tensor(out=g_[:, :], in0=g_[:, :], in1=uT[:, blk, tsl], op=ALU.mult),
                  reads=[gk, ukey], writes=[gk])
            P.act(lambda e: e.activation(out=gs[:, :], in_=g_[:, :], func=AF.Sigmoid, scale=GELU_C), reads=[gk],
                  writes=[gsk])

        def gel_b(gi):
            blk, half = gpieces[gi]
            tsl = slice(half * 1024, (half + 1) * 1024)
            ukey = ("uT", blk, half)
            gs = gsig[gi % 2]
            P.dve(lambda e: e.tensor_tensor(out=uT[:, blk, tsl], in0=gs[:, :], in1=uT[:, blk, tsl], op=ALU.mult),
                  reads=[("gsig", gi % 2), ukey], writes=[ukey])

        gel_a(0)
        for gi in range(len(gpieces)):
            if gi + 1 < len(gpieces):
                gel_a(gi + 1)
            gel_b(gi)
        P.barrier()
        if STOP == 7:
            P.emit()
            return nc

        sg = [sb(SCR + 2048 * i, [128, 512], F32) for i in range(2)]
        ssm = [sb(SCR + 4096 + 2048 * i, [128, 512], F32) for i in range(2)]
        sqb = [sb(SCR + 8192 + 1024 * i, [128, 512], BF16) for i in range(2)]
        rbc = sb(SCR + 12288, [128, 2048], F32)
        ones128 = sb(SCR + 20480, [128, 128], BF16)
        Wo = sb(R0, [128, 16, 2048], BF16)
        w_o_v = w_o.rearrange("(c p) n -> p c n", p=128)
        for q4 in range(4):
            P.dma("gpsimd", Wo[:, 4 * q4:4 * q4 + 4, :], w_o_v[:, 4 * q4:4 * q4 + 4, :], writes=[("Wo", q4)])
        P.dve(lambda e: e.memset(ones128[:, :], 1.0), writes=["ones128"])
        glu_it = [(e8, tb) for e8 in range(8) for tb in range(4)]

        def glu_mm(i):
            e8, tb = glu_it[i]
            bk = i % 4

            def mg(e):
                last = None
                for c in range(8):
                    last = e.matmul(ps[:, bk, :], lhsT=wg[:, c, e8 * 128:(e8 + 1) * 128],
                                    rhs=uT[:, c, tb * 512:(tb + 1) * 512], start=(c == 0), stop=(c == 7))
                return last
            P.pe(mg, reads=["wg"], writes=[("ps", bk)])

        def glu_ew(i):
            e8, tb = glu_it[i]
            bk = i % 4
            b2 = i % 2
            P.act(lambda e: e.activation(out=sg[b2][:, :], in_=ps[:, bk, :], func=AF.Sigmoid,
                                         bias=sm[:, C_BGLU + e8:C_BGLU + e8 + 1], scale=1.0),
                  reads=[("ps", bk)], writes=[("sg", b2)])
            P.dve(lambda e: e.tensor_tensor(out=ssm[b2][:, :], in0=uT[:, e8, tb * 512:(tb + 1) * 512], in1=sg[b2][:, :],
                                            op=ALU.mult), reads=[("sg", b2)], writes=[("ssm", b2)])
            P.dve(lambda e: e.tensor_tensor(out=sqb[b2][:, :], in0=ssm[b2][:, :], in1=ssm[b2][:, :], op=ALU.mult),
                  reads=[("ssm", b2)], writes=[("sqb", b2)])
            P.dve(lambda e: e.tensor_scalar(out=actT[:, 8 + e8, tb * 512:(tb + 1) * 512], in0=ssm[b2][:, :],
                                            scalar1=sm[:, C_GSSM + e8:C_GSSM + e8 + 1], scalar2=None, op0=ALU.mult),
                  reads=[("ssm", b2)], writes=[("mixS", e8)])

        def glu_sq(i):
            e8, tb = glu_it[i]
            b2 = i % 2
            P.pe(lambda e: e.matmul(ps[:, 4 + tb, :], lhsT=ones128[:, :], rhs=sqb[b2][:, :], start=(e8 == 0), stop=(e8 == 7)),
                 reads=[("sqb", b2), "ones128"], writes=[("ps", 4 + tb)])

        glu_mm(0)
        for i in range(len(glu_it)):
            glu_ew(i)
            if i + 1 < len(glu_it):
                glu_mm(i + 1)
            glu_sq(i)
        P.dve(lambda e: e.tensor_scalar(out=rbc[:, :].rearrange("p (a b) -> p a b", a=4), in0=ps[:, 4:8, :], scalar1=1.0 / 1024,
                                        scalar2=EPS, op0=ALU.mult, op1=ALU.add), reads=[("ps", 4 + t) for t in range(4)],
              writes=["rbc"])
        P.act(lambda e: e.activation(out=rbc[:, :], in_=rbc[:, :], func=AF.Ln), reads=["rbc"], writes=["rbc"])
        P.act(lambda e: e.activation(out=rbc[:, :], in_=rbc[:, :], func=AF.Exp, scale=-0.5), reads=["rbc"], writes=["rbc"])
        for e8 in range(8):
            P.dve(lambda e, e8=e8: e.tensor_tensor(out=actT[:, 8 + e8, :], in0=actT[:, 8 + e8, :], in1=rbc[:, :], op=ALU.mult),
                  reads=["rbc", ("mixS", e8)], writes=[("mixS", e8)])
        P.barrier()
        if STOP == 8:
            P.emit()
            return nc

        gpm = sb(65536, [128, 2048], F32)
        gpf = sb(65536 + 8192, [128, 2048], F32)
        xt2 = [sb(65536 + 16384 + 8192 * i, [128, 2048], F32) for i in range(2)]
        Abuf = [sb(SCR + 8192 * i, [128, 2048], F32) for i in range(2)]
        hnb = [sb(SCR + 16384 + 4096 * i, [128, 2048], BF16) for i in range(2)]
        ojb = sb(SCR + 24576, [128, 2048], BF16)
        P.dma("sync", gpm[:, :], gvecs[1:2, :].partition_broadcast(128), writes=["gpm"])
        P.dma("sync", gpf[:, :], gvecs[2:3, :].partition_broadcast(128), writes=["gpf"])

        def v4(ap_):
            return ap_.rearrange("p (a b) -> p a b", a=4)

        def wo_mm(tt):
            tsl = slice(tt * 128, (tt + 1) * 128)
            bset = 4 * (tt % 2)

            def mo(e):
                last = None
                for cbk in range(4):
                    for c in range(16):
                        last = e.matmul(ps[:, bset + cbk, :], lhsT=actT[:, c, tsl], rhs=Wo[:, c, cbk * 512:(cbk + 1) * 512],
                                        start=(c == 0), stop=(c == 15))
                return last
            P.pe(mo, reads=[("act", tt)], writes=[("ps", bset + i) for i in range(4)])

        def wo_post(tt):
            tsl = slice(tt * 128, (tt + 1) * 128)
            s_ = tt % 2
            bset = 4 * s_
            c0 = 16 + 8 * s_
            A = Abuf[s_]
            pk = [("ps", bset + i) for i in range(4)]
            acc = ps[:, bset:bset + 4, :]
            P.dma("sync", xt2[s_][:, :], x[tsl, :], writes=[("xt2", s_)])
            P.act(lambda e: e.activation(out=v4(A[:, :]), in_=acc, func=AF.Copy), reads=pk, writes=[("A", s_)])
            P.dve(lambda e: e.scalar_tensor_tensor(out=ojb[:, :], in0=A[:, :], scalar=1.0, in1=A[:, :], op0=ALU.mult,
                                                   op1=ALU.mult, accum_out=stat[:, c0:c0 + 1]),
                  reads=[("A", s_)], writes=["oj", ("st", c0)])
            rstd_from(("st", c0), stat[:, c0:c0 + 1], stat[:, c0 + 2:c0 + 3], ("st", c0 + 2), D, stat[:, c0 + 1:c0 + 2],
                      ("st", c0 + 1))
            P.dve(lambda e: e.scalar_tensor_tensor(out=A[:, :], in0=A[:, :], scalar=stat[:, c0 + 2:c0 + 3], in1=gpm[:, :],
                                                   op0=ALU.mult, op1=ALU.mult), reads=[("A", s_), ("st", c0 + 2), "gpm"],
                  writes=[("A", s_)])
            P.dve(lambda e: e.tensor_tensor(out=A[:, :], in0=A[:, :], in1=xt2[s_][:, :], op=ALU.add),
                  reads=[("A", s_), ("xt2", s_)], writes=[("A", s_)])
            P.dma("sync", hscr[tsl, :], A[:, :], reads=[("A", s_)], writes=[("hscr", tt)])
            P.dve(lambda e: e.scalar_tensor_tensor(out=ojb[:, :], in0=A[:, :], scalar=1.0, in1=A[:, :], op0=ALU.mult,
                                                   op1=ALU.mult, accum_out=stat[:, c0 + 3:c0 + 4]),
                  reads=[("A", s_)], writes=["oj", ("st", c0 + 3)])
            rstd_from(("st", c0 + 3), stat[:, c0 + 3:c0 + 4], stat[:, c0 + 5:c0 + 6], ("st", c0 + 5), D,
                      stat[:, c0 + 4:c0 + 5], ("st", c0 + 4))
            P.dve(lambda e: e.scalar_tensor_tensor(out=hnb[s_][:, :], in0=A[:, :], scalar=stat[:, c0 + 5:c0 + 6], in1=gpf[:, :],
                                                   op0=ALU.mult, op1=ALU.mult), reads=[("A", s_), ("st", c0 + 5), "gpf"],
                  writes=[("hnb", s_)])

            def trh(e):
                last = None
                for c in range(16):
                    last = e.transpose(out=psb[:, bset + c // 8, (c % 8) * 128:(c % 8) * 128 + 128],
                                       in_=hnb[s_][:, c * 128:(c + 1) * 128], identity=identb[:, :])
                return last
            P.pe(trh, reads=[("hnb", s_)], writes=[("ps", bset), ("ps", bset + 1)])
            P.act(lambda e: e.activation(out=actT[:, 0:8, tsl], in_=psb[:, bset, :].rearrange("p (a b) -> p a b", a=8),
                                         func=AF.Copy), reads=[("ps", bset)], writes=[("act", tt)])
            P.dve(lambda e: e.tensor_copy(out=actT[:, 8:16, tsl], in_=psb[:, bset + 1, :].rearrange("p (a b) -> p a b", a=8)),
                  reads=[("ps", bset + 1), ("act", tt)], writes=[("act", tt)])

        wo_mm(0)
        for tt in range(NT):
            if tt + 1 < NT:
                wo_mm(tt + 1)
            wo_post(tt)
        P.barrier()
        if STOP == 9:
            P.emit()
            return nc

        hidT = sb(65536, [128, NFC, 512], BF16)
        ff = sb(110592, [128, 4, 2048], F32)
        wpool = [sb(143360 + 4096 * i, [128, 4, 512], BF16) for i in range(8)]
        ht = sb(176128, [128, 2048], F32)
        gpo = sb(184320, [128, 2048], F32)
        sgf = [sb(192512 + 2048 * i, [128, 512], F32) for i in range(2)]
        fj = sb(196608, [128, 2048], BF16)
        P.dma("sync", gpo[:, :], gvecs[3:4, :].partition_broadcast(128), writes=["gpo"])
        wg_v = w_gate.rearrange("(c p) n -> p c n", p=128)
        wu_v = w_up.rearrange("(c p) n -> p c n", p=128)
        wd_v = w_down.rearrange("(f p) n -> p f n", p=128)
        nld = 0
        pending_epi = []

        def ffn_epi(tb, t4):
            tt = tb * 4 + t4
            fk = [("ff", t4, db) for db in range(4)]
            P.dma("sync", ht[:, :], hscr[tt * 128:(tt + 1) * 128, :], reads=[("hscr", tt)], writes=["ht"])
            P.dve(lambda e: e.scalar_tensor_tensor(out=fj[:, :], in0=ff[:, t4, :], scalar=1.0, in1=ff[:, t4, :],
                                                   op0=ALU.mult, op1=ALU.mult, accum_out=stat[:, 14:15]),
                  reads=fk, writes=["fj", "st14"])
            rstd_from("st14", stat[:, 14:15], stat[:, 3:4], "st3", D, stat[:, 15:16], "st15")
            P.dve(lambda e: e.scalar_tensor_tensor(out=ff[:, t4, :], in0=ff[:, t4, :], scalar=stat[:, 3:4], in1=gpo[:, :],
                                                   op0=ALU.mult, op1=ALU.mult), reads=fk + ["st3", "gpo"], writes=fk)
            P.dve(lambda e: e.tensor_tensor(out=ff[:, t4, :], in0=ff[:, t4, :], in1=ht[:, :], op=ALU.add),
                  reads=fk + ["ht"], writes=fk)
            P.dma("sync", out[tt * 128:(tt + 1) * 128, :], ff[:, t4, :], reads=fk, writes=[("out", tt)])

        for tb in range(4):
            tsl = slice(tb * 512, (tb + 1) * 512)
            for blk in range(11):
                if blk in (1, 3, 5, 7) and pending_epi:
                    ffn_epi(*pending_epi.pop(0))
                for cq in range(4):
                    gb = wpool[nld % 8]
                    gk = ("wp", nld % 8)
                    nld += 1
                    ub = wpool[nld % 8]
                    uk = ("wp", nld % 8)
                    nld += 1
                    P.dma("gpsimd", gb[:, :, :], wg_v[:, 4 * cq:4 * cq + 4, blk * 512:(blk + 1) * 512], writes=[gk])
                    P.dma("gpsimd", ub[:, :, :], wu_v[:, 4 * cq:4 * cq + 4, blk * 512:(blk + 1) * 512], writes=[uk])
                    for fcl in range(4):
                        def mgu(e, gb=gb, ub=ub, fcl=fcl, cq=cq, tsl=tsl):
                            last = None
                            for c4 in range(4):
                                c = 4 * cq + c4
                                e.matmul(ps[:, fcl, :], lhsT=gb[:, c4, fcl * 128:(fcl + 1) * 128], rhs=actT[:, c, tsl],
                                         start=(c == 0), stop=(c == 15))
                                last = e.matmul(ps[:, 4 + fcl, :], lhsT=ub[:, c4, fcl * 128:(fcl + 1) * 128],
                                                rhs=actT[:, c, tsl], start=(c == 0), stop=(c == 15))
                            return last
                        P.pe(mgu, reads=[gk, uk], writes=[("ps", fcl), ("ps", 4 + fcl)])
                        if cq == 3:
                            fc = 4 * blk + fcl
                            b2 = fc % 2
                            P.act(lambda e, fcl=fcl, b2=b2: e.activation(out=sgf[b2][:, :], in_=ps[:, fcl, :], func=AF.Silu),
                                  reads=[("ps", fcl)], writes=[("sgf", b2)])
                            P.dve(lambda e, fcl=fcl, b2=b2, fc=fc: e.tensor_tensor(out=hidT[:, fc, :], in0=sgf[b2][:, :],
                                                                                   in1=ps[:, 4 + fcl, :], op=ALU.mult),
                                  reads=[("sgf", b2), ("ps", 4 + fcl)], writes=[("hid", fc)])
            for db in range(4):
                bs = 4 * (db % 2)
                for fq in range(11):
                    wdb = wpool[nld % 8]
                    wk = ("wp", nld % 8)
                    nld += 1
                    P.dma("gpsimd", wdb[:, :, :], wd_v[:, 4 * fq:4 * fq + 4, db * 512:(db + 1) * 512], writes=[wk])

                    def md(e, wdb=wdb, fq=fq, bs=bs):
                        last = None
                        for f4 in range(4):
                            fc = fq * 4 + f4
                            for t4 in range(4):
                                last = e.matmul(ps[:, bs + t4, :], lhsT=hidT[:, fc, t4 * 128:(t4 + 1) * 128], rhs=wdb[:, f4, :],
                                                start=(fc == 0), stop=(fc == NFC - 1))
                        return last
                    P.pe(md, reads=[wk] + [("hid", fq * 4 + f4) for f4 in range(4)], writes=[("ps", bs + t4) for t4 in range(4)])
                for t4 in range(4):
                    if t4 % 2 == 0:
                        P.act(lambda e, t4=t4, db=db, bs=bs: e.activation(out=ff[:, t4, db * 512:(db + 1) * 512],
                                                                          in_=ps[:, bs + t4, :], func=AF.Copy),
                              reads=[("ps", bs + t4)], writes=[("ff", t4, db)])
                    else:
                        P.dve(lambda e, t4=t4, db=db, bs=bs: e.tensor_copy(out=ff[:, t4, db * 512:(db + 1) * 512],
                                                                           in_=ps[:, bs + t4, :]),
                              reads=[("ps", bs + t4)], writes=[("ff", t4, db)])
            pending_epi = [(tb, t4) for t4 in range(4)]
        for (tb_, t4_) in pending_epi:
            ffn_epi(tb_, t4_)
        P.emit()
        print('sig counts', P.sig_counts, 'dma cum', max(P.dma_cum))
    return nc


def _host_layouts(inp):
    f32 = np.float32
    G, N, Pp = 64, 64, 16
    sm = np.zeros((128, NSM), f32)
    sm[:, C_ID:C_ID + 128] = np.eye(128, dtype=f32)
    kk = np.arange(128)[:, None]
    qq = np.arange(128)[None, :]
    sm[:, C_MASK:C_MASK + 128] = (kk > qq).astype(f32)
    sm[:, C_MASK + 128:C_MASK + 256] = (kk <= qq).astype(f32)
    half = 32
    inv_freq = (np.float32(10000.0) ** (-np.arange(half, dtype=f32) / np.float32(half))).astype(f32)
    sm[:, C_INVF:C_INVF + 32] = inv_freq[None, :]
    sm[:, C_SINK:C_SINK + 16] = inp["sinks"][0][None, :]
    sm[:, C_GSSM:C_GSSM + 8] = inp["g_ssm_out"][0].reshape(8, 128).T
    sm[:, C_BGLU:C_BGLU + 8] = inp["b_glu"][0].reshape(8, 128).T
    sm[:, C_DSK:C_DSK + 8] = inp["d_skip"][0].reshape(8, 8, 16).reshape(8, 128).T
    a_re, a_im, ldt = inp["a_re"][0], inp["a_im"][0], inp["log_dt"][0]
    for b in range(2):
        sm[64 * b:64 * b + 64, C_ARA:C_ARA + 32] = a_re[b::2, :].T
        sm[64 * b:64 * b + 64, C_AIA:C_AIA + 32] = a_im[b::2, :].T
        sm[64 * b:64 * b + 64, C_LDA:C_LDA + 32] = np.broadcast_to(ldt[b::2][None, :], (64, 32))
    lb3 = np.zeros((128, 3, 8, 2, 64), f32)
    for gq in range(8):
        rows = slice(16 * gq, 16 * gq + 16)
        for blk in range(8):
            g = 8 * blk + gq
            lb3[rows, 0, blk, :, :] = a_re[g][None, None, :]
            lb3[rows, 1, blk, :, :] = a_im[g][None, None, :]
            lb3[rows, 2, blk, :, :] = ldt[g]
    lb3 = lb3.reshape(128, 3, 1024)
    bexp = np.zeros((128, 2, 8, 2, 64), f32)
    cexp = np.zeros((128, 2, 32, 8, 16), f32)
    maska = np.zeros((128, 4), f32)
    b_re, b_im, c_re, c_im = inp["b_re"][0], inp["b_im"][0], inp["c_re"][0], inp["c_im"][0]
    for g in range(G):
        blk, gq = divmod(g, 8)
        a, b = divmod(gq, 2)
        rows = slice(16 * gq, 16 * gq + 16)
        bexp[rows, 0, blk, b, :] = b_re[g].T
        bexp[rows, 1, blk, b, :] = b_im[g].T
        maska[rows, a] = 1.0
        pp = g // 2
        cexp[64 * b:64 * b + 64, 0, pp, gq, :] = c_re[g].T
        cexp[64 * b:64 * b + 64, 1, pp, gq, :] = c_im[g].T
    bexp = bexp.reshape(128, 2, 1024)
    cexp = cexp.reshape(128, 2, 4096)
    gv = np.zeros((5, D), f32)
    gv[0] = inp["g_pre_mix"][0]
    gv[1] = inp["g_post_mix"][0]
    gv[2] = inp["g_pre_ffn"][0]
    gv[3] = inp["g_post_ffn"][0]
    gv[4, :1024] = inp["g_attn_out"][0]
    tokc = np.broadcast_to(np.arange(1, 2049, dtype=f32)[None, :], (128, 2048)).copy()
    shared = {
        "smalls": sm, "gvecs": gv, "lb3": lb3, "bexp": bexp, "cexp": cexp, "tokc": tokc, "maska": maska,
        "w_in": np.ascontiguousarray(inp["w_in"][0]), "w_glu": np.ascontiguousarray(inp["w_glu"][0]),
        "w_o": np.ascontiguousarray(inp["w_o"][0]), "w_gate": np.ascontiguousarray(inp["w_gate"][0]),
        "w_up": np.ascontiguousarray(inp["w_up"][0]), "w_down": np.ascontiguousarray(inp["w_down"][0]),
    }
    return shared


def kernel(**inputs):
    inp = {k: np.asarray(v) for k, v in inputs.items()}
    shared = _host_layouts(inp)
    nc = build_nc()
    in_maps = []
    for c in range(8):
        m = dict(shared)
        m["x"] = np.ascontiguousarray(inp["x"][c])
        m["pos"] = np.ascontiguousarray(inp["positions"][c].astype(np.int32).reshape(16, 128).T)
        in_maps.append(m)
    res = run_bass_kernel_spmd(nc, in_maps, core_ids=list(range(8)))
    return np.stack([np.asarray(r["out"], dtype=np.float32) for r in res.results], axis=0)
```
